# Optimizing a Trainium2 kernel written in Bass

```python
import math
import jax, jax.numpy as jnp
from jax import lax
import numpy as np

D_MODEL = 1024
BATCH = 8
SEQ = 2048
DEPTH = 1

HEAD_DIM = 128
DN_HEADS = D_MODEL // 256
DN_WIDTH = DN_HEADS * HEAD_DIM
CONV_WIDTH = 4
CHUNK = 64
SWA_GROUPS = ((128, 1), (512, 4), (2048, 16))
N_GROUPS = 3
SWA_HEADS = D_MODEL // 256
SWA_WIDTH = SWA_HEADS * HEAD_DIM
ROPE_DIM = HEAD_DIM // 4
ROPE_THETA = 500000.0
N_BRANCHES = 2
EPS = 1e-6
NEG_INF = -1e30

IN_SIZES = (3 * DN_WIDTH, DN_WIDTH, DN_HEADS, DN_HEADS,
            N_GROUPS * SWA_WIDTH, N_GROUPS * SWA_WIDTH, N_GROUPS * SWA_WIDTH, SWA_WIDTH,
            N_BRANCHES * D_MODEL)
IN_COLS = sum(IN_SIZES)

kernel_name = "hybrid_gated_deltanet_dilated_swa_block"


def rms_norm(x, w):
    xf = x.astype(jnp.float32)
    y = xf * lax.rsqrt(jnp.mean(xf * xf, axis=-1, keepdims=True) + EPS)
    return y.astype(x.dtype) * w.astype(x.dtype)


def l2_normalize(x):
    return x * lax.rsqrt(jnp.sum(x * x, axis=-1, keepdims=True) + EPS)


def causal_depthwise_conv(x, w):
    c = x.shape[-1]
    return lax.conv_general_dilated(x, w.astype(x.dtype), window_strides=(1,), padding=[(CONV_WIDTH - 1, 0)],
                                    dimension_numbers=('NWC', 'WIO', 'NWC'), feature_group_count=c)


def partial_rope(x, cos, sin):
    half = ROPE_DIM // 2
    x1, x2, xp = x[..., :half], x[..., half:ROPE_DIM], x[..., ROPE_DIM:]
    c = cos[:, None, None, :].astype(x.dtype)
    s = sin[:, None, None, :].astype(x.dtype)
    return jnp.concatenate([x1 * c - x2 * s, x2 * c + x1 * s, xp], axis=-1)


def gated_delta_rule_chunked(q, k, v, g, beta):
    B, H, T, Dk = q.shape
    Dv = v.shape[-1]
    N = T // CHUNK
    q = q.reshape(B, H, N, CHUNK, Dk)
    k = k.reshape(B, H, N, CHUNK, Dk)
    v = v.reshape(B, H, N, CHUNK, Dv)
    g = g.reshape(B, H, N, CHUNK)
    beta = beta.reshape(B, H, N, CHUNK)
    gc = jnp.cumsum(g, axis=-1)
    idx = jnp.arange(CHUNK)
    causal = idx[:, None] >= idx[None, :]
    strict = idx[:, None] > idx[None, :]
    decay = jnp.exp(jnp.where(causal, gc[..., :, None] - gc[..., None, :], -jnp.inf))
    kb = k * beta[..., None]
    lower = jnp.where(strict, jnp.einsum('bhnid,bhnjd->bhnij', kb, k) * decay, 0.0)
    eye = jnp.eye(CHUNK, dtype=q.dtype)
    tmat = lax.linalg.triangular_solve(eye + lower, jnp.broadcast_to(eye, lower.shape),
                                       left_side=True, lower=True, unit_diagonal=True)
    u = jnp.einsum('bhnij,bhnjd->bhnid', tmat, v * beta[..., None])
    w = jnp.einsum('bhnij,bhnjd->bhnid', tmat, kb * jnp.exp(gc)[..., None])
    a_intra = jnp.where(causal, jnp.einsum('bhnid,bhnjd->bhnij', q, k) * decay, 0.0)

    def step(S, inp):
        qi, ki, ui, wi, ai, gci = inp
        v_new = ui - jnp.einsum('bhck,bhkv->bhcv', wi, S)
        o = jnp.einsum('bhck,bhkv->bhcv', qi * jnp.exp(gci)[..., None], S) + jnp.einsum('bhij,bhjv->bhiv', ai, v_new)
        glast = gci[..., -1]
        S = S * jnp.exp(glast)[..., None, None] + jnp.einsum(
            'bhck,bhcv->bhkv', ki * jnp.exp(glast[..., None] - gci)[..., None], v_new)
        return S, o

    xs = tuple(jnp.moveaxis(a, 2, 0) for a in (q, k, u, w, a_intra, gc))
    s0 = jnp.zeros((B, H, Dk, Dv), q.dtype)
    _, o = lax.scan(step, s0, xs)
    return jnp.moveaxis(o, 0, 2).reshape(B, H, T, Dv)


def dilated_window_attention(q, k, v, window, dilation):
    B, T, H, Dh = q.shape
    L = T // dilation
    blk = window // dilation
    nb = -(-L // blk)
    Lp = nb * blk

    def to_sub(x):
        x = x.astype(jnp.float32).reshape(B, L, dilation, H, Dh).transpose(0, 2, 3, 1, 4)
        return jnp.pad(x, ((0, 0), (0, 0), (0, 0), (0, Lp - L), (0, 0)))

    def windows(x):
        xp = jnp.pad(x, ((0, 0), (0, 0), (0, 0), (blk, 0), (0, 0))).reshape(B, dilation, H, nb + 1, blk, Dh)
        return jnp.concatenate([xp[:, :, :, :-1], xp[:, :, :, 1:]], axis=4)

    qb = to_sub(q).reshape(B, dilation, H, nb, blk, Dh)
    kw = windows(to_sub(k))
    vw = windows(to_sub(v))
    s = jnp.einsum('brhnqd,brhnkd->brhnqk', qb, kw) * (Dh ** -0.5)
    i = jnp.arange(blk)[:, None]
    j = jnp.arange(2 * blk)[None, :]
    dist = blk + i - j
    key_pos = (jnp.arange(nb)[:, None, None] - 1) * blk + j[None]
    valid = (dist >= 0)[None] & (dist <= blk)[None] & (key_pos >= 0)
    s = jnp.where(valid, s, NEG_INF)
    lse = jax.nn.logsumexp(s, axis=-1)
    p = jnp.exp(s - lse[..., None])
    o = jnp.einsum('brhnqk,brhnkd->brhnqd', p, vw)
    o = o.reshape(B, dilation, H, Lp, Dh)[:, :, :, :L].transpose(0, 3, 1, 2, 4).reshape(B, T, H, Dh)
    lse = lse.reshape(B, dilation, H, Lp)[..., :L].transpose(0, 3, 1, 2).reshape(B, T, H)
    return o, lse


def setup_inputs(seed: int = 0) -> dict:
    key = jax.random.key(seed)
    ks = jax.random.split(key, 13)
    f32 = jnp.float32
    x = jax.random.normal(ks[0], (BATCH, SEQ, D_MODEL), f32)
    norm_w = 1.0 + 0.02 * jax.random.normal(ks[1], (DEPTH, D_MODEL), f32)
    w_in = jax.random.normal(ks[2], (DEPTH, D_MODEL, IN_COLS), f32) * D_MODEL ** -0.5
    conv_w = jax.random.normal(ks[3], (DEPTH, CONV_WIDTH, 1, 3 * DN_WIDTH), f32) * CONV_WIDTH ** -0.5
    dn_a_log = jnp.log(jax.random.uniform(ks[4], (DEPTH, DN_HEADS), f32, 1.0, 16.0))
    dn_dt_bias = 0.1 * jax.random.normal(ks[5], (DEPTH, DN_HEADS), f32)
    dn_norm_w = 1.0 + 0.02 * jax.random.normal(ks[6], (DEPTH, HEAD_DIM), f32)
    q_norm_w = 1.0 + 0.02 * jax.random.normal(ks[7], (DEPTH, N_GROUPS, HEAD_DIM), f32)
    k_norm_w = 1.0 + 0.02 * jax.random.normal(ks[8], (DEPTH, N_GROUPS, HEAD_DIM), f32)
    w_branch_dn = jax.random.normal(ks[9], (DEPTH, DN_WIDTH, D_MODEL), f32) * DN_WIDTH ** -0.5
    w_branch_swa = jax.random.normal(ks[10], (DEPTH, SWA_WIDTH, D_MODEL), f32) * SWA_WIDTH ** -0.5
    w_out = jax.random.normal(ks[11], (DEPTH, D_MODEL, D_MODEL), f32) * D_MODEL ** -0.5
    return {"x": x, "norm_w": norm_w, "w_in": w_in, "conv_w": conv_w, "dn_a_log": dn_a_log,
            "dn_dt_bias": dn_dt_bias, "dn_norm_w": dn_norm_w, "q_norm_w": q_norm_w, "k_norm_w": k_norm_w,
            "w_branch_dn": w_branch_dn, "w_branch_swa": w_branch_swa, "w_out": w_out}


def reference(x, norm_w, w_in, conv_w, dn_a_log, dn_dt_bias, dn_norm_w, q_norm_w, k_norm_w,
              w_branch_dn, w_branch_swa, w_out):
    B, T, _ = x.shape
    split_points = [int(p) for p in np.cumsum(IN_SIZES)[:-1]]
    pos = jnp.arange(T, dtype=jnp.float32)
    inv_freq = ROPE_THETA ** (-jnp.arange(0, ROPE_DIM, 2, dtype=jnp.float32) / ROPE_DIM)
    ang = pos[:, None] * inv_freq[None, :]
    cos, sin = jnp.cos(ang), jnp.sin(ang)

    for layer in range(DEPTH):
        h = rms_norm(x, norm_w[layer])
        proj = h @ w_in[layer]
        (dn_qkv, dn_z, dn_b, dn_a, swa_q, swa_k, swa_v, swa_z, gates) = jnp.split(proj, split_points, axis=-1)

        qkv = jax.nn.silu(causal_depthwise_conv(dn_qkv, conv_w[layer]))
        dq, dk, dv = jnp.split(qkv.astype(jnp.float32), 3, axis=-1)
        to_heads = lambda t: t.reshape(B, T, DN_HEADS, HEAD_DIM).transpose(0, 2, 1, 3)
        dq = l2_normalize(to_heads(dq)) * (HEAD_DIM ** -0.5)
        dk = l2_normalize(to_heads(dk))
        dv = to_heads(dv)
        beta = jax.nn.sigmoid(dn_b.astype(jnp.float32)).transpose(0, 2, 1)
        g = (-jnp.exp(dn_a_log[layer].astype(jnp.float32))
             * jax.nn.softplus(dn_a.astype(jnp.float32) + dn_dt_bias[layer].astype(jnp.float32))).transpose(0, 2, 1)
        o_dn = gated_delta_rule_chunked(dq, dk, dv, g, beta).transpose(0, 2, 1, 3).astype(x.dtype)
        o_dn = rms_norm(o_dn, dn_norm_w[layer]) * jax.nn.silu(dn_z.reshape(B, T, DN_HEADS, HEAD_DIM))
        y_dn = o_dn.reshape(B, T, DN_WIDTH) @ w_branch_dn[layer]

        grp = lambda t: t.reshape(B, T, N_GROUPS, SWA_HEADS, HEAD_DIM)
        sq = partial_rope(rms_norm(grp(swa_q), q_norm_w[layer][:, None, :]), cos, sin)
        sk = partial_rope(rms_norm(grp(swa_k), k_norm_w[layer][:, None, :]), cos, sin)
        sv = grp(swa_v)
        outs, lses = [], []
        for gi, (window, dilation) in enumerate(SWA_GROUPS):
            o_g, lse_g = dilated_window_attention(sq[:, :, gi], sk[:, :, gi], sv[:, :, gi], window, dilation)
            outs.append(o_g)
            lses.append(lse_g)
        alpha = jax.nn.softmax(jnp.stack(lses, axis=0), axis=0)
        o_swa = jnp.sum(alpha[..., None] * jnp.stack(outs, axis=0), axis=0).astype(x.dtype)
        o_swa = o_swa * jax.nn.silu(swa_z.reshape(B, T, SWA_HEADS, HEAD_DIM))
        y_swa = o_swa.reshape(B, T, SWA_WIDTH) @ w_branch_swa[layer]

        g_dn, g_swa = jnp.split(gates, 2, axis=-1)
        merged = jax.nn.sigmoid(g_dn) * y_dn + jax.nn.sigmoid(g_swa) * y_swa
        x = x + merged @ w_out[layer]
    return x
```

```python
import numpy as np
from contextlib import ExitStack
import concourse.bass as bass
import concourse.mybir as mybir
from concourse.bass_utils import run_bass_kernel_spmd

F32 = mybir.dt.float32
BF16 = mybir.dt.bfloat16
AF = mybir.ActivationFunctionType
ALU = mybir.AluOpType
AX = mybir.AxisListType

T = 2048
D = 1024
NT = 16
EPS = 1e-6
BIG = 30000.0
QSCALE = 128.0 ** -0.5


class Res:
    __slots__ = ("key", "w", "r", "dsem", "dcnt", "excl")

    def __init__(self, excl=False):
        self.excl = excl
        self.key = None
        self.w = None
        self.r = {}
        self.dsem = None
        self.dcnt = 0


class FW:
    def __init__(self, nc, es):
        self.nc = nc
        self.es = es
        self.eng = {"pe": nc.tensor, "act": nc.scalar, "dve": nc.vector, "pool": nc.gpsimd, "sp": nc.sync}
        self.sem = {}
        self.cnt = {}
        for k in self.eng:
            self.sem[k] = es.enter_context(nc.semaphore("s_" + k))
            self.cnt[k] = 0
        self.waited = {k: {} for k in self.eng}
        self.semobj = dict(self.sem)
        self.ndsem = 0
        self.ninst = 0

    def _wait(self, e, deps):
        best = {}
        for (k, v) in deps:
            if k == e and e == "pe":
                continue
            if best.get(k, 0) < v:
                best[k] = v
        for k, v in best.items():
            if self.waited[e].get(k, 0) >= v:
                continue
            self.eng[e].wait_ge(self.semobj[k], v)
            self.waited[e][k] = v

    @staticmethod
    def _deps(reads, writes, e=None):
        deps = []
        for r in reads:
            if r.w is not None:
                deps.append(r.w)
            if r.excl:
                deps.extend((k, v) for k, v in r.r.items() if k != e)
        for w in writes:
            if w.w is not None:
                deps.append(w.w)
            deps.extend(w.r.items())
        return deps

    def op(self, e, fn, reads=(), writes=()):
        self._wait(e, self._deps(reads, writes, e))
        inst = fn(self.eng[e])
        self.cnt[e] += 1
        c = self.cnt[e]
        inst.then_inc(self.sem[e], 1)
        for r in reads:
            if r.r.get(e, 0) < c:
                r.r[e] = c
        for w in writes:
            w.w = (e, c)
            w.r = {}
        self.ninst += 1
        return inst

    def dma(self, out, in_, reads=(), writes=(), q="sp", semres=None):
        self._wait(q, self._deps(reads, writes, q))
        sr = semres or (writes[0] if writes else reads[0])
        if sr.dsem is None:
            sr.key = "d_%d" % self.ndsem
            sr.dsem = self.es.enter_context(self.nc.semaphore(sr.key))
            self.ndsem += 1
            self.semobj[sr.key] = sr.dsem
        key = sr.key
        inst = self.eng[q].dma_start(out=out, in_=in_)
        sr.dcnt += 16
        inst.then_inc(sr.dsem, 16)
        for r in reads:
            if r.r.get(key, 0) < sr.dcnt:
                r.r[key] = sr.dcnt
        for w in writes:
            w.w = (key, sr.dcnt)
            w.r = {}
        self.ninst += 1
        return inst

    def barrier_all(self):
        for e in self.eng:
            deps = [(k, self.cnt[k]) for k in self.eng if self.cnt[k] > 0]
            best = {}
            for k, v in deps:
                best[k] = v
            for k, v in best.items():
                if self.waited[e].get(k, 0) >= v:
                    continue
                self.eng[e].wait_ge(self.semobj[k], v)
                self.waited[e][k] = v

    def final_wait(self, reslist, e="sp"):
        deps = []
        for r in reslist:
            if r.w is not None:
                deps.append(r.w)
            deps.extend(r.r.items())
        self._wait(e, deps)


class PSP:
    def __init__(self, tiles, res):
        self.t, self.r = tiles, res
        self.live = [False] * len(tiles)
        self.i = 0

    def get(self):
        n = len(self.t)
        for _ in range(n):
            i = self.i % n
            self.i += 1
            if not self.live[i]:
                self.live[i] = True
                return self.t[i], self.r[i]
        raise RuntimeError("no free PSUM bank")

    def try_get(self):
        n = len(self.t)
        for _ in range(n):
            i = self.i % n
            self.i += 1
            if not self.live[i]:
                self.live[i] = True
                return self.t[i], self.r[i]
        return None

    def rel(self, res):
        self.live[self.r.index(res)] = False


def streams_gen(items, W, make, stagger=0):
    it = iter(items)
    active = []
    free = list(range(W))
    pending = True
    rnd = 0
    next_start = 0
    while True:
        while free and pending and rnd >= next_start:
            try:
                a = next(it)
            except StopIteration:
                pending = False
                break
            s = free.pop(0)
            active.append([make(a, s), s])
            if stagger:
                next_start = rnd + stagger
                break
        if not active and not pending:
            break
        for ent in list(active):
            try:
                next(ent[0])
            except StopIteration:
                active.remove(ent)
                free.append(ent[1])
        rnd += 1
        yield


def run_streams(items, W, make, stagger=0):
    for _ in streams_gen(items, W, make, stagger):
        pass


CF = {}
_o = 0
for _n, _w in [("ident", 128), ("tri", 128), ("tris", 128), ("ch0", 128), ("ch1", 128), ("nw", 8), ("cw", 48),
               ("dtb", 4), ("alog", 4), ("dnw", 512), ("qw", 3), ("kw", 3), ("eps", 1)]:
    CF[_n] = (_o, _w)
    _o += _w
NCF = _o
CB = {}
_o = 0
for _n, _w in [("ident", 128), ("ones", 128), ("x1", 128), ("masku", 512), ("maskl", 512), ("m2", 2048), ("rm", 32)]:
    CB[_n] = (_o, _w)
    _o += _w
NCB = _o


def _const_arrays():
    p = np.arange(128)
    same = (p[:, None] // 64) == (p[None, :] // 64)
    cf = np.zeros((128, NCF), np.float32)

    def put(name, arr):
        o, w = CF[name]
        cf[:, o:o + w] = arr
    put("ident", np.eye(128, dtype=np.float32))
    put("tri", (same & (p[:, None] <= p[None, :])).astype(np.float32))
    put("tris", (same & (p[:, None] > p[None, :])).astype(np.float32))
    put("ch0", np.broadcast_to((p[:, None] < 64), (128, 128)).astype(np.float32))
    put("ch1", np.broadcast_to((p[:, None] >= 64), (128, 128)).astype(np.float32))
    put("eps", np.full((128, 1), EPS, np.float32))
    cb = np.zeros((128, NCB), np.float32)

    def putb(name, arr):
        o, w = CB[name]
        cb[:, o:o + w] = arr
    putb("ident", np.eye(128, dtype=np.float32))
    putb("ones", np.ones((128, 128), np.float32))
    putb("x1", np.where(same & (p[:, None] > p[None, :]), 0.0, BIG).astype(np.float32))
    mu = (p[:, None] <= p[None, :]).astype(np.float32)
    ml = (p[:, None] >= p[None, :]).astype(np.float32)
    putb("masku", np.tile(mu, (1, 4)))
    putb("maskl", np.tile(ml, (1, 4)))
    m2 = np.zeros((128, 4, 16, 32), np.float32)
    for m in range(4):
        q = 32 * m + np.arange(32)
        m2[:, m, :, :] = (p[:, None] <= q[None, :]).astype(np.float32)[:, None, :]
    putb("m2", m2.reshape(128, 2048))
    rm = np.zeros((128, 32), np.float32)
    for m in range(16):
        rm[16 + m, m] = -1.0
    for m in range(16, 32):
        rm[m - 16, m] = 1.0
    putb("rm", rm)
    pos = np.arange(T, dtype=np.float32)
    inv_freq = (np.float32(500000.0) ** (-np.arange(0, 32, 2, dtype=np.float32) / np.float32(32))).astype(np.float32)
    ang = (pos[:, None] * inv_freq[None, :]).astype(np.float32)
    cos = np.cos(ang).astype(np.float32).T
    sin = np.sin(ang).astype(np.float32).T
    rope = np.zeros((32, 2, T), np.float32)
    rope[0:16, 0] = cos
    rope[16:32, 0] = cos
    rope[0:16, 1] = sin
    rope[16:32, 1] = sin
    return cf, cb, rope


def _prep_shared(inp):
    cf, cb, rope = _const_arrays()
    w_in = inp["w_in"][0]

    def put(name, arr):
        o, w = CF[name]
        cf[:, o:o + w] = arr
    put("nw", inp["norm_w"][0].reshape(8, 128).T)
    cw = inp["conv_w"][0, :, 0, :]
    put("cw", cw.T.reshape(12, 128, 4).transpose(1, 0, 2).reshape(128, 48))
    put("dtb", np.broadcast_to(inp["dn_dt_bias"][0][None, :], (128, 4)))
    put("alog", np.broadcast_to(inp["dn_a_log"][0][None, :], (128, 4)))
    put("dnw", np.broadcast_to(np.tile(inp["dn_norm_w"][0], 4)[None, :], (128, 512)))
    put("qw", inp["q_norm_w"][0].T)
    put("kw", inp["k_norm_w"][0].T)
    w_dn = np.ascontiguousarray(w_in[:, 0:2056])
    base = 2056
    w_swa = np.empty((4, 1024, 1280), np.float32)
    for h in range(4):
        cols = []
        for typ in range(3):
            for g in range(3):
                c0 = base + typ * 1536 + g * 512 + h * 128
                cols.append(w_in[:, c0:c0 + 128])
        c0 = base + 3 * 1536 + h * 128
        cols.append(w_in[:, c0:c0 + 128])
        w_swa[h] = np.concatenate(cols, axis=1)
    w_g = np.ascontiguousarray(w_in[:, base + 3 * 1536 + 512:])
    assert w_g.shape[1] == 2048
    return {"cf": cf, "cb": cb, "rope": rope, "w_dn": w_dn, "w_swa": w_swa, "w_g": w_g,
            "w_bd": np.ascontiguousarray(inp["w_branch_dn"][0]), "w_bs": np.ascontiguousarray(inp["w_branch_swa"][0]),
            "w_o": np.ascontiguousarray(inp["w_out"][0])}


def build(stage=99, debug=False):
    nc = bass.Bass("TRN2", target_bir_lowering=False)
    x_d = nc.dram_tensor("x", [T, D], F32, kind="ExternalInput").ap()
    cf_d = nc.dram_tensor("cf", [128, NCF], F32, kind="ExternalInput").ap()
    cb_d = nc.dram_tensor("cb", [128, NCB], F32, kind="ExternalInput").ap()
    rope_d = nc.dram_tensor("rope", [32, 2, T], F32, kind="ExternalInput").ap()
    wdn_d = nc.dram_tensor("w_dn", [D, 2056], F32, kind="ExternalInput").ap()
    wswa_d = nc.dram_tensor("w_swa", [4, D, 1280], F32, kind="ExternalInput").ap()
    wg_d = nc.dram_tensor("w_g", [D, 2048], F32, kind="ExternalInput").ap()
    wbd_d = nc.dram_tensor("w_bd", [512, D], F32, kind="ExternalInput").ap()
    wbs_d = nc.dram_tensor("w_bs", [512, D], F32, kind="ExternalInput").ap()
    wo_d = nc.dram_tensor("w_o", [D, D], F32, kind="ExternalInput").ap()
    out_d = nc.dram_tensor("out", [T, D], F32, kind="ExternalOutput").ap()
    dbg_d = None
    if debug:
        dbg_d = nc.dram_tensor("dbg", [128, 8, T], BF16, kind="ExternalOutput").ap()

    with ExitStack() as es:
        fw = FW(nc, es)

        def sb(name, shape, dt, ctx=es):
            return ctx.enter_context(nc.sbuf_tensor("sb_" + name, shape, dt))

        hT = sb("hT", [128, 8, T], BF16)
        class _OT:
            d = None
            s_ = None

            def __getitem__(self, key):
                p, hsel, cols = key
                if isinstance(hsel, int):
                    return self.d[p, hsel, cols] if hsel < 4 else self.s_[p, hsel - 4, cols]
                assert hsel == slice(0, 4)
                return self.d[p, 0:4, cols]
        oT = _OT()
        oT.d = sb("oTd", [128, 4, T], BF16)
        cf = sb("cf", [128, NCF], F32)
        cb = sb("cb", [128, NCB], BF16)
        r_hT = [Res() for _ in range(NT)]
        r_oT = [[Res() for _ in range(4)] for _ in range(8)]
        r_cf, r_cb = Res(), Res()
        pst = [es.enter_context(nc.psum_tensor("ps%d" % i, [128, 512], F32)) for i in range(8)]
        r_ps = [Res(excl=True) for _ in range(8)]
        rot = {"list": list(range(8)), "i": 0}

        def psget():
            lst = rot["list"]
            i = lst[rot["i"] % len(lst)]
            rot["i"] += 1
            return pst[i], r_ps[i]

        def cfs(name, a=0, b=None):
            o, w = CF[name]
            if b is None:
                b = w
            return cf[:, o + a:o + b]

        def cbs(name, a=0, b=None):
            o, w = CB[name]
            if b is None:
                b = w
            return cb[:, o + a:o + b]

        def mm(out, lhsT, rhs, start, stop, reads, writes):
            return fw.op("pe", lambda e: e.matmul(out, lhsT, rhs, start=start, stop=stop, skip_group_check=True),
                         reads, writes)

        def load_w(dst, src, K, C):
            srcv = src.rearrange("(k p) c -> p k c", p=128)
            pieces = []
            for c0 in range(0, C, 512):
                c1 = min(C, c0 + 512)
                r = Res()
                fw.dma(dst[:, 0:K, c0:c1], srcv[:, :, c0:c1], writes=[r], q="pool")
                pieces.append((c0, c1, r))
            return pieces

        def wres(pieces, a, b):
            return [r for (c0, c1, r) in pieces if c0 < b and c1 > a]

        fw.dma(cf[:], cf_d[:, :], writes=[r_cf])
        with ExitStack() as ph:
            cbst = sb("cbst", [128, NCB], F32, ph)
            r_cbst = Res()
            fw.dma(cbst[:], cb_d[:, :], writes=[r_cbst])
            fw.op("dve", lambda e: e.tensor_copy(out=cb[:], in_=cbst[:]), [r_cbst], [r_cb])
            fw.barrier_all()
        ident_f = cfs("ident")
        ident_b = cbs("ident")
        ones_b = cbs("ones")
        eps_c = cfs("eps")

        ph_dn = es.enter_context(ExitStack())
        wdn_pre = sb("wdn", [128, 8, 2056], BF16, ph_dn)
        wdv_ = wdn_d.rearrange("(k p) c -> p k c", p=128)
        wp_pre = []
        for c0 in range(0, 2056, 512):
            c1 = min(2056, c0 + 512)
            r_ = Res()
            fw.dma(wdn_pre[:, :, c0:c1], wdv_[:, :, c0:c1], writes=[r_], q="pool")
            wp_pre.append((c0, c1, r_))

        with ExitStack() as ph:
            xt = [sb("xt%d" % i, [128, D], F32, ph) for i in range(6)]
            xs = [sb("xs%d" % i, [128, D], BF16, ph) for i in range(6)]
            junk = [sb("junk%d" % i, [128, D], F32, ph) for i in range(6)]
            ss = sb("ss", [128, NT], F32, ph)
            rt = sb("rt", [128, NT], F32, ph)
            rstd = sb("rstd", [128, NT], F32, ph)
            r_xt = [Res() for _ in range(6)]
            r_xs = [Res() for _ in range(6)]
            r_junk = [Res() for _ in range(6)]
            r_st = [Res() for _ in range(NT)]
            def p0_stream(i, s):
                fw.dma(xt[s][:], x_d[i * 128:(i + 1) * 128, :], writes=[r_xt[s]])
                yield
                fw.op("act", lambda e: e.activation(out=junk[s][:], in_=xt[s][:], func=AF.Square),
                      [r_xt[s]], [r_junk[s]])
                yield
                fw.op("dve", lambda e: e.tensor_reduce(out=ss[:, i:i + 1], in_=junk[s][:], axis=AX.X, op=ALU.add),
                      [r_junk[s]], [r_st[i]])
                yield
                fw.op("act", lambda e: e.activation(out=rt[:, i:i + 1], in_=ss[:, i:i + 1], func=AF.Sqrt,
                                                    bias=eps_c, scale=1.0 / D), [r_st[i], r_cf], [r_st[i]])
                yield
                fw.op("dve", lambda e: e.reciprocal(out=rstd[:, i:i + 1], in_=rt[:, i:i + 1]), [r_st[i]], [r_st[i]])
                yield
                fw.op("dve", lambda e: e.tensor_scalar(out=xs[s][:], in0=xt[s][:], scalar1=rstd[:, i:i + 1],
                                                       scalar2=None, op0=ALU.mult), [r_xt[s], r_st[i]], [r_xs[s]])
                yield
                pt, rp = psget()
                pb = pt[:].bitcast(BF16)
                for k in range(8):
                    fw.op("pe", lambda e: e.transpose(out=pb[:, k * 128:(k + 1) * 128],
                                                      in_=xs[s][:, k * 128:(k + 1) * 128], identity=ident_b),
                          [r_xs[s], r_cb], [rp])
                yield
                for k in range(8):
                    dst = hT[:, k, i * 128:(i + 1) * 128]
                    src = pb[:, k * 128:(k + 1) * 128]
                    if k % 2 == 0:
                        fw.op("act", lambda e: e.activation(out=dst, in_=src, func=AF.Copy,
                                                            scale=cfs("nw", k, k + 1)), [rp, r_cf], [r_hT[i]])
                    else:
                        fw.op("dve", lambda e: e.tensor_scalar(out=dst, in0=src, scalar1=cfs("nw", k, k + 1),
                                                               scalar2=None, op0=ALU.mult), [rp, r_cf], [r_hT[i]])
                    if k % 2 == 1:
                        yield

            run_streams(list(range(NT)), 6, p0_stream, stagger=2)
            fw.barrier_all()

        def hres(n):
            return r_hT[4 * n:4 * n + 4]

        if stage >= 1:
            phase_dn(nc, fw, ph_dn, sb, psget, mm, load_w, wres, cfs, cbs, hT, oT, r_hT, r_oT, r_cf, r_cb, hres,
                     wdn_d, pst, r_ps, wdn_pre, wp_pre)
            fw.barrier_all()
        ph_dn.close()
        oT.s_ = sb("oTs", [128, 4, T], BF16)

        if stage >= 2:
            with ExitStack() as ph:
                phase_swa(nc, fw, ph, sb, psget, mm, load_w, wres, cfs, cbs, hT, oT, r_hT, r_oT, r_cf, r_cb, hres,
                          wswa_d, rope_d, rot, pst, r_ps)
                fw.barrier_all()
        r_out = [Res(), Res()]
        if stage >= 3:
            with ExitStack() as ph:
                phase_out(nc, fw, ph, sb, psget, mm, load_w, wres, cfs, cbs, hT, oT, r_hT, r_oT, r_cf, r_cb, hres,
                          wg_d, wbd_d, wbs_d, wo_d, x_d, out_d, r_out)
        else:
            with ExitStack() as ph:
                zt = sb("zt", [128, D], F32, ph)
                fw.op("dve", lambda e: e.memset(zt[:], 0.0), [], [r_out[0]])
                for i in range(NT):
                    fw.dma(out_d[i * 128:(i + 1) * 128, :], zt[:], reads=[r_out[0]], semres=r_out[0])
                fw.final_wait(r_out)
        if debug:
            r_dbg = Res()
            allr = [r for hh in r_oT for r in hh]
            for hh in range(8):
                fw.dma(dbg_d[:, hh, :], (hT if stage == 0 else oT)[:, hh, :], reads=allr + r_hT, semres=r_dbg)
            fw.final_wait([r_dbg] + allr)
        fw.final_wait(r_out)
        fw.barrier_all()
    return nc


def phase_dn(nc, fw, ph, sb, psget, mm, load_w, wres, cfs, cbs, hT, oT, r_hT, r_oT, r_cf, r_cb, hres, wdn_d,
             pst, r_ps, wdn, wp):
    ident_f = cfs("ident")
    ident_b = cbs("ident")
    ones_b = cbs("ones")
    eps_c = cfs("eps")
    psp = PSP(pst, r_ps)

    def pget():
        n = 0
        while True:
            p = psp.try_get()
            if p is not None:
                return p
            n += 1
            if n > 10000:
                raise RuntimeError("PSUM starvation")
            yield


    ba = sb("ba", [128, NT, 8], F32, ph)
    beta = sb("beta", [128, NT, 4], F32, ph)
    nbeta = sb("nbeta", [128, NT, 4], F32, ph)
    g = sb("g", [128, NT, 4], F32, ph)
    t0 = sb("a0t0", [128, NT, 4], F32, ph)
    t1 = sb("a0t1", [128, NT, 4], F32, ph)
    t2 = sb("a0t2", [128, NT, 4], F32, ph)
    negA = sb("negA", [128, 4], F32, ph)
    gc = sb("gc", [128, NT, 4], F32, ph)
    egc = sb("egc", [128, NT, 4], F32, ph)
    sckbg = sb("sckbg", [128, NT, 4], F32, ph)
    sckd = sb("sckd", [128, NT, 4], F32, ph)
    egl = sb("egl", [128, 2, NT, 4], F32, ph)
    r_a0 = Res()
    pba, rpba = psp.get()
    for i in range(NT):
        for k in range(8):
            mm(pba[:, i * 8:(i + 1) * 8], hT[:, k, i * 128:(i + 1) * 128], wdn[:, k, 2048:2056], k == 0, k == 7,
               [r_hT[i]] + wres(wp, 2048, 2056), [rpba])
    bav = ba[:].rearrange("p t c -> p (t c)")
    fw.op("act", lambda e: e.activation(out=bav, in_=pba[:, 0:128], func=AF.Copy), [rpba], [r_a0])
    psp.rel(rpba)
    fw.op("act", lambda e: e.activation(out=beta[:], in_=ba[:, :, 0:4], func=AF.Sigmoid), [r_a0], [r_a0])
    fw.op("dve", lambda e: e.tensor_scalar(out=nbeta[:], in0=beta[:], scalar1=-1.0, scalar2=None, op0=ALU.mult),
          [r_a0], [r_a0])
    dtb_b = cfs("dtb").unsqueeze(1).to_broadcast([128, NT, 4])
    fw.op("dve", lambda e: e.tensor_tensor(out=t0[:], in0=ba[:, :, 4:8], in1=dtb_b, op=ALU.add), [r_a0, r_cf], [r_a0])
    fw.op("dve", lambda e: e.tensor_scalar(out=t2[:], in0=t0[:], scalar1=-1.0, scalar2=None, op0=ALU.mult),
          [r_a0], [r_a0])
    fw.op("dve", lambda e: e.tensor_tensor(out=t1[:], in0=t0[:], in1=t2[:], op=ALU.max), [r_a0], [r_a0])
    fw.op("act", lambda e: e.activation(out=t1[:], in_=t1[:], func=AF.Exp, scale=-1.0), [r_a0], [r_a0])
    fw.op("act", lambda e: e.activation(out=t1[:], in_=t1[:], func=AF.Ln, bias=1.0), [r_a0], [r_a0])
    fw.op("dve", lambda e: e.tensor_scalar(out=t2[:], in0=t0[:], scalar1=0.0, scalar2=None, op0=ALU.max),
          [r_a0], [r_a0])
    fw.op("dve", lambda e: e.tensor_tensor(out=t2[:], in0=t2[:], in1=t1[:], op=ALU.add), [r_a0], [r_a0])
    fw.op("act", lambda e: e.activation(out=negA[:], in_=cfs("alog"), func=AF.Exp), [r_cf], [r_a0])
    fw.op("dve", lambda e: e.tensor_scalar(out=negA[:], in0=negA[:], scalar1=-1.0, scalar2=None, op0=ALU.mult),
          [r_a0], [r_a0])
    negA_b = negA[:].unsqueeze(1).to_broadcast([128, NT, 4])
    fw.op("dve", lambda e: e.tensor_tensor(out=g[:], in0=t2[:], in1=negA_b, op=ALU.mult), [r_a0], [r_a0])
    gv = g[:].rearrange("p t c -> p (t c)")
    pg1, rpg1 = psp.get()
    mm(pg1[:, 0:64], cfs("tri"), gv, True, True, [r_a0, r_cf], [rpg1])
    mm(pg1[:, 64:128], cfs("tris"), gv, True, True, [r_a0, r_cf], [rpg1])
    mm(pg1[:, 128:192], cfs("ch0"), gv, True, True, [r_a0, r_cf], [rpg1])
    mm(pg1[:, 192:256], cfs("ch1"), gv, True, True, [r_a0, r_cf], [rpg1])
    fw.op("act", lambda e: e.activation(out=gc[:].rearrange("p t c -> p (t c)"), in_=pg1[:, 0:64], func=AF.Copy),
          [rpg1], [r_a0])
    fw.op("act", lambda e: e.activation(out=egc[:].rearrange("p t c -> p (t c)"), in_=pg1[:, 0:64], func=AF.Exp),
          [rpg1], [r_a0])
    fw.op("act", lambda e: e.activation(out=sckd[:].rearrange("p t c -> p (t c)"), in_=pg1[:, 64:128], func=AF.Exp),
          [rpg1], [r_a0])
    fw.op("act", lambda e: e.activation(out=egl[:].rearrange("p a t c -> p (a t c)"), in_=pg1[:, 128:256],
                                        func=AF.Exp), [rpg1], [r_a0])
    psp.rel(rpg1)
    fw.op("dve", lambda e: e.tensor_tensor(out=sckbg[:], in0=beta[:], in1=egc[:], op=ALU.mult), [r_a0], [r_a0])

    NS = 2
    halo = sb("halo", [128, 12, 3], F32, ph)
    r_halo = [Res() for _ in range(12)]
    raw = [sb("raw%d" % i, [128, 515], F32, ph) for i in range(NS)]
    cv = [sb("cv%d" % i, [128, 512], F32, ph) for i in range(NS)]
    cs = [sb("cs%d" % i, [128, 512], F32, ph) for i in range(NS)]
    sq = [sb("sq%d" % i, [128, 512], BF16, ph) for i in range(NS)]
    rn = [sb("rn%d" % i, [128, 512], F32, ph) for i in range(NS)]
    r_raw = [Res() for _ in range(NS)]
    r_cv = [Res() for _ in range(NS)]
    r_cs = [Res() for _ in range(NS)]
    r_sq = [Res() for _ in range(NS)]
    r_rn = [Res() for _ in range(NS)]
    qT = [sb("qT%d" % i, [128, 4, 512], BF16, ph) for i in range(2)]
    kT = [sb("kT%d" % i, [128, 4, 512], BF16, ph) for i in range(2)]
    kbg = [sb("kbg%d" % i, [128, 4, 4, 128], BF16, ph) for i in range(2)]
    kd = [sb("kd%d" % i, [128, 4, 4, 128], BF16, ph) for i in range(2)]
    vb = [sb("vb%d" % i, [128, 4, 4, 128], BF16, ph) for i in range(2)]
    r_qT = [[Res() for _ in range(4)] for _ in range(2)]
    r_kT = [[Res() for _ in range(4)] for _ in range(2)]
    r_kbg = [[Res() for _ in range(4)] for _ in range(2)]
    r_kd = [[Res() for _ in range(4)] for _ in range(2)]
    r_vb = [[Res() for _ in range(4)] for _ in range(2)]
    NP = 2
    arena = [sb("prearena%d" % i, [128, 2816], F32, ph) for i in range(NP)]
    e1m = [arena[i][:, 0:512] for i in range(NP)]
    tg = [arena[i][:, 512:1024] for i in range(NP)]
    abf = [arena[i][:, 1024:2816].bitcast(BF16) for i in range(NP)]
    Mb = [[abf[i][:, 0:512], abf[i][:, 512:1024]] for i in range(NP)]
    MTb = [[abf[i][:, 1024:1536], abf[i][:, 1536:2048]] for i in range(NP)]
    IMb = [abf[i][:, 2048:2560] for i in range(NP)]
    PTb = [[abf[i][:, 2560:3072], abf[i][:, 3072:3584]] for i in range(NP)]
    for i in range(NP):
        raw.append(arena[i][:, 0:515])
        cv.append(arena[i][:, 516:1028])
        cs.append(arena[i][:, 1028:1540])
        rn.append(arena[i][:, 1540:2052])
        sq.append(arena[i][:, 2052:2308].bitcast(BF16))
        for lst in (r_raw, r_cv, r_cs, r_sq, r_rn):
            lst.append(Res())
    HS = 4
    AT = [sb("AT%d" % i, [128, 512], BF16, ph) for i in range(HS)]
    u32 = [sb("u32%d" % i, [128, 512], F32, ph) for i in range(HS)]
    wT = [sb("wT%d" % i, [128, 512], BF16, ph) for i in range(HS)]
    r_e1m = [Res() for _ in range(NP)]
    r_tg = [Res() for _ in range(NP)]
    r_M = [[Res(), Res()] for _ in range(NP)]
    r_MT = [[Res(), Res()] for _ in range(NP)]
    r_IM = [Res() for _ in range(NP)]
    r_PT = [[Res(), Res()] for _ in range(NP)]
    r_AT = [Res() for _ in range(HS)]
    r_u32 = [Res() for _ in range(HS)]
    r_wT = [Res() for _ in range(HS)]
    vn = sb("vn", [128, 512], BF16, ph)
    tq = sb("tq", [128, 512], F32, ph)
    o32b = [sb("o32%d" % i, [128, 512], F32, ph) for i in range(2)]
    r_o32b = [Res(), Res()]
    osq = sb("osq", [128, 512], F32, ph)
    oss = sb("oss", [128, 4], F32, ph)
    on = sb("on", [128, 512], BF16, ph)
    zs = sb("zs", [128, 512], F32, ph)
    zsw = zs
    S32 = sb("S32", [128, 512], F32, ph)
    Sb = sb("Sb", [128, 512], BF16, ph)
    r_vn, r_tq, r_osq, r_oss, r_on, r_zs, r_zsw, r_S32, r_Sb = [Res() for _ in range(9)]
    fw.op("dve", lambda e: e.memset(S32[:], 0.0), [], [r_S32])
    fw.op("dve", lambda e: e.memset(Sb[:], 0.0), [], [r_Sb])

    def v4(ap):
        return ap.rearrange("p (h d) -> p h d", h=4)

    idf_b = ident_f.unsqueeze(1).to_broadcast([128, 4, 128])
    idb_b = ident_b.unsqueeze(1).to_broadcast([128, 4, 128])

    def sec_stream(arg, s):
        n, h, typ = arg
        par = n % 2
        tok = slice(n * 512, (n + 1) * 512)
        sec = typ * 4 + h
        col0 = typ * 512 + h * 128
        pr, rpr = yield from pget()
        for k in range(8):
            mm(pr[:, 0:512], wdn[:, k, col0:col0 + 128], hT[:, k, tok], k == 0, k == 7,
               hres(n) + wres(wp, col0, col0 + 128), [rpr])
        if n == 0:
            fw.op("pool", lambda e: e.memset(raw[s][:, 0:3], 0.0), [], [r_raw[s]])
        else:
            fw.op("pool", lambda e: e.tensor_copy(out=raw[s][:, 0:3], in_=halo[:, sec, :]),
                  [r_halo[sec]], [r_raw[s]])
        yield
        fw.op("act", lambda e: e.activation(out=raw[s][:, 3:515], in_=pr[:, 0:512], func=AF.Copy),
              [rpr], [r_raw[s]])
        psp.rel(rpr)
        yield
        fw.op("pool", lambda e: e.tensor_copy(out=halo[:, sec, :], in_=raw[s][:, 512:515]),
              [r_raw[s]], [r_halo[sec]])
        cwo = sec * 4
        fw.op("act", lambda e: e.activation(out=cv[s][:], in_=raw[s][:, 0:512], func=AF.Copy,
                                            scale=cfs("cw", cwo, cwo + 1)), [r_raw[s], r_cf], [r_cv[s]])
        yield
        for j in range(1, 4):
            fw.op("dve", lambda e: e.scalar_tensor_tensor(out=cv[s][:], in0=raw[s][:, j:j + 512],
                                                          scalar=cfs("cw", cwo + j, cwo + j + 1),
                                                          in1=cv[s][:], op0=ALU.mult, op1=ALU.add),
                  [r_raw[s], r_cf, r_cv[s]], [r_cv[s]])
            yield
        fw.op("act", lambda e: e.activation(out=cs[s][:], in_=cv[s][:], func=AF.Silu), [r_cv[s]], [r_cs[s]])
        yield
        if typ < 2:
            fw.op("act", lambda e: e.activation(out=sq[s][:], in_=cs[s][:], func=AF.Square), [r_cs[s]], [r_sq[s]])
            yield
            pss, rpss = yield from pget()
            mm(pss[:, 0:512], ones_b, sq[s][:], True, True, [r_sq[s], r_cb], [rpss])
            yield
            fw.op("act", lambda e: e.activation(out=rn[s][:], in_=pss[:, 0:512], func=AF.Ln, bias=eps_c,
                                                scale=1.0), [rpss, r_cf], [r_rn[s]])
            psp.rel(rpss)
            yield
            fw.op("act", lambda e: e.activation(out=rn[s][:], in_=rn[s][:], func=AF.Exp, scale=-0.5),
                  [r_rn[s]], [r_rn[s]])
            yield
        if typ == 0:
            fw.op("dve", lambda e: e.scalar_tensor_tensor(out=qT[par][:, h, :], in0=cs[s][:], scalar=QSCALE,
                                                          in1=rn[s][:], op0=ALU.mult, op1=ALU.mult),
                  [r_cs[s], r_rn[s]], [r_qT[par][h]])
            return
        if typ == 1:
            fw.op("dve", lambda e: e.tensor_tensor(out=cv[s][:], in0=cs[s][:], in1=rn[s][:], op=ALU.mult),
                  [r_cs[s], r_rn[s]], [r_cv[s]])
            yield
            fw.op("act", lambda e: e.activation(out=kT[par][:, h, :], in_=cv[s][:], func=AF.Copy),
                  [r_cv[s]], [r_kT[par][h]])
            src, rsrc = cv[s], r_cv[s]
        else:
            src, rsrc = cs[s], r_cs[s]
        ptr, rptr = yield from pget()
        for j in range(4):
            fw.op("pe", lambda e: e.transpose(out=ptr[:, j * 128:(j + 1) * 128],
                                              in_=src[:, j * 128:(j + 1) * 128], identity=ident_f),
                  [rsrc, r_cf], [rptr])
        yield
        pv = ptr[:, 0:512].rearrange("p (j d) -> p j d", j=4)
        if typ == 1:
            sc1 = sckbg[:, 4 * n:4 * n + 4, h:h + 1].to_broadcast([128, 4, 128])
            sc2 = sckd[:, 4 * n:4 * n + 4, h:h + 1].to_broadcast([128, 4, 128])
            fw.op("dve", lambda e: e.tensor_tensor(out=kbg[par][:, :, h, :], in0=pv, in1=sc1, op=ALU.mult),
                  [rptr, r_a0], [r_kbg[par][h]])
            yield
            fw.op("dve", lambda e: e.tensor_tensor(out=kd[par][:, :, h, :], in0=pv, in1=sc2, op=ALU.mult),
                  [rptr, r_a0], [r_kd[par][h]])
        else:
            sc3 = beta[:, 4 * n:4 * n + 4, h:h + 1].to_broadcast([128, 4, 128])
            fw.op("dve", lambda e: e.tensor_tensor(out=vb[par][:, :, h, :], in0=pv, in1=sc3, op=ALU.mult),
                  [rptr, r_a0], [r_vb[par][h]])
        psp.rel(rptr)

    def pre_stream(i, s):
        n, j = i // 4, i % 4
        par = n % 2
        hs_ = i % HS
        tk = slice(j * 128, (j + 1) * 128)
        pB, rpB = yield from pget()
        for h in range(4):
            gb = g[:, i, h:h + 1].to_broadcast([128, 128])
            mm(pB[:, h * 128:(h + 1) * 128], gb, cfs("tri"), True, False, [r_a0, r_cf], [rpB])
            mm(pB[:, h * 128:(h + 1) * 128], ident_b, cbs("x1"), False, True, [r_cb], [rpB])
        yield
        for h in range(4):
            fw.op("act", lambda e: e.activation(out=e1m[s][:, h * 128:(h + 1) * 128],
                                                in_=pB[:, h * 128:(h + 1) * 128], func=AF.Exp,
                                                bias=gc[:, i, h:h + 1], scale=-1.0), [rpB, r_a0], [r_e1m[s]])
        psp.rel(rpB)
        pG, rpG = yield from pget()
        for h in range(4):
            mm(pG[:, h * 128:(h + 1) * 128], kT[par][:, h, tk], kT[par][:, h, tk], True, True,
               [r_kT[par][h]], [rpG])
        yield
        nb_b = nbeta[:, i, :].unsqueeze(2).to_broadcast([128, 4, 128])
        fw.op("dve", lambda e: e.tensor_tensor(out=v4(tg[s][:]), in0=v4(pG[:, 0:512]), in1=nb_b, op=ALU.mult),
              [rpG, r_a0], [r_tg[s]])
        psp.rel(rpG)
        yield
        fw.op("dve", lambda e: e.tensor_tensor(out=Mb[s][0][:], in0=tg[s][:], in1=e1m[s][:], op=ALU.mult),
              [r_tg[s], r_e1m[s]], [r_M[s][0]])
        pE, rpE = yield from pget()
        for h in range(4):
            fw.op("pe", lambda e: e.transpose(out=pE[:, h * 128:(h + 1) * 128],
                                              in_=e1m[s][:, h * 128:(h + 1) * 128], identity=ident_f),
                  [r_e1m[s], r_cf], [rpE])
        yield
        pT, rpT = yield from pget()
        pTb = pT[:].bitcast(BF16)
        for h in range(4):
            fw.op("pe", lambda e: e.transpose(out=pTb[:, h * 128:(h + 1) * 128],
                                              in_=Mb[s][0][:, h * 128:(h + 1) * 128], identity=ident_b),
                  [r_M[s][0], r_cb], [rpT])
        fw.op("dve", lambda e: e.tensor_tensor(out=v4(tg[s][:]), in0=v4(pE[:, 0:512]), in1=idf_b, op=ALU.add),
              [rpE, r_cf, r_tg[s]], [r_tg[s]])
        psp.rel(rpE)
        yield
        fw.op("act", lambda e: e.activation(out=MTb[s][0][:], in_=pTb[:, 0:512], func=AF.Copy), [rpT], [r_MT[s][0]])
        yield
        fw.op("dve", lambda e: e.tensor_tensor(out=v4(PTb[s][0][:]), in0=v4(pTb[:, 0:512]), in1=idb_b, op=ALU.add),
              [rpT, r_cb], [r_PT[s][0]])
        psp.rel(rpT)
        pQ, rpQ = yield from pget()
        for h in range(4):
            mm(pQ[:, h * 128:(h + 1) * 128], kT[par][:, h, tk], qT[par][:, h, tk], True, True,
               [r_kT[par][h], r_qT[par][h]], [rpQ])
        yield
        fw.op("dve", lambda e: e.tensor_tensor(out=AT[hs_][:], in0=pQ[:, 0:512], in1=tg[s][:], op=ALU.mult),
              [rpQ, r_tg[s]], [r_AT[hs_]])
        psp.rel(rpQ)
        cur = 0
        for lev in range(1, 6):
            nxt = 1 - cur
            pM, rpM = yield from pget()
            for h in range(4):
                hs = slice(h * 128, (h + 1) * 128)
                mm(pM[:, hs], MTb[s][cur][:, hs], Mb[s][cur][:, hs], True, True, [r_MT[s][cur], r_M[s][cur]], [rpM])
            if lev < 5:
                pMT, rpMT = yield from pget()
                for h in range(4):
                    hs = slice(h * 128, (h + 1) * 128)
                    mm(pMT[:, hs], Mb[s][cur][:, hs], MTb[s][cur][:, hs], True, True,
                       [r_MT[s][cur], r_M[s][cur]], [rpMT])
            yield
            fw.op("dve", lambda e: e.tensor_tensor(out=v4(IMb[s][:]), in0=v4(pM[:, 0:512]), in1=idf_b, op=ALU.add),
                  [rpM, r_cf], [r_IM[s]])
            if lev < 5:
                fw.op("act", lambda e: e.activation(out=MTb[s][nxt][:], in_=pMT[:, 0:512], func=AF.Copy),
                      [rpMT], [r_MT[s][nxt]])
                psp.rel(rpMT)
            yield
            if lev < 5:
                fw.op("act", lambda e: e.activation(out=Mb[s][nxt][:], in_=pM[:, 0:512], func=AF.Copy),
                      [rpM], [r_M[s][nxt]])
            psp.rel(rpM)
            pP, rpP = yield from pget()
            for h in range(4):
                hs = slice(h * 128, (h + 1) * 128)
                mm(pP[:, hs], IMb[s][:, hs], PTb[s][cur][:, hs], True, True, [r_IM[s], r_PT[s][cur]], [rpP])
            yield
            fw.op("act" if lev % 2 else "dve",
                  (lambda e: e.activation(out=PTb[s][nxt][:], in_=pP[:, 0:512], func=AF.Copy)) if lev % 2 else
                  (lambda e: e.tensor_copy(out=PTb[s][nxt][:], in_=pP[:, 0:512])), [rpP], [r_PT[s][nxt]])
            psp.rel(rpP)
            cur = nxt
            yield
        PTf, r_PTf = PTb[s][cur], r_PT[s][cur]
        pU, rpU = yield from pget()
        for h in range(4):
            hs = slice(h * 128, (h + 1) * 128)
            mm(pU[:, hs], PTf[:, hs], vb[par][:, j, h, :], True, True, [r_PTf, r_vb[par][h]], [rpU])
        pW, rpW = yield from pget()
        for h in range(4):
            hs = slice(h * 128, (h + 1) * 128)
            mm(pW[:, hs], kbg[par][:, j, h, :], PTf[:, hs], True, True, [r_PTf, r_kbg[par][h]], [rpW])
        yield
        fw.op("act", lambda e: e.activation(out=u32[hs_][:], in_=pU[:, 0:512], func=AF.Copy), [rpU], [r_u32[hs_]])
        psp.rel(rpU)
        fw.op("dve", lambda e: e.tensor_copy(out=wT[hs_][:], in_=pW[:, 0:512]), [rpW], [r_wT[hs_]])
        psp.rel(rpW)

    def chain_stream(i):
        n, j = i // 4, i % 4
        par = n % 2
        s = i % HS
        o32 = o32b[i % 2]
        r_o32 = r_o32b[i % 2]
        tk = slice(j * 128, (j + 1) * 128)
        for c in range(2):
            R = slice(c * 64, (c + 1) * 64)
            pSw, rpSw = yield from pget()
            for h in range(4):
                hs = slice(h * 128, (h + 1) * 128)
                mm(pSw[:, hs], wT[s][:, hs], Sb[:, hs], True, True, [r_wT[s], r_Sb], [rpSw])
            pSq, rpSq = yield from pget()
            for h in range(4):
                hs = slice(h * 128, (h + 1) * 128)
                mm(pSq[:, hs], qT[par][:, h, tk], Sb[:, hs], True, True, [r_qT[par][h], r_Sb], [rpSq])
            yield
            egl_b = egl[:, c, i, :].unsqueeze(2).to_broadcast([128, 4, 128])
            fw.op("dve", lambda e: e.tensor_tensor(out=v4(S32[:]), in0=v4(S32[:]), in1=egl_b, op=ALU.mult),
                  [r_S32, r_a0], [r_S32])
            fw.op("dve", lambda e: e.tensor_tensor(out=vn[R, :], in0=u32[s][R, :], in1=pSw[R, 0:512],
                                                   op=ALU.subtract), [r_u32[s], rpSw], [r_vn])
            psp.rel(rpSw)
            yield
            pSs, rpSs = yield from pget()
            for h in range(4):
                hs = slice(h * 128, (h + 1) * 128)
                mm(pSs[:, hs], kd[par][R, j, h, :], vn[R, hs], True, True, [r_kd[par][h], r_vn], [rpSs])
            pSa, rpSa = yield from pget()
            for h in range(4):
                hs = slice(h * 128, (h + 1) * 128)
                mm(pSa[:, hs], AT[s][R, hs], vn[R, hs], True, True, [r_AT[s], r_vn], [rpSa])
            egc_b = egc[R, i, :].unsqueeze(2).to_broadcast([64, 4, 128])
            fw.op("dve", lambda e: e.tensor_tensor(out=v4(tq[R, :]), in0=v4(pSq[R, 0:512]), in1=egc_b,
                                                   op=ALU.mult), [rpSq, r_a0], [r_tq])
            psp.rel(rpSq)
            yield
            fw.op("dve", lambda e: e.tensor_tensor(out=S32[:], in0=S32[:], in1=pSs[:, 0:512], op=ALU.add),
                  [r_S32, rpSs], [r_S32])
            psp.rel(rpSs)
            yield
            fw.op("act", lambda e: e.activation(out=Sb[:], in_=S32[:], func=AF.Copy), [r_S32], [r_Sb])
            fw.op("dve", lambda e: e.tensor_tensor(out=o32[R, :], in0=tq[R, :], in1=pSa[R, 0:512], op=ALU.add),
                  [r_tq, rpSa], [r_o32])
            psp.rel(rpSa)
            yield

    def epi_stream(i):
        n, j = i // 4, i % 4
        o32 = o32b[i % 2]
        r_o32 = r_o32b[i % 2]
        pz, rpz = yield from pget()
        for k in range(8):
            mm(pz[:, 0:512], hT[:, k, i * 128:(i + 1) * 128], wdn[:, k, 1536:2048], k == 0, k == 7,
               [r_hT[i]] + wres(wp, 1536, 2048), [rpz])
        yield
        fw.op("act", lambda e: e.activation(out=zs[:], in_=pz[:, 0:512], func=AF.Silu), [rpz], [r_zs])
        psp.rel(rpz)
        fw.op("act", lambda e: e.activation(out=osq[:], in_=o32[:], func=AF.Square), [r_o32], [r_osq])
        yield
        fw.op("dve", lambda e: e.tensor_tensor(out=zsw[:], in0=zs[:], in1=cfs("dnw"), op=ALU.mult),
              [r_zs, r_cf], [r_zsw])
        fw.op("dve", lambda e: e.tensor_reduce(out=oss[:], in_=v4(osq[:]), axis=AX.X, op=ALU.add),
              [r_osq], [r_oss])
        yield
        fw.op("act", lambda e: e.activation(out=oss[:], in_=oss[:], func=AF.Ln, bias=eps_c, scale=1.0 / 128),
              [r_oss, r_cf], [r_oss])
        fw.op("act", lambda e: e.activation(out=oss[:], in_=oss[:], func=AF.Exp, scale=-0.5), [r_oss], [r_oss])
        yield
        oss_b = oss[:].unsqueeze(2).to_broadcast([128, 4, 128])
        fw.op("dve", lambda e: e.tensor_tensor(out=v4(osq[:]), in0=v4(o32[:]), in1=oss_b, op=ALU.mult),
              [r_o32, r_oss, r_osq], [r_osq])
        fw.op("dve", lambda e: e.tensor_tensor(out=on[:], in0=osq[:], in1=zsw[:], op=ALU.mult),
              [r_osq, r_zsw], [r_on])
        yield
        pO, rpO = yield from pget()
        pOb = pO[:].bitcast(BF16)
        for h in range(4):
            fw.op("pe", lambda e: e.transpose(out=pOb[:, h * 128:(h + 1) * 128],
                                              in_=on[:, h * 128:(h + 1) * 128], identity=ident_b),
                  [r_on, r_cb], [rpO])
        yield
        fw.op("act", lambda e: e.activation(out=oT[:, 0:4, i * 128:(i + 1) * 128], in_=v4(pOb[:, 0:512]),
                                            func=AF.Copy), [rpO], [r_oT[h][n] for h in range(4)])
        psp.rel(rpO)

    def sec_items(n):
        return [(n, h, typ) for h in range(4) for typ in range(2)] + [(n, h, 2) for h in range(4)]

    def chain(gens):
        for g_ in gens:
            yield from g_

    def tiles_gen():
        for i in range(NT):
            yield ("need_sections", i // 4)
            yield from pre_stream(i, i % NP)
            yield ("pre_done", i)

    import os
    nbanks = int(os.environ.get('DN_BANKS', '4'))
    ntl = 4 * nbanks
    sec_done = [False] * 5
    pre_done = [False] * (NT + 1)
    chain_done = [False] * (NT + 1)
    epi_done = [False] * (NT + 1)

    def sections_meta():
        for n in range(nbanks):
            while n >= 2 and not chain_done[4 * (n - 2) + 3]:
                yield
            if n == 0:
                yield from streams_gen(sec_items(n), NS + NP, sec_stream)
                fw.barrier_all()
            else:
                yield from streams_gen(sec_items(n), NS, sec_stream)
            sec_done[n] = True

    def pre_one(i, s):
        while not sec_done[i // 4]:
            yield
        while i >= HS and not chain_done[i - HS]:
            yield
        yield from pre_stream(i, s)
        pre_done[i] = True

    def pre_meta():
        yield from streams_gen(list(range(ntl)), NP, pre_one, stagger=12)

    def chain_meta():
        for i in range(ntl):
            while not pre_done[i]:
                yield
            while i >= 2 and not epi_done[i - 2]:
                yield
            yield from chain_stream(i)
            chain_done[i] = True

    def epi_meta():
        for i in range(ntl):
            while not chain_done[i]:
                yield
            yield from epi_stream(i)
            epi_done[i] = True

    metas = [sections_meta(), pre_meta(), chain_meta(), epi_meta()]
    stall = 0
    last = fw.ninst
    while metas:
        for m_ in list(metas):
            try:
                next(m_)
            except StopIteration:
                metas.remove(m_)
        if fw.ninst == last:
            stall += 1
            if stall > 100000:
                raise RuntimeError("stream scheduler livelock")
        else:
            stall = 0
            last = fw.ninst


def phase_swa(nc, fw, ph, sb, psget, mm, load_w, wres, cfs, cbs, hT, oT, r_hT, r_oT, r_cf, r_cb, hres,
              wswa_d, rope_d, rot, pst, r_ps):
    ident_b = cbs("ident")
    ones_b = cbs("ones")
    eps_c = cfs("eps")
    psp = PSP(pst, r_ps)

    def pget():
        n = 0
        while True:
            p = psp.try_get()
            if p is not None:
                return p
            n += 1
            if n > 100000:
                raise RuntimeError("PSUM starvation")
            yield

    rope = sb("rope", [32, 2, T], F32, ph)
    r_rope = Res()
    fw.dma(rope[:], rope_d[:, :, :], writes=[r_rope])
    wsw = sb("wsw", [128, 8, 1280], BF16, ph)
    wr = {"qk": Res(), "v": Res(), "z": Res()}
    wrange = {"qk": (0, 768), "v": (768, 1152), "z": (1152, 1280)}

    def load_head(h, part):
        c0, c1 = wrange[part]
        srcv = wswa_d[h].rearrange("(k p) c -> p k c", p=128)
        fw.dma(wsw[:, :, c0:c1], srcv[:, :, c0:c1], writes=[wr[part]], q="pool")

    qk = [[sb("qk%d%d" % (t_, g_), [128, T], BF16, ph) for g_ in range(3)] for t_ in range(2)]
    r_qk = [[[Res() for _ in range(4)] for _ in range(3)] for _ in range(2)]
    V = [sb("V%d" % g_, [128, 16, 128], BF16, ph) for g_ in range(3)]
    r_V = [[Res() for _ in range(16)] for _ in range(3)]
    NQ = 6
    sq = [sb("ssq%d" % i, [128, 512], BF16, ph) for i in range(NQ)]
    rn = [sb("srn%d" % i, [128, 512], F32, ph) for i in range(NQ)]
    qn = [sb("sqn%d" % i, [128, 512], F32, ph) for i in range(NQ)]
    qnb = [sb("sqnb%d" % i, [32, 512], BF16, ph) for i in range(NQ)]
    r_sq = [Res() for _ in range(NQ)]
    r_rn = [Res() for _ in range(NQ)]
    r_qn = [Res() for _ in range(NQ)]
    r_qnb = [Res() for _ in range(NQ)]
    NA = 3
    zsall = sb("zsall", [128, T], F32, ph)
    r_zsall = [Res() for _ in range(4)]
    ee = [[sb("see%d%d" % (s, i), [128, 512], BF16, ph) for i in range(2)] for s in range(NA)]
    PP = ee
    r_ee = [[Res(), Res()] for _ in range(NA)]
    r_PP = r_ee
    rden = [sb("srden%d" % s, [128, 512], F32, ph) for s in range(NA)]
    tn = rden
    r_rden = [Res() for _ in range(NA)]
    r_tn = r_rden

    for part in ("qk", "v", "z"):
        load_head(0, part)

    for h in range(4):
        def qk_stream(arg, s):
            g_, typ, n = arg
            c0 = typ * 384 + g_ * 128
            dst = qk[typ][g_]
            wname = "qw" if typ == 0 else "kw"
            tok = slice(n * 512, (n + 1) * 512)
            pr, rpr = yield from pget()
            for k in range(8):
                mm(pr[:, 0:512], wsw[:, k, c0:c0 + 128], hT[:, k, tok], k == 0, k == 7, hres(n) + [wr["qk"]], [rpr])
            yield
            fw.op("act", lambda e: e.activation(out=sq[s][:], in_=pr[:, 0:512], func=AF.Square), [rpr], [r_sq[s]])
            yield
            pss, rpss = yield from pget()
            mm(pss[:, 0:512], ones_b, sq[s][:], True, True, [r_sq[s], r_cb], [rpss])
            yield
            fw.op("act", lambda e: e.activation(out=rn[s][:], in_=pss[:, 0:512], func=AF.Ln, bias=eps_c,
                                                scale=1.0 / 128), [rpss, r_cf], [r_rn[s]])
            psp.rel(rpss)
            yield
            fw.op("act", lambda e: e.activation(out=rn[s][:], in_=rn[s][:], func=AF.Exp, scale=-0.5),
                  [r_rn[s]], [r_rn[s]])
            yield
            fw.op("dve", lambda e: e.scalar_tensor_tensor(out=qn[s][:], in0=pr[:, 0:512],
                                                          scalar=cfs(wname, g_, g_ + 1), in1=rn[s][:],
                                                          op0=ALU.mult, op1=ALU.mult),
                  [rpr, r_rn[s], r_cf], [r_qn[s]])
            psp.rel(rpr)
            yield
            fw.op("dve", lambda e: e.tensor_tensor(out=qnb[s][:], in0=qn[s][0:32, :], in1=rope[:, 1, tok],
                                                   op=ALU.mult), [r_qn[s], r_rope], [r_qnb[s]])
            fw.op("act", lambda e: e.activation(out=dst[:, tok], in_=qn[s][:], func=AF.Copy),
                  [r_qn[s]], [r_qk[typ][g_][n]])
            yield
            prt, rprt = yield from pget()
            mm(prt[0:32, 0:512], cbs("rm")[0:32, :], qnb[s][:], True, True, [r_qnb[s], r_cb], [rprt])
            fw.op("dve", lambda e: e.tensor_tensor(out=rn[s][0:32, :], in0=qn[s][0:32, :], in1=rope[:, 0, tok],
                                                   op=ALU.mult), [r_qn[s], r_rope, r_rn[s]], [r_rn[s]])
            yield
            fw.op("dve", lambda e: e.tensor_tensor(out=dst[0:32, tok], in0=rn[s][0:32, :], in1=prt[0:32, 0:512],
                                                   op=ALU.add), [r_rn[s], rprt], [r_qk[typ][g_][n]])
            psp.rel(rprt)

        def v_stream(arg, s):
            g_, b4 = arg
            c0 = 768 + g_ * 128
            pv, rpv = yield from pget()
            for bb in range(4):
                blk = b4 * 4 + bb
                if g_ == 0:
                    tsl = slice(blk * 128, (blk + 1) * 128)
                    hr = [r_hT[blk]]
                elif g_ == 1:
                    n_, r_ = blk // 4, blk % 4
                    tsl = slice(512 * n_ + r_, 512 * (n_ + 1), 4)
                    hr = hres(n_)
                else:
                    tsl = slice(blk, T, 16)
                    hr = r_hT
                for k in range(8):
                    mm(pv[:, bb * 128:(bb + 1) * 128], hT[:, k, tsl], wsw[:, k, c0:c0 + 128], k == 0, k == 7,
                       list(hr) + [wr["v"]], [rpv])
                yield
            vout = V[g_][:, b4 * 4:(b4 + 1) * 4, :]
            vin = pv[:, 0:512].rearrange("p (b d) -> p b d", b=4)
            vres = [r_V[g_][b4 * 4 + bb] for bb in range(4)]
            if (g_ * 4 + b4) % 2 == 0:
                fw.op("act", lambda e: e.activation(out=vout, in_=vin, func=AF.Copy), [rpv], vres)
            else:
                fw.op("dve", lambda e: e.tensor_copy(out=vout, in_=vin), [rpv], vres)
            psp.rel(rpv)

        def mixed_stream(arg, s):
            if arg[0] == "v":
                return v_stream(arg[1:], s)
            return qk_stream(arg[1:], s)

        qk_items = [("qk", g_, typ, n) for g_ in range(3) for typ in range(2) for n in range(4)]
        v_items = [("v", g_, b4) for g_ in range(3) for b4 in range(4)]
        items = []
        for ii in range(12):
            items += qk_items[2 * ii:2 * ii + 2] + [v_items[ii]]
        run_streams(items, NQ, mixed_stream, stagger=2)
        if h < 3:
            load_head(h + 1, "qk")
        if h < 3:
            load_head(h + 1, "v")

        def attn_stream(m, s):
            num, r_num = yield from pget()
            den, r_den = yield from pget()
            first = [True, True]

            def st(idx):
                f = first[idx]
                first[idx] = False
                return f

            def geom(g_):
                def cols(n_, r_):
                    if g_ == 0:
                        tt = 4 * n_ + r_
                        return slice(tt * 128, (tt + 1) * 128)
                    return slice(512 * n_ + r_, 512 * (n_ + 1), 4)

                def outsl(r_):
                    if g_ == 0:
                        return slice(r_ * 128, (r_ + 1) * 128)
                    return slice(r_, 512, 4)

                def prevblk(n_, r_):
                    if g_ == 0:
                        tt = 4 * n_ + r_ - 1
                        if tt < 0:
                            return None
                        return (tt // 4, tt % 4)
                    if n_ == 0:
                        return None
                    return (n_ - 1, r_)
                return cols, outsl, prevblk

            groups = []
            for g_ in range(2):
                groups.append((g_, "cur"))
                _, _, prevblk = geom(g_)
                if any(prevblk(m, r_) is not None for r_ in range(4)):
                    groups.append((g_, "prev"))
            groups.append((2, "cur"))
            pend = None
            gi = 0

            def emit_pv(p):
                g_, kind, buf, rbuf, lo = p
                if g_ == 2:
                    for r_ in range(16):
                        mm(num[:, slice(r_, 512, 16)], V[2][:, r_, :], buf[:, r_ * 32:(r_ + 1) * 32], st(0), False,
                           [r_V[2][r_], rbuf], [r_num])
                        mm(den[:, slice(r_, 512, 16)], ones_b, buf[:, r_ * 32:(r_ + 1) * 32], st(1), False,
                           [r_cb, rbuf], [r_den])
                    return
                cols, outsl, prevblk = geom(g_)
                for r_ in range(4):
                    if kind == "cur":
                        vb_ = 4 * m + r_
                    else:
                        pb = prevblk(m, r_)
                        if pb is None:
                            continue
                        vb_ = 4 * pb[0] + pb[1]
                    mm(num[:, outsl(r_)], V[g_][:, vb_, :], buf[:, r_ * 128:(r_ + 1) * 128], st(0), False,
                       [r_V[g_][vb_], rbuf], [r_num])
                    mm(den[:, outsl(r_)], ones_b, buf[:, r_ * 128:(r_ + 1) * 128], st(1), False,
                       [r_cb, rbuf], [r_den])

            def emit_scores(pc, rpc, g_, kind):
                lo = 0
                if g_ == 2:
                    qTt, kTt = qk[0][2], qk[1][2]
                    for r_ in range(16):
                        mm(pc[:, r_ * 32:(r_ + 1) * 32], kTt[:, slice(r_, T, 16)],
                           qTt[:, slice(512 * m + r_, 512 * (m + 1), 16)], True, True,
                           [r_qk[0][2][m]] + r_qk[1][2], [rpc])
                    mask = cbs("m2", m * 512, (m + 1) * 512)
                else:
                    qTt, kTt = qk[0][g_], qk[1][g_]
                    cols, outsl, prevblk = geom(g_)
                    if kind == "cur":
                        for r_ in range(4):
                            mm(pc[:, r_ * 128:(r_ + 1) * 128], kTt[:, cols(m, r_)], qTt[:, cols(m, r_)], True, True,
                               [r_qk[0][g_][m], r_qk[1][g_][m]], [rpc])
                        mask = cbs("masku")
                    else:
                        prs = [(r_, prevblk(m, r_)) for r_ in range(4) if prevblk(m, r_) is not None]
                        lo = prs[0][0] * 128
                        for r_, (pn, pr_) in prs:
                            mm(pc[:, r_ * 128:(r_ + 1) * 128], kTt[:, cols(pn, pr_)], qTt[:, cols(m, r_)], True, True,
                               [r_qk[0][g_][m], r_qk[1][g_][pn], r_qk[1][g_][m]], [rpc])
                        mask = cbs("maskl")
                return pc, rpc, lo, mask

            pc0, rpc0 = yield from pget()
            nxt = emit_scores(pc0, rpc0, *groups[0])
            yield
            for gi, (g_, kind) in enumerate(groups):
                b = gi % 2
                pc, rpc, lo, mask = nxt
                fw.op("act", lambda e: e.activation(out=ee[s][b][:, lo:512], in_=pc[:, lo:512], func=AF.Exp,
                                                    scale=QSCALE), [rpc], [r_ee[s][b]])
                psp.rel(rpc)
                yield
                meng = "dve"
                fw.op(meng, lambda e: e.tensor_tensor(out=PP[s][b][:, lo:512], in0=ee[s][b][:, lo:512],
                                                      in1=mask[:, lo:512], op=ALU.mult),
                      [r_ee[s][b], r_cb], [r_PP[s][b]])
                if gi + 1 < len(groups):
                    pcn, rpcn = yield from pget()
                    nxt = emit_scores(pcn, rpcn, *groups[gi + 1])
                yield
                emit_pv((g_, kind, PP[s][b], r_PP[s][b], lo))
                yield
            tok = slice(m * 512, (m + 1) * 512)
            fw.op("act", lambda e: e.activation(out=rden[s][:], in_=den[:, 0:512], func=AF.Ln), [r_den], [r_rden[s]])
            psp.rel(r_den)
            yield
            fw.op("act", lambda e: e.activation(out=rden[s][:], in_=rden[s][:], func=AF.Exp, scale=-1.0),
                  [r_rden[s]], [r_rden[s]])
            yield
            fw.op("dve", lambda e: e.tensor_tensor(out=tn[s][:], in0=num[:, 0:512], in1=rden[s][:], op=ALU.mult),
                  [r_num, r_rden[s]], [r_tn[s]])
            psp.rel(r_num)
            yield
            fw.op("dve", lambda e: e.tensor_tensor(out=oT[:, 4 + h, tok], in0=tn[s][:], in1=zsall[:, tok], op=ALU.mult),
                  [r_tn[s], r_zsall[m]], [r_oT[4 + h][m]])

        for m in range(4):
            tok = slice(m * 512, (m + 1) * 512)
            pz, rpz = psp.get()
            for k in range(8):
                mm(pz[:, 0:512], wsw[:, k, 1152:1280], hT[:, k, tok], k == 0, k == 7, hres(m) + [wr["z"]], [rpz])
            fw.op("act", lambda e: e.activation(out=zsall[:, tok], in_=pz[:, 0:512], func=AF.Silu), [rpz], [r_zsall[m]])
            psp.rel(rpz)
        run_streams(list(range(4)), NA, attn_stream, stagger=5)
        if h < 3:
            load_head(h + 1, "z")


def phase_out(nc, fw, ph, sb, psget, mm, load_w, wres, cfs, cbs, hT, oT, r_hT, r_oT, r_cf, r_cb, hres,
              wg_d, wbd_d, wbs_d, wo_d, x_d, out_d, r_out):
    wbd = sb("wbd", [128, 4, 1024], BF16, ph)
    wbs = sb("wbs", [128, 4, 1024], BF16, ph)
    wpbd = load_w(wbd, wbd_d, 4, 1024)
    wpbs = load_w(wbs, wbs_d, 4, 1024)
    wg = sb("wg", [128, 8, 2048], BF16, ph)
    wo = sb("wo", [128, 8, 1024], BF16, ph)
    wgv = wg_d.rearrange("(k p) c -> p k c", p=128)
    wpg = []
    for c in range(0, 8, 2):
        for base in (0, 1024):
            c0, c1 = base + c * 128, base + (c + 2) * 128
            r_ = Res()
            fw.dma(wg[:, :, c0:c1], wgv[:, :, c0:c1], writes=[r_], q="pool")
            wpg.append((c0, c1, r_))
    wpo = load_w(wo, wo_d, 8, 1024)
    mT = sb("mT", [128, 8, 512], BF16, ph)
    r_mT = [Res() for _ in range(8)]
    sg = [sb("sg%d" % i, [128, 512], F32, ph) for i in range(2)]
    r_sg = [Res(), Res()]
    m1 = [sb("m1%d" % i, [128, 512], F32, ph) for i in range(2)]
    r_m1 = [Res(), Res()]
    xt = [sb("oxt%d" % i, [128, D], F32, ph) for i in range(2)]
    r_xt = [Res(), Res()]
    ot = [sb("oot%d" % i, [128, D], F32, ph) for i in range(2)]
    for n in range(4):
        tok = slice(n * 512, (n + 1) * 512)
        for c in range(8):
            cs_ = slice(c * 128, (c + 1) * 128)
            pyd, rpyd = psget()
            for hh in range(4):
                mm(pyd[:, 0:512], wbd[:, hh, cs_], oT[:, hh, tok], hh == 0, hh == 3,
                   [r_oT[hh][n]] + wres(wpbd, c * 128, (c + 1) * 128), [rpyd])
            pys, rpys = psget()
            for hh in range(4):
                mm(pys[:, 0:512], wbs[:, hh, cs_], oT[:, 4 + hh, tok], hh == 0, hh == 3,
                   [r_oT[4 + hh][n]] + wres(wpbs, c * 128, (c + 1) * 128), [rpys])
            pgd, rpgd = psget()
            for k in range(8):
                mm(pgd[:, 0:512], wg[:, k, cs_], hT[:, k, tok], k == 0, k == 7,
                   hres(n) + wres(wpg, c * 128, (c + 1) * 128), [rpgd])
            pgs, rpgs = psget()
            for k in range(8):
                mm(pgs[:, 0:512], wg[:, k, 1024 + c * 128:1024 + (c + 1) * 128], hT[:, k, tok], k == 0, k == 7,
                   hres(n) + wres(wpg, 1024 + c * 128, 1024 + (c + 1) * 128), [rpgs])
            fw.op("act", lambda e: e.activation(out=sg[0][:], in_=pgd[:, 0:512], func=AF.Sigmoid), [rpgd], [r_sg[0]])
            fw.op("act", lambda e: e.activation(out=sg[1][:], in_=pgs[:, 0:512], func=AF.Sigmoid), [rpgs], [r_sg[1]])
            fw.op("dve", lambda e: e.tensor_tensor(out=m1[0][:], in0=pyd[:, 0:512], in1=sg[0][:], op=ALU.mult),
                  [rpyd, r_sg[0]], [r_m1[0]])
            fw.op("dve", lambda e: e.tensor_tensor(out=m1[1][:], in0=pys[:, 0:512], in1=sg[1][:], op=ALU.mult),
                  [rpys, r_sg[1]], [r_m1[1]])
            fw.op("pool", lambda e: e.tensor_tensor(out=mT[:, c, :], in0=m1[0][:], in1=m1[1][:], op=ALU.add),
                  [r_m1[0], r_m1[1]], [r_mT[c]])
        for j in range(4):
            i = 4 * n + j
            s = i % 2
            fw.dma(xt[s][:], x_d[i * 128:(i + 1) * 128, :], writes=[r_xt[s]])
            for half in range(2):
                hs = slice(half * 512, (half + 1) * 512)
                po, rpo = psget()
                for k in range(8):
                    mm(po[:, 0:512], mT[:, k, j * 128:(j + 1) * 128], wo[:, k, hs], k == 0, k == 7,
                       [r_mT[k]] + wres(wpo, half * 512, (half + 1) * 512), [rpo])
                fw.op("dve", lambda e: e.tensor_tensor(out=ot[s][:, hs], in0=po[:, 0:512], in1=xt[s][:, hs],
                                                       op=ALU.add), [rpo, r_xt[s]], [r_out[s]])
            fw.dma(out_d[i * 128:(i + 1) * 128, :], ot[s][:], reads=[r_out[s]], semres=r_out[s])
    fw.final_wait(r_out)


_CACHE = {}


def kernel(**inputs):
    inp = {k: np.asarray(v) for k, v in inputs.items()}
    shared = _prep_shared(inp)
    if "nc" not in _CACHE:
        _CACHE["nc"] = build()
    nc = _CACHE["nc"]
    x = np.ascontiguousarray(inp["x"], dtype=np.float32)
    in_maps = []
    for b in range(8):
        m = dict(shared)
        m["x"] = x[b]
        in_maps.append(m)
    res = run_bass_kernel_spmd(nc, in_maps, core_ids=list(range(8)))
    out = np.stack([np.asarray(res.results[b]["out"]) for b in range(8)], axis=0)
    return out.astype(np.float32)
```

```python
import numpy as np
from contextlib import ExitStack
import concourse.bass as bass
import concourse.mybir as mybir
from concourse.bass_utils import run_bass_kernel_spmd

F32 = mybir.dt.float32
BF16 = mybir.dt.bfloat16
AF = mybir.ActivationFunctionType
ALU = mybir.AluOpType
AX = mybir.AxisListType

T = 2048
D = 1024
NT = 16
EPS = 1e-6
BIG = 30000.0
QSCALE = 128.0 ** -0.5


class Res:
    __slots__ = ("key", "w", "r", "dsem", "dcnt", "excl")

    def __init__(self, excl=False):
        self.excl = excl
        self.key = None
        self.w = None
        self.r = {}
        self.dsem = None
        self.dcnt = 0


class FW:
    def __init__(self, nc, es):
        self.nc = nc
        self.es = es
        self.eng = {"pe": nc.tensor, "act": nc.scalar, "dve": nc.vector, "pool": nc.gpsimd, "sp": nc.sync}
        self.sem = {}
        self.cnt = {}
        for k in self.eng:
            self.sem[k] = es.enter_context(nc.semaphore("s_" + k))
            self.cnt[k] = 0
        self.waited = {k: {} for k in self.eng}
        self.semobj = dict(self.sem)
        self.ndsem = 0
        self.ninst = 0

    def _wait(self, e, deps):
        best = {}
        for (k, v) in deps:
            if k == e and e == "pe":
                continue
            if best.get(k, 0) < v:
                best[k] = v
        for k, v in best.items():
            if self.waited[e].get(k, 0) >= v:
                continue
            self.eng[e].wait_ge(self.semobj[k], v)
            self.waited[e][k] = v

    @staticmethod
    def _deps(reads, writes, e=None):
        deps = []
        for r in reads:
            if r.w is not None:
                deps.append(r.w)
            if r.excl:
                deps.extend((k, v) for k, v in r.r.items() if k != e)
        for w in writes:
            if w.w is not None:
                deps.append(w.w)
            deps.extend(w.r.items())
        return deps

    def op(self, e, fn, reads=(), writes=()):
        self._wait(e, self._deps(reads, writes, e))
        inst = fn(self.eng[e])
        self.cnt[e] += 1
        c = self.cnt[e]
        inst.then_inc(self.sem[e], 1)
        for r in reads:
            if r.r.get(e, 0) < c:
                r.r[e] = c
        for w in writes:
            w.w = (e, c)
            w.r = {}
        self.ninst += 1
        return inst

    def dma(self, out, in_, reads=(), writes=(), q="sp", semres=None):
        self._wait(q, self._deps(reads, writes, q))
        sr = semres or (writes[0] if writes else reads[0])
        if sr.dsem is None:
            sr.key = "d_%d" % self.ndsem
            sr.dsem = self.es.enter_context(self.nc.semaphore(sr.key))
            self.ndsem += 1
            self.semobj[sr.key] = sr.dsem
        key = sr.key
        inst = self.eng[q].dma_start(out=out, in_=in_)
        sr.dcnt += 16
        inst.then_inc(sr.dsem, 16)
        for r in reads:
            if r.r.get(key, 0) < sr.dcnt:
                r.r[key] = sr.dcnt
        for w in writes:
            w.w = (key, sr.dcnt)
            w.r = {}
        self.ninst += 1
        return inst

    def barrier_all(self):
        for e in self.eng:
            deps = [(k, self.cnt[k]) for k in self.eng if self.cnt[k] > 0]
            best = {}
            for k, v in deps:
                best[k] = v
            for k, v in best.items():
                if self.waited[e].get(k, 0) >= v:
                    continue
                self.eng[e].wait_ge(self.semobj[k], v)
                self.waited[e][k] = v

    def final_wait(self, reslist, e="sp"):
        deps = []
        for r in reslist:
            if r.w is not None:
                deps.append(r.w)
            deps.extend(r.r.items())
        self._wait(e, deps)


class PSP:
    def __init__(self, tiles, res):
        self.t, self.r = tiles, res
        self.live = [False] * len(tiles)
        self.i = 0

    def get(self):
        n = len(self.t)
        for _ in range(n):
            i = self.i % n
            self.i += 1
            if not self.live[i]:
                self.live[i] = True
                return self.t[i], self.r[i]
        raise RuntimeError("no free PSUM bank")

    def try_get(self):
        n = len(self.t)
        for _ in range(n):
            i = self.i % n
            self.i += 1
            if not self.live[i]:
                self.live[i] = True
                return self.t[i], self.r[i]
        return None

    def rel(self, res):
        self.live[self.r.index(res)] = False


def streams_gen(items, W, make, stagger=0):
    it = iter(items)
    active = []
    free = list(range(W))
    pending = True
    rnd = 0
    next_start = 0
    while True:
        while free and pending and rnd >= next_start:
            try:
                a = next(it)
            except StopIteration:
                pending = False
                break
            s = free.pop(0)
            active.append([make(a, s), s])
            if stagger:
                next_start = rnd + stagger
                break
        if not active and not pending:
            break
        for ent in list(active):
            try:
                next(ent[0])
            except StopIteration:
                active.remove(ent)
                free.append(ent[1])
        rnd += 1
        yield


def run_streams(items, W, make, stagger=0):
    for _ in streams_gen(items, W, make, stagger):
        pass


CF = {}
_o = 0
for _n, _w in [("ident", 128), ("tri", 128), ("tris", 128), ("ch0", 128), ("ch1", 128), ("nw", 8), ("cw", 48),
               ("dtb", 4), ("alog", 4), ("dnw", 512), ("qw", 3), ("kw", 3), ("eps", 1)]:
    CF[_n] = (_o, _w)
    _o += _w
NCF = _o
CB = {}
_o = 0
for _n, _w in [("ident", 128), ("ones", 128), ("x1", 128), ("masku", 512), ("maskl", 512), ("m2", 2048), ("rm", 32)]:
    CB[_n] = (_o, _w)
    _o += _w
NCB = _o


def _const_arrays():
    p = np.arange(128)
    same = (p[:, None] // 64) == (p[None, :] // 64)
    cf = np.zeros((128, NCF), np.float32)

    def put(name, arr):
        o, w = CF[name]
        cf[:, o:o + w] = arr
    put("ident", np.eye(128, dtype=np.float32))
    put("tri", (same & (p[:, None] <= p[None, :])).astype(np.float32))
    put("tris", (same & (p[:, None] > p[None, :])).astype(np.float32))
    put("ch0", np.broadcast_to((p[:, None] < 64), (128, 128)).astype(np.float32))
    put("ch1", np.broadcast_to((p[:, None] >= 64), (128, 128)).astype(np.float32))
    put("eps", np.full((128, 1), EPS, np.float32))
    cb = np.zeros((128, NCB), np.float32)

    def putb(name, arr):
        o, w = CB[name]
        cb[:, o:o + w] = arr
    putb("ident", np.eye(128, dtype=np.float32))
    putb("ones", np.ones((128, 128), np.float32))
    putb("x1", np.where(same & (p[:, None] > p[None, :]), 0.0, BIG).astype(np.float32))
    mu = (p[:, None] <= p[None, :]).astype(np.float32)
    ml = (p[:, None] >= p[None, :]).astype(np.float32)
    putb("masku", np.tile(mu, (1, 4)))
    putb("maskl", np.tile(ml, (1, 4)))
    m2 = np.zeros((128, 4, 16, 32), np.float32)
    for m in range(4):
        q = 32 * m + np.arange(32)
        m2[:, m, :, :] = (p[:, None] <= q[None, :]).astype(np.float32)[:, None, :]
    putb("m2", m2.reshape(128, 2048))
    rm = np.zeros((128, 32), np.float32)
    for m in range(16):
        rm[16 + m, m] = -1.0
    for m in range(16, 32):
        rm[m - 16, m] = 1.0
    putb("rm", rm)
    pos = np.arange(T, dtype=np.float32)
    inv_freq = (np.float32(500000.0) ** (-np.arange(0, 32, 2, dtype=np.float32) / np.float32(32))).astype(np.float32)
    ang = (pos[:, None] * inv_freq[None, :]).astype(np.float32)
    cos = np.cos(ang).astype(np.float32).T
    sin = np.sin(ang).astype(np.float32).T
    rope = np.zeros((32, 2, T), np.float32)
    rope[0:16, 0] = cos
    rope[16:32, 0] = cos
    rope[0:16, 1] = sin
    rope[16:32, 1] = sin
    return cf, cb, rope


def _prep_shared(inp):
    cf, cb, rope = _const_arrays()
    w_in = inp["w_in"][0]

    def put(name, arr):
        o, w = CF[name]
        cf[:, o:o + w] = arr
    put("nw", inp["norm_w"][0].reshape(8, 128).T)
    cw = inp["conv_w"][0, :, 0, :]
    put("cw", cw.T.reshape(12, 128, 4).transpose(1, 0, 2).reshape(128, 48))
    put("dtb", np.broadcast_to(inp["dn_dt_bias"][0][None, :], (128, 4)))
    put("alog", np.broadcast_to(inp["dn_a_log"][0][None, :], (128, 4)))
    put("dnw", np.broadcast_to(np.tile(inp["dn_norm_w"][0], 4)[None, :], (128, 512)))
    put("qw", inp["q_norm_w"][0].T)
    put("kw", inp["k_norm_w"][0].T)
    w_dn = np.ascontiguousarray(w_in[:, 0:2056])
    base = 2056
    w_swa = np.empty((4, 1024, 1280), np.float32)
    for h in range(4):
        cols = []
        for typ in range(3):
            for g in range(3):
                c0 = base + typ * 1536 + g * 512 + h * 128
                cols.append(w_in[:, c0:c0 + 128])
        c0 = base + 3 * 1536 + h * 128
        cols.append(w_in[:, c0:c0 + 128])
        w_swa[h] = np.concatenate(cols, axis=1)
    w_g = np.ascontiguousarray(w_in[:, base + 3 * 1536 + 512:])
    assert w_g.shape[1] == 2048
    return {"cf": cf, "cb": cb, "rope": rope, "w_dn": w_dn, "w_swa": w_swa, "w_g": w_g,
            "w_bd": np.ascontiguousarray(inp["w_branch_dn"][0]), "w_bs": np.ascontiguousarray(inp["w_branch_swa"][0]),
            "w_o": np.ascontiguousarray(inp["w_out"][0])}


def build(stage=99, debug=False):
    nc = bass.Bass("TRN2", target_bir_lowering=False)
    x_d = nc.dram_tensor("x", [T, D], F32, kind="ExternalInput").ap()
    cf_d = nc.dram_tensor("cf", [128, NCF], F32, kind="ExternalInput").ap()
    cb_d = nc.dram_tensor("cb", [128, NCB], F32, kind="ExternalInput").ap()
    rope_d = nc.dram_tensor("rope", [32, 2, T], F32, kind="ExternalInput").ap()
    wdn_d = nc.dram_tensor("w_dn", [D, 2056], F32, kind="ExternalInput").ap()
    wswa_d = nc.dram_tensor("w_swa", [4, D, 1280], F32, kind="ExternalInput").ap()
    wg_d = nc.dram_tensor("w_g", [D, 2048], F32, kind="ExternalInput").ap()
    wbd_d = nc.dram_tensor("w_bd", [512, D], F32, kind="ExternalInput").ap()
    wbs_d = nc.dram_tensor("w_bs", [512, D], F32, kind="ExternalInput").ap()
    wo_d = nc.dram_tensor("w_o", [D, D], F32, kind="ExternalInput").ap()
    out_d = nc.dram_tensor("out", [T, D], F32, kind="ExternalOutput").ap()
    dbg_d = None
    if debug:
        dbg_d = nc.dram_tensor("dbg", [128, 8, T], BF16, kind="ExternalOutput").ap()

    with ExitStack() as es:
        fw = FW(nc, es)

        def sb(name, shape, dt, ctx=es):
            return ctx.enter_context(nc.sbuf_tensor("sb_" + name, shape, dt))

        hT = sb("hT", [128, 8, T], BF16)
        class _OT:
            d = None
            s_ = None

            def __getitem__(self, key):
                p, hsel, cols = key
                if isinstance(hsel, int):
                    return self.d[p, hsel, cols] if hsel < 4 else self.s_[p, hsel - 4, cols]
                assert hsel == slice(0, 4)
                return self.d[p, 0:4, cols]
        oT = _OT()
        oT.d = sb("oTd", [128, 4, T], BF16)
        cf = sb("cf", [128, NCF], F32)
        cb = sb("cb", [128, NCB], BF16)
        r_hT = [Res() for _ in range(NT)]
        r_oT = [[Res() for _ in range(4)] for _ in range(8)]
        r_cf, r_cb = Res(), Res()
        pst = [es.enter_context(nc.psum_tensor("ps%d" % i, [128, 512], F32)) for i in range(8)]
        r_ps = [Res(excl=True) for _ in range(8)]
        rot = {"list": list(range(8)), "i": 0}

        def psget():
            lst = rot["list"]
            i = lst[rot["i"] % len(lst)]
            rot["i"] += 1
            return pst[i], r_ps[i]

        def cfs(name, a=0, b=None):
            o, w = CF[name]
            if b is None:
                b = w
            return cf[:, o + a:o + b]

        def cbs(name, a=0, b=None):
            o, w = CB[name]
            if b is None:
                b = w
            return cb[:, o + a:o + b]

        def mm(out, lhsT, rhs, start, stop, reads, writes):
            return fw.op("pe", lambda e: e.matmul(out, lhsT, rhs, start=start, stop=stop, skip_group_check=True),
                         reads, writes)

        def load_w(dst, src, K, C):
            srcv = src.rearrange("(k p) c -> p k c", p=128)
            pieces = []
            for c0 in range(0, C, 512):
                c1 = min(C, c0 + 512)
                r = Res()
                fw.dma(dst[:, 0:K, c0:c1], srcv[:, :, c0:c1], writes=[r], q="pool")
                pieces.append((c0, c1, r))
            return pieces

        def wres(pieces, a, b):
            return [r for (c0, c1, r) in pieces if c0 < b and c1 > a]

        fw.dma(cf[:], cf_d[:, :], writes=[r_cf])
        with ExitStack() as ph:
            cbst = sb("cbst", [128, NCB], F32, ph)
            r_cbst = Res()
            fw.dma(cbst[:], cb_d[:, :], writes=[r_cbst])
            fw.op("dve", lambda e: e.tensor_copy(out=cb[:], in_=cbst[:]), [r_cbst], [r_cb])
            fw.barrier_all()
        ident_f = cfs("ident")
        ident_b = cbs("ident")
        ones_b = cbs("ones")
        eps_c = cfs("eps")

        ph_dn = es.enter_context(ExitStack())
        wdn_pre = sb("wdn", [128, 8, 2056], BF16, ph_dn)
        wdv_ = wdn_d.rearrange("(k p) c -> p k c", p=128)
        wp_pre = []
        for c0 in range(0, 2056, 512):
            c1 = min(2056, c0 + 512)
            r_ = Res()
            fw.dma(wdn_pre[:, :, c0:c1], wdv_[:, :, c0:c1], writes=[r_], q="pool")
            wp_pre.append((c0, c1, r_))

        with ExitStack() as ph:
            xt = [sb("xt%d" % i, [128, D], F32, ph) for i in range(6)]
            xs = [sb("xs%d" % i, [128, D], BF16, ph) for i in range(6)]
            junk = sb("junk", [128, D], F32, ph)
            ss = sb("ss", [128, NT], F32, ph)
            rt = sb("rt", [128, NT], F32, ph)
            rstd = sb("rstd", [128, NT], F32, ph)
            r_xt = [Res() for _ in range(6)]
            r_xs = [Res() for _ in range(6)]
            r_junk = Res()
            r_st = [Res() for _ in range(NT)]
            def p0_stream(i, s):
                fw.dma(xt[s][:], x_d[i * 128:(i + 1) * 128, :], writes=[r_xt[s]])
                yield
                fw.op("act", lambda e: e.activation(out=junk[:], in_=xt[s][:], func=AF.Square,
                                                    accum_out=ss[:, i:i + 1]), [r_xt[s]], [r_junk, r_st[i]])
                yield
                fw.op("act", lambda e: e.activation(out=rt[:, i:i + 1], in_=ss[:, i:i + 1], func=AF.Sqrt,
                                                    bias=eps_c, scale=1.0 / D), [r_st[i], r_cf], [r_st[i]])
                yield
                fw.op("dve", lambda e: e.reciprocal(out=rstd[:, i:i + 1], in_=rt[:, i:i + 1]), [r_st[i]], [r_st[i]])
                yield
                fw.op("dve", lambda e: e.tensor_scalar(out=xs[s][:], in0=xt[s][:], scalar1=rstd[:, i:i + 1],
                                                       scalar2=None, op0=ALU.mult), [r_xt[s], r_st[i]], [r_xs[s]])
                yield
                pt, rp = psget()
                pb = pt[:].bitcast(BF16)
                for k in range(8):
                    fw.op("pe", lambda e: e.transpose(out=pb[:, k * 128:(k + 1) * 128],
                                                      in_=xs[s][:, k * 128:(k + 1) * 128], identity=ident_b),
                          [r_xs[s], r_cb], [rp])
                yield
                for k in range(8):
                    dst = hT[:, k, i * 128:(i + 1) * 128]
                    src = pb[:, k * 128:(k + 1) * 128]
                    if k % 2 == 0:
                        fw.op("act", lambda e: e.activation(out=dst, in_=src, func=AF.Copy,
                                                            scale=cfs("nw", k, k + 1)), [rp, r_cf], [r_hT[i]])
                    else:
                        fw.op("dve", lambda e: e.tensor_scalar(out=dst, in0=src, scalar1=cfs("nw", k, k + 1),
                                                               scalar2=None, op0=ALU.mult), [rp, r_cf], [r_hT[i]])
                    if k % 2 == 1:
                        yield

            run_streams(list(range(NT)), 6, p0_stream, stagger=2)
            fw.barrier_all()

        def hres(n):
            return r_hT[4 * n:4 * n + 4]

        if stage >= 1:
            phase_dn(nc, fw, ph_dn, sb, psget, mm, load_w, wres, cfs, cbs, hT, oT, r_hT, r_oT, r_cf, r_cb, hres,
                     wdn_d, pst, r_ps, wdn_pre, wp_pre)
            fw.barrier_all()
        ph_dn.close()
        oT.s_ = sb("oTs", [128, 4, T], BF16)

        if stage >= 2:
            with ExitStack() as ph:
                phase_swa(nc, fw, ph, sb, psget, mm, load_w, wres, cfs, cbs, hT, oT, r_hT, r_oT, r_cf, r_cb, hres,
                          wswa_d, rope_d, rot, pst, r_ps)
                fw.barrier_all()
        r_out = [Res(), Res()]
        if stage >= 3:
            with ExitStack() as ph:
                phase_out(nc, fw, ph, sb, psget, mm, load_w, wres, cfs, cbs, hT, oT, r_hT, r_oT, r_cf, r_cb, hres,
                          wg_d, wbd_d, wbs_d, wo_d, x_d, out_d, r_out)
        else:
            with ExitStack() as ph:
                zt = sb("zt", [128, D], F32, ph)
                fw.op("dve", lambda e: e.memset(zt[:], 0.0), [], [r_out[0]])
                for i in range(NT):
                    fw.dma(out_d[i * 128:(i + 1) * 128, :], zt[:], reads=[r_out[0]], semres=r_out[0])
                fw.final_wait(r_out)
        if debug:
            r_dbg = Res()
            allr = [r for hh in r_oT for r in hh]
            for hh in range(8):
                fw.dma(dbg_d[:, hh, :], (hT if stage == 0 else oT)[:, hh, :], reads=allr + r_hT, semres=r_dbg)
            fw.final_wait([r_dbg] + allr)
        fw.final_wait(r_out)
        fw.barrier_all()
    return nc


def phase_dn(nc, fw, ph, sb, psget, mm, load_w, wres, cfs, cbs, hT, oT, r_hT, r_oT, r_cf, r_cb, hres, wdn_d,
             pst, r_ps, wdn, wp):
    ident_f = cfs("ident")
    ident_b = cbs("ident")
    ones_b = cbs("ones")
    eps_c = cfs("eps")
    psp = PSP(pst, r_ps)

    def pget():
        n = 0
        while True:
            p = psp.try_get()
            if p is not None:
                return p
            n += 1
            if n > 10000:
                raise RuntimeError("PSUM starvation")
            yield


    ba = sb("ba", [128, NT, 8], F32, ph)
    beta = sb("beta", [128, NT, 4], F32, ph)
    nbeta = sb("nbeta", [128, NT, 4], F32, ph)
    g = sb("g", [128, NT, 4], F32, ph)
    t0 = sb("a0t0", [128, NT, 4], F32, ph)
    t1 = sb("a0t1", [128, NT, 4], F32, ph)
    t2 = sb("a0t2", [128, NT, 4], F32, ph)
    negA = sb("negA", [128, 4], F32, ph)
    gc = sb("gc", [128, NT, 4], F32, ph)
    egc = sb("egc", [128, NT, 4], F32, ph)
    sckbg = sb("sckbg", [128, NT, 4], F32, ph)
    sckd = sb("sckd", [128, NT, 4], F32, ph)
    egl = sb("egl", [128, 2, NT, 4], F32, ph)
    r_a0 = Res()
    pba, rpba = psp.get()
    for i in range(NT):
        for k in range(8):
            mm(pba[:, i * 8:(i + 1) * 8], hT[:, k, i * 128:(i + 1) * 128], wdn[:, k, 2048:2056], k == 0, k == 7,
               [r_hT[i]] + wres(wp, 2048, 2056), [rpba])
    bav = ba[:].rearrange("p t c -> p (t c)")
    fw.op("act", lambda e: e.activation(out=bav, in_=pba[:, 0:128], func=AF.Copy), [rpba], [r_a0])
    psp.rel(rpba)
    fw.op("act", lambda e: e.activation(out=beta[:], in_=ba[:, :, 0:4], func=AF.Sigmoid), [r_a0], [r_a0])
    fw.op("dve", lambda e: e.tensor_scalar(out=nbeta[:], in0=beta[:], scalar1=-1.0, scalar2=None, op0=ALU.mult),
          [r_a0], [r_a0])
    dtb_b = cfs("dtb").unsqueeze(1).to_broadcast([128, NT, 4])
    fw.op("dve", lambda e: e.tensor_tensor(out=t0[:], in0=ba[:, :, 4:8], in1=dtb_b, op=ALU.add), [r_a0, r_cf], [r_a0])
    fw.op("dve", lambda e: e.tensor_scalar(out=t2[:], in0=t0[:], scalar1=-1.0, scalar2=None, op0=ALU.mult),
          [r_a0], [r_a0])
    fw.op("dve", lambda e: e.tensor_tensor(out=t1[:], in0=t0[:], in1=t2[:], op=ALU.max), [r_a0], [r_a0])
    fw.op("act", lambda e: e.activation(out=t1[:], in_=t1[:], func=AF.Exp, scale=-1.0), [r_a0], [r_a0])
    fw.op("act", lambda e: e.activation(out=t1[:], in_=t1[:], func=AF.Ln, bias=1.0), [r_a0], [r_a0])
    fw.op("dve", lambda e: e.tensor_scalar(out=t2[:], in0=t0[:], scalar1=0.0, scalar2=None, op0=ALU.max),
          [r_a0], [r_a0])
    fw.op("dve", lambda e: e.tensor_tensor(out=t2[:], in0=t2[:], in1=t1[:], op=ALU.add), [r_a0], [r_a0])
    fw.op("act", lambda e: e.activation(out=negA[:], in_=cfs("alog"), func=AF.Exp), [r_cf], [r_a0])
    fw.op("dve", lambda e: e.tensor_scalar(out=negA[:], in0=negA[:], scalar1=-1.0, scalar2=None, op0=ALU.mult),
          [r_a0], [r_a0])
    negA_b = negA[:].unsqueeze(1).to_broadcast([128, NT, 4])
    fw.op("dve", lambda e: e.tensor_tensor(out=g[:], in0=t2[:], in1=negA_b, op=ALU.mult), [r_a0], [r_a0])
    gv = g[:].rearrange("p t c -> p (t c)")
    pg1, rpg1 = psp.get()
    mm(pg1[:, 0:64], cfs("tri"), gv, True, True, [r_a0, r_cf], [rpg1])
    mm(pg1[:, 64:128], cfs("tris"), gv, True, True, [r_a0, r_cf], [rpg1])
    mm(pg1[:, 128:192], cfs("ch0"), gv, True, True, [r_a0, r_cf], [rpg1])
    mm(pg1[:, 192:256], cfs("ch1"), gv, True, True, [r_a0, r_cf], [rpg1])
    fw.op("act", lambda e: e.activation(out=gc[:].rearrange("p t c -> p (t c)"), in_=pg1[:, 0:64], func=AF.Copy),
          [rpg1], [r_a0])
    fw.op("act", lambda e: e.activation(out=egc[:].rearrange("p t c -> p (t c)"), in_=pg1[:, 0:64], func=AF.Exp),
          [rpg1], [r_a0])
    fw.op("act", lambda e: e.activation(out=sckd[:].rearrange("p t c -> p (t c)"), in_=pg1[:, 64:128], func=AF.Exp),
          [rpg1], [r_a0])
    fw.op("act", lambda e: e.activation(out=egl[:].rearrange("p a t c -> p (a t c)"), in_=pg1[:, 128:256],
                                        func=AF.Exp), [rpg1], [r_a0])
    psp.rel(rpg1)
    fw.op("dve", lambda e: e.tensor_tensor(out=sckbg[:], in0=beta[:], in1=egc[:], op=ALU.mult), [r_a0], [r_a0])

    NS = 2
    halo = sb("halo", [128, 12, 3], F32, ph)
    r_halo = [Res() for _ in range(12)]
    raw = [sb("raw%d" % i, [128, 515], F32, ph) for i in range(NS)]
    cv = [sb("cv%d" % i, [128, 512], F32, ph) for i in range(NS)]
    cs = [sb("cs%d" % i, [128, 512], F32, ph) for i in range(NS)]
    sq = [sb("sq%d" % i, [128, 512], BF16, ph) for i in range(NS)]
    rn = [sb("rn%d" % i, [128, 512], F32, ph) for i in range(NS)]
    r_raw = [Res() for _ in range(NS)]
    r_cv = [Res() for _ in range(NS)]
    r_cs = [Res() for _ in range(NS)]
    r_sq = [Res() for _ in range(NS)]
    r_rn = [Res() for _ in range(NS)]
    qT = [sb("qT%d" % i, [128, 4, 512], BF16, ph) for i in range(2)]
    kT = [sb("kT%d" % i, [128, 4, 512], BF16, ph) for i in range(2)]
    kbg = [sb("kbg%d" % i, [128, 4, 4, 128], BF16, ph) for i in range(2)]
    kd = [sb("kd%d" % i, [128, 4, 4, 128], BF16, ph) for i in range(2)]
    vb = [sb("vb%d" % i, [128, 4, 4, 128], BF16, ph) for i in range(2)]
    r_qT = [[Res() for _ in range(4)] for _ in range(2)]
    r_kT = [[Res() for _ in range(4)] for _ in range(2)]
    r_kbg = [[Res() for _ in range(4)] for _ in range(2)]
    r_kd = [[Res() for _ in range(4)] for _ in range(2)]
    r_vb = [[Res() for _ in range(4)] for _ in range(2)]
    NP = 2
    arena = [sb("prearena%d" % i, [128, 2816], F32, ph) for i in range(NP)]
    e1m = [arena[i][:, 0:512] for i in range(NP)]
    tg = [arena[i][:, 512:1024] for i in range(NP)]
    abf = [arena[i][:, 1024:2816].bitcast(BF16) for i in range(NP)]
    Mb = [[abf[i][:, 0:512], abf[i][:, 512:1024]] for i in range(NP)]
    MTb = [[abf[i][:, 1024:1536], abf[i][:, 1536:2048]] for i in range(NP)]
    IMb = [abf[i][:, 2048:2560] for i in range(NP)]
    PTb = [[abf[i][:, 2560:3072], abf[i][:, 3072:3584]] for i in range(NP)]
    for i in range(NP):
        raw.append(arena[i][:, 0:515])
        cv.append(arena[i][:, 516:1028])
        cs.append(arena[i][:, 1028:1540])
        rn.append(arena[i][:, 1540:2052])
        sq.append(arena[i][:, 2052:2308].bitcast(BF16))
        for lst in (r_raw, r_cv, r_cs, r_sq, r_rn):
            lst.append(Res())
    HS = 4
    AT = [sb("AT%d" % i, [128, 512], BF16, ph) for i in range(HS)]
    u32 = [sb("u32%d" % i, [128, 512], F32, ph) for i in range(HS)]
    wT = [sb("wT%d" % i, [128, 512], BF16, ph) for i in range(HS)]
    r_e1m = [Res() for _ in range(NP)]
    r_tg = [Res() for _ in range(NP)]
    r_M = [[Res(), Res()] for _ in range(NP)]
    r_MT = [[Res(), Res()] for _ in range(NP)]
    r_IM = [Res() for _ in range(NP)]
    r_PT = [[Res(), Res()] for _ in range(NP)]
    r_AT = [Res() for _ in range(HS)]
    r_u32 = [Res() for _ in range(HS)]
    r_wT = [Res() for _ in range(HS)]
    vn = sb("vn", [128, 512], BF16, ph)
    tq = sb("tq", [128, 512], F32, ph)
    o32b = [sb("o32%d" % i, [128, 512], F32, ph) for i in range(2)]
    r_o32b = [Res(), Res()]
    osq = sb("osq", [128, 512], F32, ph)
    oss = sb("oss", [128, 4], F32, ph)
    on = sb("on", [128, 512], BF16, ph)
    zs = sb("zs", [128, 512], F32, ph)
    zsw = zs
    S32 = sb("S32", [128, 512], F32, ph)
    Sb = sb("Sb", [128, 512], BF16, ph)
    r_vn, r_tq, r_osq, r_oss, r_on, r_zs, r_zsw, r_S32, r_Sb = [Res() for _ in range(9)]
    fw.op("dve", lambda e: e.memset(S32[:], 0.0), [], [r_S32])
    fw.op("dve", lambda e: e.memset(Sb[:], 0.0), [], [r_Sb])

    def v4(ap):
        return ap.rearrange("p (h d) -> p h d", h=4)

    idf_b = ident_f.unsqueeze(1).to_broadcast([128, 4, 128])
    idb_b = ident_b.unsqueeze(1).to_broadcast([128, 4, 128])

    def sec_stream(arg, s):
        n, h, typ = arg
        par = n % 2
        tok = slice(n * 512, (n + 1) * 512)
        sec = typ * 4 + h
        col0 = typ * 512 + h * 128
        pr, rpr = yield from pget()
        for k in range(8):
            mm(pr[:, 0:512], wdn[:, k, col0:col0 + 128], hT[:, k, tok], k == 0, k == 7,
               hres(n) + wres(wp, col0, col0 + 128), [rpr])
        if n == 0:
            fw.op("pool", lambda e: e.memset(raw[s][:, 0:3], 0.0), [], [r_raw[s]])
        else:
            fw.op("pool", lambda e: e.tensor_copy(out=raw[s][:, 0:3], in_=halo[:, sec, :]),
                  [r_halo[sec]], [r_raw[s]])
        yield
        fw.op("act", lambda e: e.activation(out=raw[s][:, 3:515], in_=pr[:, 0:512], func=AF.Copy),
              [rpr], [r_raw[s]])
        psp.rel(rpr)
        yield
        fw.op("pool", lambda e: e.tensor_copy(out=halo[:, sec, :], in_=raw[s][:, 512:515]),
              [r_raw[s]], [r_halo[sec]])
        cwo = sec * 4
        fw.op("act", lambda e: e.activation(out=cv[s][:], in_=raw[s][:, 0:512], func=AF.Copy,
                                            scale=cfs("cw", cwo, cwo + 1)), [r_raw[s], r_cf], [r_cv[s]])
        yield
        for j in range(1, 4):
            fw.op("dve", lambda e: e.scalar_tensor_tensor(out=cv[s][:], in0=raw[s][:, j:j + 512],
                                                          scalar=cfs("cw", cwo + j, cwo + j + 1),
                                                          in1=cv[s][:], op0=ALU.mult, op1=ALU.add),
                  [r_raw[s], r_cf, r_cv[s]], [r_cv[s]])
            yield
        fw.op("act", lambda e: e.activation(out=cs[s][:], in_=cv[s][:], func=AF.Silu), [r_cv[s]], [r_cs[s]])
        yield
        if typ < 2:
            fw.op("act", lambda e: e.activation(out=sq[s][:], in_=cs[s][:], func=AF.Square), [r_cs[s]], [r_sq[s]])
            yield
            pss, rpss = yield from pget()
            mm(pss[:, 0:512], ones_b, sq[s][:], True, True, [r_sq[s], r_cb], [rpss])
            yield
            fw.op("act", lambda e: e.activation(out=rn[s][:], in_=pss[:, 0:512], func=AF.Ln, bias=eps_c,
                                                scale=1.0), [rpss, r_cf], [r_rn[s]])
            psp.rel(rpss)
            yield
            fw.op("act", lambda e: e.activation(out=rn[s][:], in_=rn[s][:], func=AF.Exp, scale=-0.5),
                  [r_rn[s]], [r_rn[s]])
            yield
        if typ == 0:
            fw.op("dve", lambda e: e.scalar_tensor_tensor(out=qT[par][:, h, :], in0=cs[s][:], scalar=QSCALE,
                                                          in1=rn[s][:], op0=ALU.mult, op1=ALU.mult),
                  [r_cs[s], r_rn[s]], [r_qT[par][h]])
            return
        if typ == 1:
            fw.op("dve", lambda e: e.tensor_tensor(out=cv[s][:], in0=cs[s][:], in1=rn[s][:], op=ALU.mult),
                  [r_cs[s], r_rn[s]], [r_cv[s]])
            yield
            fw.op("act", lambda e: e.activation(out=kT[par][:, h, :], in_=cv[s][:], func=AF.Copy),
                  [r_cv[s]], [r_kT[par][h]])
            src, rsrc = cv[s], r_cv[s]
        else:
            src, rsrc = cs[s], r_cs[s]
        ptr, rptr = yield from pget()
        for j in range(4):
            fw.op("pe", lambda e: e.transpose(out=ptr[:, j * 128:(j + 1) * 128],
                                              in_=src[:, j * 128:(j + 1) * 128], identity=ident_f),
                  [rsrc, r_cf], [rptr])
        yield
        pv = ptr[:, 0:512].rearrange("p (j d) -> p j d", j=4)
        if typ == 1:
            sc1 = sckbg[:, 4 * n:4 * n + 4, h:h + 1].to_broadcast([128, 4, 128])
            sc2 = sckd[:, 4 * n:4 * n + 4, h:h + 1].to_broadcast([128, 4, 128])
            fw.op("dve", lambda e: e.tensor_tensor(out=kbg[par][:, :, h, :], in0=pv, in1=sc1, op=ALU.mult),
                  [rptr, r_a0], [r_kbg[par][h]])
            yield
            fw.op("dve", lambda e: e.tensor_tensor(out=kd[par][:, :, h, :], in0=pv, in1=sc2, op=ALU.mult),
                  [rptr, r_a0], [r_kd[par][h]])
        else:
            sc3 = beta[:, 4 * n:4 * n + 4, h:h + 1].to_broadcast([128, 4, 128])
            fw.op("dve", lambda e: e.tensor_tensor(out=vb[par][:, :, h, :], in0=pv, in1=sc3, op=ALU.mult),
                  [rptr, r_a0], [r_vb[par][h]])
        psp.rel(rptr)

    def pre_stream(i, s):
        n, j = i // 4, i % 4
        par = n % 2
        hs_ = i % HS
        tk = slice(j * 128, (j + 1) * 128)
        pB, rpB = yield from pget()
        for h in range(4):
            gb = g[:, i, h:h + 1].to_broadcast([128, 128])
            mm(pB[:, h * 128:(h + 1) * 128], gb, cfs("tri"), True, False, [r_a0, r_cf], [rpB])
            mm(pB[:, h * 128:(h + 1) * 128], ident_b, cbs("x1"), False, True, [r_cb], [rpB])
        yield
        for h in range(4):
            fw.op("act", lambda e: e.activation(out=e1m[s][:, h * 128:(h + 1) * 128],
                                                in_=pB[:, h * 128:(h + 1) * 128], func=AF.Exp,
                                                bias=gc[:, i, h:h + 1], scale=-1.0), [rpB, r_a0], [r_e1m[s]])
        psp.rel(rpB)
        pG, rpG = yield from pget()
        for h in range(4):
            mm(pG[:, h * 128:(h + 1) * 128], kT[par][:, h, tk], kT[par][:, h, tk], True, True,
               [r_kT[par][h]], [rpG])
        yield
        nb_b = nbeta[:, i, :].unsqueeze(2).to_broadcast([128, 4, 128])
        fw.op("dve", lambda e: e.tensor_tensor(out=v4(tg[s][:]), in0=v4(pG[:, 0:512]), in1=nb_b, op=ALU.mult),
              [rpG, r_a0], [r_tg[s]])
        psp.rel(rpG)
        yield
        fw.op("dve", lambda e: e.tensor_tensor(out=Mb[s][0][:], in0=tg[s][:], in1=e1m[s][:], op=ALU.mult),
              [r_tg[s], r_e1m[s]], [r_M[s][0]])
        pE, rpE = yield from pget()
        for h in range(4):
            fw.op("pe", lambda e: e.transpose(out=pE[:, h * 128:(h + 1) * 128],
                                              in_=e1m[s][:, h * 128:(h + 1) * 128], identity=ident_f),
                  [r_e1m[s], r_cf], [rpE])
        yield
        pT, rpT = yield from pget()
        pTb = pT[:].bitcast(BF16)
        for h in range(4):
            fw.op("pe", lambda e: e.transpose(out=pTb[:, h * 128:(h + 1) * 128],
                                              in_=Mb[s][0][:, h * 128:(h + 1) * 128], identity=ident_b),
                  [r_M[s][0], r_cb], [rpT])
        fw.op("dve", lambda e: e.tensor_tensor(out=v4(tg[s][:]), in0=v4(pE[:, 0:512]), in1=idf_b, op=ALU.add),
              [rpE, r_cf, r_tg[s]], [r_tg[s]])
        psp.rel(rpE)
        yield
        fw.op("act", lambda e: e.activation(out=MTb[s][0][:], in_=pTb[:, 0:512], func=AF.Copy), [rpT], [r_MT[s][0]])
        yield
        fw.op("dve", lambda e: e.tensor_tensor(out=v4(PTb[s][0][:]), in0=v4(pTb[:, 0:512]), in1=idb_b, op=ALU.add),
              [rpT, r_cb], [r_PT[s][0]])
        psp.rel(rpT)
        pQ, rpQ = yield from pget()
        for h in range(4):
            mm(pQ[:, h * 128:(h + 1) * 128], kT[par][:, h, tk], qT[par][:, h, tk], True, True,
               [r_kT[par][h], r_qT[par][h]], [rpQ])
        yield
        fw.op("dve", lambda e: e.tensor_tensor(out=AT[hs_][:], in0=pQ[:, 0:512], in1=tg[s][:], op=ALU.mult),
              [rpQ, r_tg[s]], [r_AT[hs_]])
        psp.rel(rpQ)
        cur = 0
        for lev in range(1, 6):
            nxt = 1 - cur
            pM, rpM = yield from pget()
            for h in range(4):
                hs = slice(h * 128, (h + 1) * 128)
                mm(pM[:, hs], MTb[s][cur][:, hs], Mb[s][cur][:, hs], True, True, [r_MT[s][cur], r_M[s][cur]], [rpM])
            if lev < 5:
                pMT, rpMT = yield from pget()
                for h in range(4):
                    hs = slice(h * 128, (h + 1) * 128)
                    mm(pMT[:, hs], Mb[s][cur][:, hs], MTb[s][cur][:, hs], True, True,
                       [r_MT[s][cur], r_M[s][cur]], [rpMT])
            yield
            fw.op("dve", lambda e: e.tensor_tensor(out=v4(IMb[s][:]), in0=v4(pM[:, 0:512]), in1=idf_b, op=ALU.add),
                  [rpM, r_cf], [r_IM[s]])
            if lev < 5:
                fw.op("act", lambda e: e.activation(out=MTb[s][nxt][:], in_=pMT[:, 0:512], func=AF.Copy),
                      [rpMT], [r_MT[s][nxt]])
                psp.rel(rpMT)
            yield
            if lev < 5:
                fw.op("act", lambda e: e.activation(out=Mb[s][nxt][:], in_=pM[:, 0:512], func=AF.Copy),
                      [rpM], [r_M[s][nxt]])
            psp.rel(rpM)
            pP, rpP = yield from pget()
            for h in range(4):
                hs = slice(h * 128, (h + 1) * 128)
                mm(pP[:, hs], IMb[s][:, hs], PTb[s][cur][:, hs], True, True, [r_IM[s], r_PT[s][cur]], [rpP])
            yield
            fw.op("act" if lev % 2 else "dve",
                  (lambda e: e.activation(out=PTb[s][nxt][:], in_=pP[:, 0:512], func=AF.Copy)) if lev % 2 else
                  (lambda e: e.tensor_copy(out=PTb[s][nxt][:], in_=pP[:, 0:512])), [rpP], [r_PT[s][nxt]])
            psp.rel(rpP)
            cur = nxt
            yield
        PTf, r_PTf = PTb[s][cur], r_PT[s][cur]
        pU, rpU = yield from pget()
        for h in range(4):
            hs = slice(h * 128, (h + 1) * 128)
            mm(pU[:, hs], PTf[:, hs], vb[par][:, j, h, :], True, True, [r_PTf, r_vb[par][h]], [rpU])
        pW, rpW = yield from pget()
        for h in range(4):
            hs = slice(h * 128, (h + 1) * 128)
            mm(pW[:, hs], kbg[par][:, j, h, :], PTf[:, hs], True, True, [r_PTf, r_kbg[par][h]], [rpW])
        yield
        fw.op("act", lambda e: e.activation(out=u32[hs_][:], in_=pU[:, 0:512], func=AF.Copy), [rpU], [r_u32[hs_]])
        psp.rel(rpU)
        fw.op("dve", lambda e: e.tensor_copy(out=wT[hs_][:], in_=pW[:, 0:512]), [rpW], [r_wT[hs_]])
        psp.rel(rpW)

    def chain_stream(i):
        n, j = i // 4, i % 4
        par = n % 2
        s = i % HS
        o32 = o32b[i % 2]
        r_o32 = r_o32b[i % 2]
        tk = slice(j * 128, (j + 1) * 128)
        for c in range(2):
            R = slice(c * 64, (c + 1) * 64)
            pSw, rpSw = yield from pget()
            for h in range(4):
                hs = slice(h * 128, (h + 1) * 128)
                mm(pSw[:, hs], wT[s][:, hs], Sb[:, hs], True, True, [r_wT[s], r_Sb], [rpSw])
            pSq, rpSq = yield from pget()
            for h in range(4):
                hs = slice(h * 128, (h + 1) * 128)
                mm(pSq[:, hs], qT[par][:, h, tk], Sb[:, hs], True, True, [r_qT[par][h], r_Sb], [rpSq])
            yield
            egl_b = egl[:, c, i, :].unsqueeze(2).to_broadcast([128, 4, 128])
            fw.op("dve", lambda e: e.tensor_tensor(out=v4(S32[:]), in0=v4(S32[:]), in1=egl_b, op=ALU.mult),
                  [r_S32, r_a0], [r_S32])
            fw.op("dve", lambda e: e.tensor_tensor(out=vn[R, :], in0=u32[s][R, :], in1=pSw[R, 0:512],
                                                   op=ALU.subtract), [r_u32[s], rpSw], [r_vn])
            psp.rel(rpSw)
            yield
            pSs, rpSs = yield from pget()
            for h in range(4):
                hs = slice(h * 128, (h + 1) * 128)
                mm(pSs[:, hs], kd[par][R, j, h, :], vn[R, hs], True, True, [r_kd[par][h], r_vn], [rpSs])
            pSa, rpSa = yield from pget()
            for h in range(4):
                hs = slice(h * 128, (h + 1) * 128)
                mm(pSa[:, hs], AT[s][R, hs], vn[R, hs], True, True, [r_AT[s], r_vn], [rpSa])
            egc_b = egc[R, i, :].unsqueeze(2).to_broadcast([64, 4, 128])
            fw.op("dve", lambda e: e.tensor_tensor(out=v4(tq[R, :]), in0=v4(pSq[R, 0:512]), in1=egc_b,
                                                   op=ALU.mult), [rpSq, r_a0], [r_tq])
            psp.rel(rpSq)
            yield
            fw.op("dve", lambda e: e.tensor_tensor(out=S32[:], in0=S32[:], in1=pSs[:, 0:512], op=ALU.add),
                  [r_S32, rpSs], [r_S32])
            psp.rel(rpSs)
            yield
            fw.op("act", lambda e: e.activation(out=Sb[:], in_=S32[:], func=AF.Copy), [r_S32], [r_Sb])
            fw.op("dve", lambda e: e.tensor_tensor(out=o32[R, :], in0=tq[R, :], in1=pSa[R, 0:512], op=ALU.add),
                  [r_tq, rpSa], [r_o32])
            psp.rel(rpSa)
            yield

    def epi_stream(i):
        n, j = i // 4, i % 4
        o32 = o32b[i % 2]
        r_o32 = r_o32b[i % 2]
        pz, rpz = yield from pget()
        for k in range(8):
            mm(pz[:, 0:512], hT[:, k, i * 128:(i + 1) * 128], wdn[:, k, 1536:2048], k == 0, k == 7,
               [r_hT[i]] + wres(wp, 1536, 2048), [rpz])
        yield
        fw.op("act", lambda e: e.activation(out=zs[:], in_=pz[:, 0:512], func=AF.Silu), [rpz], [r_zs])
        psp.rel(rpz)
        fw.op("act", lambda e: e.activation(out=osq[:], in_=o32[:], func=AF.Square), [r_o32], [r_osq])
        yield
        fw.op("dve", lambda e: e.tensor_tensor(out=zsw[:], in0=zs[:], in1=cfs("dnw"), op=ALU.mult),
              [r_zs, r_cf], [r_zsw])
        fw.op("dve", lambda e: e.tensor_reduce(out=oss[:], in_=v4(osq[:]), axis=AX.X, op=ALU.add),
              [r_osq], [r_oss])
        yield
        fw.op("act", lambda e: e.activation(out=oss[:], in_=oss[:], func=AF.Ln, bias=eps_c, scale=1.0 / 128),
              [r_oss, r_cf], [r_oss])
        fw.op("act", lambda e: e.activation(out=oss[:], in_=oss[:], func=AF.Exp, scale=-0.5), [r_oss], [r_oss])
        yield
        oss_b = oss[:].unsqueeze(2).to_broadcast([128, 4, 128])
        fw.op("dve", lambda e: e.tensor_tensor(out=v4(osq[:]), in0=v4(o32[:]), in1=oss_b, op=ALU.mult),
              [r_o32, r_oss, r_osq], [r_osq])
        fw.op("dve", lambda e: e.tensor_tensor(out=on[:], in0=osq[:], in1=zsw[:], op=ALU.mult),
              [r_osq, r_zsw], [r_on])
        yield
        pO, rpO = yield from pget()
        pOb = pO[:].bitcast(BF16)
        for h in range(4):
            fw.op("pe", lambda e: e.transpose(out=pOb[:, h * 128:(h + 1) * 128],
                                              in_=on[:, h * 128:(h + 1) * 128], identity=ident_b),
                  [r_on, r_cb], [rpO])
        yield
        fw.op("act", lambda e: e.activation(out=oT[:, 0:4, i * 128:(i + 1) * 128], in_=v4(pOb[:, 0:512]),
                                            func=AF.Copy), [rpO], [r_oT[h][n] for h in range(4)])
        psp.rel(rpO)

    def sec_items(n):
        return [(n, h, typ) for h in range(4) for typ in range(2)] + [(n, h, 2) for h in range(4)]

    def chain(gens):
        for g_ in gens:
            yield from g_

    def tiles_gen():
        for i in range(NT):
            yield ("need_sections", i // 4)
            yield from pre_stream(i, i % NP)
            yield ("pre_done", i)

    import os
    nbanks = int(os.environ.get('DN_BANKS', '4'))
    ntl = 4 * nbanks
    sec_done = [False] * 5
    pre_done = [False] * (NT + 1)
    chain_done = [False] * (NT + 1)
    epi_done = [False] * (NT + 1)

    def sections_meta():
        for n in range(nbanks):
            while n >= 2 and not chain_done[4 * (n - 2) + 3]:
                yield
            if n == 0:
                yield from streams_gen(sec_items(n), NS + NP, sec_stream)
                fw.barrier_all()
            else:
                yield from streams_gen(sec_items(n), NS, sec_stream)
            sec_done[n] = True

    def pre_one(i, s):
        while not sec_done[i // 4]:
            yield
        while i >= HS and not chain_done[i - HS]:
            yield
        yield from pre_stream(i, s)
        pre_done[i] = True

    def pre_meta():
        yield from streams_gen(list(range(ntl)), NP, pre_one, stagger=12)

    def chain_meta():
        for i in range(ntl):
            while not pre_done[i]:
                yield
            while i >= 2 and not epi_done[i - 2]:
                yield
            yield from chain_stream(i)
            chain_done[i] = True

    def epi_meta():
        for i in range(ntl):
            while not chain_done[i]:
                yield
            yield from epi_stream(i)
            epi_done[i] = True

    metas = [sections_meta(), pre_meta(), chain_meta(), epi_meta()]
    stall = 0
    last = fw.ninst
    while metas:
        for m_ in list(metas):
            try:
                next(m_)
            except StopIteration:
                metas.remove(m_)
        if fw.ninst == last:
            stall += 1
            if stall > 100000:
                raise RuntimeError("stream scheduler livelock")
        else:
            stall = 0
            last = fw.ninst


def phase_swa(nc, fw, ph, sb, psget, mm, load_w, wres, cfs, cbs, hT, oT, r_hT, r_oT, r_cf, r_cb, hres,
              wswa_d, rope_d, rot, pst, r_ps):
    ident_b = cbs("ident")
    ones_b = cbs("ones")
    eps_c = cfs("eps")
    psp = PSP(pst, r_ps)

    def pget():
        n = 0
        while True:
            p = psp.try_get()
            if p is not None:
                return p
            n += 1
            if n > 100000:
                raise RuntimeError("PSUM starvation")
            yield

    rope = sb("rope", [32, 2, T], F32, ph)
    r_rope = Res()
    fw.dma(rope[:], rope_d[:, :, :], writes=[r_rope])
    wsw = sb("wsw", [128, 8, 1280], BF16, ph)
    wr = {"qk": Res(), "v": Res(), "z": Res()}
    wrange = {"qk": (0, 768), "v": (768, 1152), "z": (1152, 1280)}

    def load_head(h, part):
        c0, c1 = wrange[part]
        srcv = wswa_d[h].rearrange("(k p) c -> p k c", p=128)
        fw.dma(wsw[:, :, c0:c1], srcv[:, :, c0:c1], writes=[wr[part]], q="pool")

    qk = [[sb("qk%d%d" % (t_, g_), [128, T], BF16, ph) for g_ in range(3)] for t_ in range(2)]
    r_qk = [[[Res() for _ in range(4)] for _ in range(3)] for _ in range(2)]
    V = [sb("V%d" % g_, [128, 16, 128], BF16, ph) for g_ in range(3)]
    r_V = [[Res() for _ in range(16)] for _ in range(3)]
    NQ = 6
    sq = [sb("ssq%d" % i, [128, 512], BF16, ph) for i in range(NQ)]
    rn = [sb("srn%d" % i, [128, 512], F32, ph) for i in range(NQ)]
    qn = [sb("sqn%d" % i, [128, 512], F32, ph) for i in range(NQ)]
    qnb = [sb("sqnb%d" % i, [32, 512], BF16, ph) for i in range(NQ)]
    r_sq = [Res() for _ in range(NQ)]
    r_rn = [Res() for _ in range(NQ)]
    r_qn = [Res() for _ in range(NQ)]
    r_qnb = [Res() for _ in range(NQ)]
    NA = 3
    zsall = sb("zsall", [128, T], F32, ph)
    r_zsall = [Res() for _ in range(4)]
    ee = [[sb("see%d%d" % (s, i), [128, 512], BF16, ph) for i in range(2)] for s in range(NA)]
    PP = ee
    r_ee = [[Res(), Res()] for _ in range(NA)]
    r_PP = r_ee
    rden = [sb("srden%d" % s, [128, 512], F32, ph) for s in range(NA)]
    tn = rden
    r_rden = [Res() for _ in range(NA)]
    r_tn = r_rden

    for part in ("qk", "v", "z"):
        load_head(0, part)

    for h in range(4):
        def qk_stream(arg, s):
            g_, typ, n = arg
            c0 = typ * 384 + g_ * 128
            dst = qk[typ][g_]
            wname = "qw" if typ == 0 else "kw"
            tok = slice(n * 512, (n + 1) * 512)
            pr, rpr = yield from pget()
            for k in range(8):
                mm(pr[:, 0:512], wsw[:, k, c0:c0 + 128], hT[:, k, tok], k == 0, k == 7, hres(n) + [wr["qk"]], [rpr])
            yield
            fw.op("act", lambda e: e.activation(out=sq[s][:], in_=pr[:, 0:512], func=AF.Square), [rpr], [r_sq[s]])
            yield
            pss, rpss = yield from pget()
            mm(pss[:, 0:512], ones_b, sq[s][:], True, True, [r_sq[s], r_cb], [rpss])
            yield
            fw.op("act", lambda e: e.activation(out=rn[s][:], in_=pss[:, 0:512], func=AF.Ln, bias=eps_c,
                                                scale=1.0 / 128), [rpss, r_cf], [r_rn[s]])
            psp.rel(rpss)
            yield
            fw.op("act", lambda e: e.activation(out=rn[s][:], in_=rn[s][:], func=AF.Exp, scale=-0.5),
                  [r_rn[s]], [r_rn[s]])
            yield
            fw.op("dve", lambda e: e.scalar_tensor_tensor(out=qn[s][:], in0=pr[:, 0:512],
                                                          scalar=cfs(wname, g_, g_ + 1), in1=rn[s][:],
                                                          op0=ALU.mult, op1=ALU.mult),
                  [rpr, r_rn[s], r_cf], [r_qn[s]])
            psp.rel(rpr)
            yield
            fw.op("dve", lambda e: e.tensor_tensor(out=qnb[s][:], in0=qn[s][0:32, :], in1=rope[:, 1, tok],
                                                   op=ALU.mult), [r_qn[s], r_rope], [r_qnb[s]])
            fw.op("act", lambda e: e.activation(out=dst[:, tok], in_=qn[s][:], func=AF.Copy),
                  [r_qn[s]], [r_qk[typ][g_][n]])
            yield
            prt, rprt = yield from pget()
            mm(prt[0:32, 0:512], cbs("rm")[0:32, :], qnb[s][:], True, True, [r_qnb[s], r_cb], [rprt])
            fw.op("dve", lambda e: e.tensor_tensor(out=rn[s][0:32, :], in0=qn[s][0:32, :], in1=rope[:, 0, tok],
                                                   op=ALU.mult), [r_qn[s], r_rope, r_rn[s]], [r_rn[s]])
            yield
            fw.op("dve", lambda e: e.tensor_tensor(out=dst[0:32, tok], in0=rn[s][0:32, :], in1=prt[0:32, 0:512],
                                                   op=ALU.add), [r_rn[s], rprt], [r_qk[typ][g_][n]])
            psp.rel(rprt)

        def v_stream(arg, s):
            g_, b4 = arg
            c0 = 768 + g_ * 128
            pv, rpv = yield from pget()
            for bb in range(4):
                blk = b4 * 4 + bb
                if g_ == 0:
                    tsl = slice(blk * 128, (blk + 1) * 128)
                    hr = [r_hT[blk]]
                elif g_ == 1:
                    n_, r_ = blk // 4, blk % 4
                    tsl = slice(512 * n_ + r_, 512 * (n_ + 1), 4)
                    hr = hres(n_)
                else:
                    tsl = slice(blk, T, 16)
                    hr = r_hT
                for k in range(8):
                    mm(pv[:, bb * 128:(bb + 1) * 128], hT[:, k, tsl], wsw[:, k, c0:c0 + 128], k == 0, k == 7,
                       list(hr) + [wr["v"]], [rpv])
                yield
            vout = V[g_][:, b4 * 4:(b4 + 1) * 4, :]
            vin = pv[:, 0:512].rearrange("p (b d) -> p b d", b=4)
            vres = [r_V[g_][b4 * 4 + bb] for bb in range(4)]
            if (g_ * 4 + b4) % 2 == 0:
                fw.op("act", lambda e: e.activation(out=vout, in_=vin, func=AF.Copy), [rpv], vres)
            else:
                fw.op("dve", lambda e: e.tensor_copy(out=vout, in_=vin), [rpv], vres)
            psp.rel(rpv)

        def mixed_stream(arg, s):
            if arg[0] == "v":
                return v_stream(arg[1:], s)
            return qk_stream(arg[1:], s)

        qk_items = [("qk", g_, typ, n) for g_ in range(3) for typ in range(2) for n in range(4)]
        v_items = [("v", g_, b4) for g_ in range(3) for b4 in range(4)]
        items = []
        for ii in range(12):
            items += qk_items[2 * ii:2 * ii + 2] + [v_items[ii]]
        run_streams(items, NQ, mixed_stream, stagger=2)
        if h < 3:
            load_head(h + 1, "qk")
        if h < 3:
            load_head(h + 1, "v")

        def attn_stream(m, s):
            num, r_num = yield from pget()
            den, r_den = yield from pget()
            first = [True, True]

            def st(idx):
                f = first[idx]
                first[idx] = False
                return f

            def geom(g_):
                def cols(n_, r_):
                    if g_ == 0:
                        tt = 4 * n_ + r_
                        return slice(tt * 128, (tt + 1) * 128)
                    return slice(512 * n_ + r_, 512 * (n_ + 1), 4)

                def outsl(r_):
                    if g_ == 0:
                        return slice(r_ * 128, (r_ + 1) * 128)
                    return slice(r_, 512, 4)

                def prevblk(n_, r_):
                    if g_ == 0:
                        tt = 4 * n_ + r_ - 1
                        if tt < 0:
                            return None
                        return (tt // 4, tt % 4)
                    if n_ == 0:
                        return None
                    return (n_ - 1, r_)
                return cols, outsl, prevblk

            groups = []
            for g_ in range(2):
                groups.append((g_, "cur"))
                _, _, prevblk = geom(g_)
                if any(prevblk(m, r_) is not None for r_ in range(4)):
                    groups.append((g_, "prev"))
            groups.append((2, "cur"))
            pend = None
            gi = 0

            def emit_pv(p):
                g_, kind, buf, rbuf, lo = p
                if g_ == 2:
                    for r_ in range(16):
                        mm(num[:, slice(r_, 512, 16)], V[2][:, r_, :], buf[:, r_ * 32:(r_ + 1) * 32], st(0), False,
                           [r_V[2][r_], rbuf], [r_num])
                        mm(den[:, slice(r_, 512, 16)], ones_b, buf[:, r_ * 32:(r_ + 1) * 32], st(1), False,
                           [r_cb, rbuf], [r_den])
                    return
                cols, outsl, prevblk = geom(g_)
                for r_ in range(4):
                    if kind == "cur":
                        vb_ = 4 * m + r_
                    else:
                        pb = prevblk(m, r_)
                        if pb is None:
                            continue
                        vb_ = 4 * pb[0] + pb[1]
                    mm(num[:, outsl(r_)], V[g_][:, vb_, :], buf[:, r_ * 128:(r_ + 1) * 128], st(0), False,
                       [r_V[g_][vb_], rbuf], [r_num])
                    if g_ != 0:
                        mm(den[:, outsl(r_)], ones_b, buf[:, r_ * 128:(r_ + 1) * 128], st(1), False,
                           [r_cb, rbuf], [r_den])
                if g_ == 0:
                    mm(den[:, lo:512], ones_b, buf[:, lo:512], st(1), False, [r_cb, rbuf], [r_den])

            def emit_scores(pc, rpc, g_, kind):
                lo = 0
                if g_ == 2:
                    qTt, kTt = qk[0][2], qk[1][2]
                    for r_ in range(16):
                        mm(pc[:, r_ * 32:(r_ + 1) * 32], kTt[:, slice(r_, T, 16)],
                           qTt[:, slice(512 * m + r_, 512 * (m + 1), 16)], True, True,
                           [r_qk[0][2][m]] + r_qk[1][2], [rpc])
                    mask = cbs("m2", m * 512, (m + 1) * 512)
                else:
                    qTt, kTt = qk[0][g_], qk[1][g_]
                    cols, outsl, prevblk = geom(g_)
                    if kind == "cur":
                        for r_ in range(4):
                            mm(pc[:, r_ * 128:(r_ + 1) * 128], kTt[:, cols(m, r_)], qTt[:, cols(m, r_)], True, True,
                               [r_qk[0][g_][m], r_qk[1][g_][m]], [rpc])
                        mask = cbs("masku")
                    else:
                        prs = [(r_, prevblk(m, r_)) for r_ in range(4) if prevblk(m, r_) is not None]
                        lo = prs[0][0] * 128
                        for r_, (pn, pr_) in prs:
                            mm(pc[:, r_ * 128:(r_ + 1) * 128], kTt[:, cols(pn, pr_)], qTt[:, cols(m, r_)], True, True,
                               [r_qk[0][g_][m], r_qk[1][g_][pn], r_qk[1][g_][m]], [rpc])
                        mask = cbs("maskl")
                return pc, rpc, lo, mask

            pc0, rpc0 = yield from pget()
            nxt = emit_scores(pc0, rpc0, *groups[0])
            yield
            for gi, (g_, kind) in enumerate(groups):
                b = gi % 2
                pc, rpc, lo, mask = nxt
                fw.op("act", lambda e: e.activation(out=ee[s][b][:, lo:512], in_=pc[:, lo:512], func=AF.Exp,
                                                    scale=QSCALE), [rpc], [r_ee[s][b]])
                psp.rel(rpc)
                yield
                meng = "dve"
                fw.op(meng, lambda e: e.tensor_tensor(out=PP[s][b][:, lo:512], in0=ee[s][b][:, lo:512],
                                                      in1=mask[:, lo:512], op=ALU.mult),
                      [r_ee[s][b], r_cb], [r_PP[s][b]])
                if gi + 1 < len(groups):
                    pcn, rpcn = yield from pget()
                    nxt = emit_scores(pcn, rpcn, *groups[gi + 1])
                yield
                emit_pv((g_, kind, PP[s][b], r_PP[s][b], lo))
                yield
            tok = slice(m * 512, (m + 1) * 512)
            fw.op("act", lambda e: e.activation(out=rden[s][:], in_=den[:, 0:512], func=AF.Ln), [r_den], [r_rden[s]])
            psp.rel(r_den)
            yield
            fw.op("act", lambda e: e.activation(out=rden[s][:], in_=rden[s][:], func=AF.Exp, scale=-1.0),
                  [r_rden[s]], [r_rden[s]])
            yield
            fw.op("dve", lambda e: e.tensor_tensor(out=tn[s][:], in0=num[:, 0:512], in1=rden[s][:], op=ALU.mult),
                  [r_num, r_rden[s]], [r_tn[s]])
            psp.rel(r_num)
            yield
            fw.op("dve", lambda e: e.tensor_tensor(out=oT[:, 4 + h, tok], in0=tn[s][:], in1=zsall[:, tok], op=ALU.mult),
                  [r_tn[s], r_zsall[m]], [r_oT[4 + h][m]])

        for m in range(4):
            tok = slice(m * 512, (m + 1) * 512)
            pz, rpz = psp.get()
            for k in range(8):
                mm(pz[:, 0:512], wsw[:, k, 1152:1280], hT[:, k, tok], k == 0, k == 7, hres(m) + [wr["z"]], [rpz])
            fw.op("act", lambda e: e.activation(out=zsall[:, tok], in_=pz[:, 0:512], func=AF.Silu), [rpz], [r_zsall[m]])
            psp.rel(rpz)
        run_streams(list(range(4)), NA, attn_stream, stagger=5)
        if h < 3:
            load_head(h + 1, "z")


def phase_out(nc, fw, ph, sb, psget, mm, load_w, wres, cfs, cbs, hT, oT, r_hT, r_oT, r_cf, r_cb, hres,
              wg_d, wbd_d, wbs_d, wo_d, x_d, out_d, r_out):
    wbd = sb("wbd", [128, 4, 1024], BF16, ph)
    wbs = sb("wbs", [128, 4, 1024], BF16, ph)
    wpbd = load_w(wbd, wbd_d, 4, 1024)
    wpbs = load_w(wbs, wbs_d, 4, 1024)
    wg = sb("wg", [128, 8, 2048], BF16, ph)
    wo = sb("wo", [128, 8, 1024], BF16, ph)
    wgv = wg_d.rearrange("(k p) c -> p k c", p=128)
    wpg = []
    for c in range(0, 8, 2):
        for base in (0, 1024):
            c0, c1 = base + c * 128, base + (c + 2) * 128
            r_ = Res()
            fw.dma(wg[:, :, c0:c1], wgv[:, :, c0:c1], writes=[r_], q="pool")
            wpg.append((c0, c1, r_))
    wpo = load_w(wo, wo_d, 8, 1024)
    mT = sb("mT", [128, 8, 512], BF16, ph)
    r_mT = [Res() for _ in range(8)]
    sg = [sb("sg%d" % i, [128, 512], F32, ph) for i in range(2)]
    r_sg = [Res(), Res()]
    m1 = [sb("m1%d" % i, [128, 512], F32, ph) for i in range(2)]
    r_m1 = [Res(), Res()]
    xt = [sb("oxt%d" % i, [128, D], F32, ph) for i in range(2)]
    r_xt = [Res(), Res()]
    ot = [sb("oot%d" % i, [128, D], F32, ph) for i in range(2)]
    for n in range(4):
        tok = slice(n * 512, (n + 1) * 512)
        for c in range(8):
            cs_ = slice(c * 128, (c + 1) * 128)
            pyd, rpyd = psget()
            for hh in range(4):
                mm(pyd[:, 0:512], wbd[:, hh, cs_], oT[:, hh, tok], hh == 0, hh == 3,
                   [r_oT[hh][n]] + wres(wpbd, c * 128, (c + 1) * 128), [rpyd])
            pys, rpys = psget()
            for hh in range(4):
                mm(pys[:, 0:512], wbs[:, hh, cs_], oT[:, 4 + hh, tok], hh == 0, hh == 3,
                   [r_oT[4 + hh][n]] + wres(wpbs, c * 128, (c + 1) * 128), [rpys])
            pgd, rpgd = psget()
            for k in range(8):
                mm(pgd[:, 0:512], wg[:, k, cs_], hT[:, k, tok], k == 0, k == 7,
                   hres(n) + wres(wpg, c * 128, (c + 1) * 128), [rpgd])
            pgs, rpgs = psget()
            for k in range(8):
                mm(pgs[:, 0:512], wg[:, k, 1024 + c * 128:1024 + (c + 1) * 128], hT[:, k, tok], k == 0, k == 7,
                   hres(n) + wres(wpg, 1024 + c * 128, 1024 + (c + 1) * 128), [rpgs])
            fw.op("act", lambda e: e.activation(out=sg[0][:], in_=pgd[:, 0:512], func=AF.Sigmoid), [rpgd], [r_sg[0]])
            fw.op("act", lambda e: e.activation(out=sg[1][:], in_=pgs[:, 0:512], func=AF.Sigmoid), [rpgs], [r_sg[1]])
            fw.op("dve", lambda e: e.tensor_tensor(out=m1[0][:], in0=pyd[:, 0:512], in1=sg[0][:], op=ALU.mult),
                  [rpyd, r_sg[0]], [r_m1[0]])
            fw.op("dve", lambda e: e.tensor_tensor(out=m1[1][:], in0=pys[:, 0:512], in1=sg[1][:], op=ALU.mult),
                  [rpys, r_sg[1]], [r_m1[1]])
            fw.op("pool", lambda e: e.tensor_tensor(out=mT[:, c, :], in0=m1[0][:], in1=m1[1][:], op=ALU.add),
                  [r_m1[0], r_m1[1]], [r_mT[c]])
        for j in range(4):
            i = 4 * n + j
            s = i % 2
            fw.dma(xt[s][:], x_d[i * 128:(i + 1) * 128, :], writes=[r_xt[s]])
            for half in range(2):
                hs = slice(half * 512, (half + 1) * 512)
                po, rpo = psget()
                for k in range(8):
                    mm(po[:, 0:512], mT[:, k, j * 128:(j + 1) * 128], wo[:, k, hs], k == 0, k == 7,
                       [r_mT[k]] + wres(wpo, half * 512, (half + 1) * 512), [rpo])
                fw.op("dve", lambda e: e.tensor_tensor(out=ot[s][:, hs], in0=po[:, 0:512], in1=xt[s][:, hs],
                                                       op=ALU.add), [rpo, r_xt[s]], [r_out[s]])
            fw.dma(out_d[i * 128:(i + 1) * 128, :], ot[s][:], reads=[r_out[s]], semres=r_out[s])
    fw.final_wait(r_out)


_CACHE = {}


def kernel(**inputs):
    inp = {k: np.asarray(v) for k, v in inputs.items()}
    shared = _prep_shared(inp)
    if "nc" not in _CACHE:
        _CACHE["nc"] = build()
    nc = _CACHE["nc"]
    x = np.ascontiguousarray(inp["x"], dtype=np.float32)
    in_maps = []
    for b in range(8):
        m = dict(shared)
        m["x"] = x[b]
        in_maps.append(m)
    res = run_bass_kernel_spmd(nc, in_maps, core_ids=list(range(8)))
    out = np.stack([np.asarray(res.results[b]["out"]) for b in range(8)], axis=0)
    return out.astype(np.float32)
```

```python
import numpy as np
from contextlib import ExitStack
import concourse.bass as bass
import concourse.mybir as mybir
from concourse.bass_utils import run_bass_kernel_spmd

F32 = mybir.dt.float32
BF16 = mybir.dt.bfloat16
AF = mybir.ActivationFunctionType
ALU = mybir.AluOpType
AX = mybir.AxisListType

T = 2048
D = 1024
NT = 16
EPS = 1e-6
BIG = 30000.0
QSCALE = 128.0 ** -0.5


class Res:
    __slots__ = ("key", "w", "r", "dsem", "dcnt", "excl")

    def __init__(self, excl=False):
        self.excl = excl
        self.key = None
        self.w = None
        self.r = {}
        self.dsem = None
        self.dcnt = 0


class FW:
    def __init__(self, nc, es):
        self.nc = nc
        self.es = es
        self.eng = {"pe": nc.tensor, "act": nc.scalar, "dve": nc.vector, "pool": nc.gpsimd, "sp": nc.sync}
        self.sem = {}
        self.cnt = {}
        for k in self.eng:
            self.sem[k] = es.enter_context(nc.semaphore("s_" + k))
            self.cnt[k] = 0
        self.waited = {k: {} for k in self.eng}
        self.semobj = dict(self.sem)
        self.ndsem = 0
        self.ninst = 0

    def _wait(self, e, deps):
        best = {}
        for (k, v) in deps:
            if k == e and e == "pe":
                continue
            if best.get(k, 0) < v:
                best[k] = v
        for k, v in best.items():
            if self.waited[e].get(k, 0) >= v:
                continue
            self.eng[e].wait_ge(self.semobj[k], v)
            self.waited[e][k] = v

    @staticmethod
    def _deps(reads, writes, e=None):
        deps = []
        for r in reads:
            if r.w is not None:
                deps.append(r.w)
            if r.excl:
                deps.extend((k, v) for k, v in r.r.items() if k != e)
        for w in writes:
            if w.w is not None:
                deps.append(w.w)
            deps.extend(w.r.items())
        return deps

    def op(self, e, fn, reads=(), writes=()):
        self._wait(e, self._deps(reads, writes, e))
        inst = fn(self.eng[e])
        self.cnt[e] += 1
        c = self.cnt[e]
        inst.then_inc(self.sem[e], 1)
        for r in reads:
            if r.r.get(e, 0) < c:
                r.r[e] = c
        for w in writes:
            w.w = (e, c)
            w.r = {}
        self.ninst += 1
        return inst

    def dma(self, out, in_, reads=(), writes=(), q="sp", semres=None):
        self._wait(q, self._deps(reads, writes, q))
        sr = semres or (writes[0] if writes else reads[0])
        if sr.dsem is None:
            sr.key = "d_%d" % self.ndsem
            sr.dsem = self.es.enter_context(self.nc.semaphore(sr.key))
            self.ndsem += 1
            self.semobj[sr.key] = sr.dsem
        key = sr.key
        inst = self.eng[q].dma_start(out=out, in_=in_)
        sr.dcnt += 16
        inst.then_inc(sr.dsem, 16)
        for r in reads:
            if r.r.get(key, 0) < sr.dcnt:
                r.r[key] = sr.dcnt
        for w in writes:
            w.w = (key, sr.dcnt)
            w.r = {}
        self.ninst += 1
        return inst

    def barrier_all(self):
        for e in self.eng:
            deps = [(k, self.cnt[k]) for k in self.eng if self.cnt[k] > 0]
            best = {}
            for k, v in deps:
                best[k] = v
            for k, v in best.items():
                if self.waited[e].get(k, 0) >= v:
                    continue
                self.eng[e].wait_ge(self.semobj[k], v)
                self.waited[e][k] = v

    def final_wait(self, reslist, e="sp"):
        deps = []
        for r in reslist:
            if r.w is not None:
                deps.append(r.w)
            deps.extend(r.r.items())
        self._wait(e, deps)


class PSP:
    def __init__(self, tiles, res):
        self.t, self.r = tiles, res
        self.live = [False] * len(tiles)
        self.i = 0

    def get(self):
        n = len(self.t)
        for _ in range(n):
            i = self.i % n
            self.i += 1
            if not self.live[i]:
                self.live[i] = True
                return self.t[i], self.r[i]
        raise RuntimeError("no free PSUM bank")

    def try_get(self):
        n = len(self.t)
        for _ in range(n):
            i = self.i % n
            self.i += 1
            if not self.live[i]:
                self.live[i] = True
                return self.t[i], self.r[i]
        return None

    def rel(self, res):
        self.live[self.r.index(res)] = False


def streams_gen(items, W, make, stagger=0):
    it = iter(items)
    active = []
    free = list(range(W))
    pending = True
    rnd = 0
    next_start = 0
    while True:
        while free and pending and rnd >= next_start:
            try:
                a = next(it)
            except StopIteration:
                pending = False
                break
            s = free.pop(0)
            active.append([make(a, s), s])
            if stagger:
                next_start = rnd + stagger
                break
        if not active and not pending:
            break
        for ent in list(active):
            try:
                next(ent[0])
            except StopIteration:
                active.remove(ent)
                free.append(ent[1])
        rnd += 1
        yield


def run_streams(items, W, make, stagger=0):
    for _ in streams_gen(items, W, make, stagger):
        pass


CF = {}
_o = 0
for _n, _w in [("ident", 128), ("tri", 128), ("tris", 128), ("ch0", 128), ("ch1", 128), ("nw", 8), ("cw", 48),
               ("dtb", 4), ("alog", 4), ("dnw", 512), ("qw", 3), ("kw", 3), ("eps", 1)]:
    CF[_n] = (_o, _w)
    _o += _w
NCF = _o
CB = {}
_o = 0
for _n, _w in [("ident", 128), ("ones", 128), ("x1", 128), ("masku", 512), ("maskl", 512), ("m2", 2048), ("rm", 32)]:
    CB[_n] = (_o, _w)
    _o += _w
NCB = _o


def _const_arrays():
    p = np.arange(128)
    same = (p[:, None] // 64) == (p[None, :] // 64)
    cf = np.zeros((128, NCF), np.float32)

    def put(name, arr):
        o, w = CF[name]
        cf[:, o:o + w] = arr
    put("ident", np.eye(128, dtype=np.float32))
    put("tri", (same & (p[:, None] <= p[None, :])).astype(np.float32))
    put("tris", (same & (p[:, None] > p[None, :])).astype(np.float32))
    put("ch0", np.broadcast_to((p[:, None] < 64), (128, 128)).astype(np.float32))
    put("ch1", np.broadcast_to((p[:, None] >= 64), (128, 128)).astype(np.float32))
    put("eps", np.full((128, 1), EPS, np.float32))
    cb = np.zeros((128, NCB), np.float32)

    def putb(name, arr):
        o, w = CB[name]
        cb[:, o:o + w] = arr
    putb("ident", np.eye(128, dtype=np.float32))
    putb("ones", np.ones((128, 128), np.float32))
    putb("x1", np.where(same & (p[:, None] > p[None, :]), 0.0, BIG).astype(np.float32))
    mu = (p[:, None] <= p[None, :]).astype(np.float32)
    ml = (p[:, None] >= p[None, :]).astype(np.float32)
    putb("masku", np.tile(mu, (1, 4)))
    putb("maskl", np.tile(ml, (1, 4)))
    m2 = np.zeros((128, 4, 16, 32), np.float32)
    for m in range(4):
        q = 32 * m + np.arange(32)
        m2[:, m, :, :] = (p[:, None] <= q[None, :]).astype(np.float32)[:, None, :]
    putb("m2", m2.reshape(128, 2048))
    rm = np.zeros((128, 32), np.float32)
    for m in range(16):
        rm[16 + m, m] = -1.0
    for m in range(16, 32):
        rm[m - 16, m] = 1.0
    putb("rm", rm)
    pos = np.arange(T, dtype=np.float32)
    inv_freq = (np.float32(500000.0) ** (-np.arange(0, 32, 2, dtype=np.float32) / np.float32(32))).astype(np.float32)
    ang = (pos[:, None] * inv_freq[None, :]).astype(np.float32)
    cos = np.cos(ang).astype(np.float32).T
    sin = np.sin(ang).astype(np.float32).T
    rope = np.zeros((32, 2, T), np.float32)
    rope[0:16, 0] = cos
    rope[16:32, 0] = cos
    rope[0:16, 1] = sin
    rope[16:32, 1] = sin
    return cf, cb, rope


def _prep_shared(inp):
    cf, cb, rope = _const_arrays()
    w_in = inp["w_in"][0]

    def put(name, arr):
        o, w = CF[name]
        cf[:, o:o + w] = arr
    put("nw", inp["norm_w"][0].reshape(8, 128).T)
    cw = inp["conv_w"][0, :, 0, :]
    put("cw", cw.T.reshape(12, 128, 4).transpose(1, 0, 2).reshape(128, 48))
    put("dtb", np.broadcast_to(inp["dn_dt_bias"][0][None, :], (128, 4)))
    put("alog", np.broadcast_to(inp["dn_a_log"][0][None, :], (128, 4)))
    put("dnw", np.broadcast_to(np.tile(inp["dn_norm_w"][0], 4)[None, :], (128, 512)))
    put("qw", inp["q_norm_w"][0].T)
    put("kw", inp["k_norm_w"][0].T)
    w_dn = np.ascontiguousarray(w_in[:, 0:2056])
    base = 2056
    w_swa = np.empty((4, 1024, 1280), np.float32)
    for h in range(4):
        cols = []
        for typ in range(3):
            for g in range(3):
                c0 = base + typ * 1536 + g * 512 + h * 128
                cols.append(w_in[:, c0:c0 + 128])
        c0 = base + 3 * 1536 + h * 128
        cols.append(w_in[:, c0:c0 + 128])
        w_swa[h] = np.concatenate(cols, axis=1)
    w_g = np.ascontiguousarray(w_in[:, base + 3 * 1536 + 512:])
    assert w_g.shape[1] == 2048
    return {"cf": cf, "cb": cb, "rope": rope, "w_dn": w_dn, "w_swa": w_swa, "w_g": w_g,
            "w_bd": np.ascontiguousarray(inp["w_branch_dn"][0]), "w_bs": np.ascontiguousarray(inp["w_branch_swa"][0]),
            "w_o": np.ascontiguousarray(inp["w_out"][0])}


def build(stage=99, debug=False):
    nc = bass.Bass("TRN2", target_bir_lowering=False)
    x_d = nc.dram_tensor("x", [T, D], F32, kind="ExternalInput").ap()
    cf_d = nc.dram_tensor("cf", [128, NCF], F32, kind="ExternalInput").ap()
    cb_d = nc.dram_tensor("cb", [128, NCB], F32, kind="ExternalInput").ap()
    rope_d = nc.dram_tensor("rope", [32, 2, T], F32, kind="ExternalInput").ap()
    wdn_d = nc.dram_tensor("w_dn", [D, 2056], F32, kind="ExternalInput").ap()
    wswa_d = nc.dram_tensor("w_swa", [4, D, 1280], F32, kind="ExternalInput").ap()
    wg_d = nc.dram_tensor("w_g", [D, 2048], F32, kind="ExternalInput").ap()
    wbd_d = nc.dram_tensor("w_bd", [512, D], F32, kind="ExternalInput").ap()
    wbs_d = nc.dram_tensor("w_bs", [512, D], F32, kind="ExternalInput").ap()
    wo_d = nc.dram_tensor("w_o", [D, D], F32, kind="ExternalInput").ap()
    out_d = nc.dram_tensor("out", [T, D], F32, kind="ExternalOutput").ap()
    dbg_d = None
    if debug:
        dbg_d = nc.dram_tensor("dbg", [128, 8, T], BF16, kind="ExternalOutput").ap()

    with ExitStack() as es:
        fw = FW(nc, es)

        def sb(name, shape, dt, ctx=es):
            return ctx.enter_context(nc.sbuf_tensor("sb_" + name, shape, dt))

        hT = sb("hT", [128, 8, T], BF16)
        class _OT:
            d = None
            s_ = None

            def __getitem__(self, key):
                p, hsel, cols = key
                if isinstance(hsel, int):
                    return self.d[p, hsel, cols] if hsel < 4 else self.s_[p, hsel - 4, cols]
                assert hsel == slice(0, 4)
                return self.d[p, 0:4, cols]
        oT = _OT()
        oT.d = sb("oTd", [128, 4, T], BF16)
        cf = sb("cf", [128, NCF], F32)
        cb = sb("cb", [128, NCB], BF16)
        r_hT = [Res() for _ in range(NT)]
        r_oT = [[Res() for _ in range(4)] for _ in range(8)]
        r_cf, r_cb = Res(), Res()
        pst = [es.enter_context(nc.psum_tensor("ps%d" % i, [128, 512], F32)) for i in range(8)]
        r_ps = [Res(excl=True) for _ in range(8)]
        rot = {"list": list(range(8)), "i": 0}

        def psget():
            lst = rot["list"]
            i = lst[rot["i"] % len(lst)]
            rot["i"] += 1
            return pst[i], r_ps[i]

        def cfs(name, a=0, b=None):
            o, w = CF[name]
            if b is None:
                b = w
            return cf[:, o + a:o + b]

        def cbs(name, a=0, b=None):
            o, w = CB[name]
            if b is None:
                b = w
            return cb[:, o + a:o + b]

        def mm(out, lhsT, rhs, start, stop, reads, writes):
            return fw.op("pe", lambda e: e.matmul(out, lhsT, rhs, start=start, stop=stop, skip_group_check=True),
                         reads, writes)

        def load_w(dst, src, K, C):
            srcv = src.rearrange("(k p) c -> p k c", p=128)
            pieces = []
            for c0 in range(0, C, 512):
                c1 = min(C, c0 + 512)
                r = Res()
                fw.dma(dst[:, 0:K, c0:c1], srcv[:, :, c0:c1], writes=[r], q="pool")
                pieces.append((c0, c1, r))
            return pieces

        def wres(pieces, a, b):
            return [r for (c0, c1, r) in pieces if c0 < b and c1 > a]

        fw.dma(cf[:], cf_d[:, :], writes=[r_cf])
        with ExitStack() as ph:
            cbst = sb("cbst", [128, NCB], F32, ph)
            r_cbst = Res()
            fw.dma(cbst[:], cb_d[:, :], writes=[r_cbst])
            fw.op("dve", lambda e: e.tensor_copy(out=cb[:], in_=cbst[:]), [r_cbst], [r_cb])
            fw.barrier_all()
        ident_f = cfs("ident")
        ident_b = cbs("ident")
        ones_b = cbs("ones")
        eps_c = cfs("eps")

        ph_dn = es.enter_context(ExitStack())
        wdn_pre = sb("wdn", [128, 8, 2056], BF16, ph_dn)
        wdv_ = wdn_d.rearrange("(k p) c -> p k c", p=128)
        wp_pre = []
        for c0 in range(0, 2056, 512):
            c1 = min(2056, c0 + 512)
            r_ = Res()
            fw.dma(wdn_pre[:, :, c0:c1], wdv_[:, :, c0:c1], writes=[r_], q="pool")
            wp_pre.append((c0, c1, r_))

        with ExitStack() as ph:
            xt = [sb("xt%d" % i, [128, D], F32, ph) for i in range(6)]
            xs = [sb("xs%d" % i, [128, D], BF16, ph) for i in range(6)]
            junk = sb("junk", [128, D], F32, ph)
            ss = sb("ss", [128, NT], F32, ph)
            rt = sb("rt", [128, NT], F32, ph)
            rstd = sb("rstd", [128, NT], F32, ph)
            r_xt = [Res() for _ in range(6)]
            r_xs = [Res() for _ in range(6)]
            r_junk = Res()
            r_st = [Res() for _ in range(NT)]
            def p0_stream(i, s):
                fw.dma(xt[s][:], x_d[i * 128:(i + 1) * 128, :], writes=[r_xt[s]])
                yield
                fw.op("act", lambda e: e.activation(out=junk[:], in_=xt[s][:], func=AF.Square,
                                                    accum_out=ss[:, i:i + 1]), [r_xt[s]], [r_junk, r_st[i]])
                yield
                fw.op("act", lambda e: e.activation(out=rt[:, i:i + 1], in_=ss[:, i:i + 1], func=AF.Sqrt,
                                                    bias=eps_c, scale=1.0 / D), [r_st[i], r_cf], [r_st[i]])
                yield
                fw.op("dve", lambda e: e.reciprocal(out=rstd[:, i:i + 1], in_=rt[:, i:i + 1]), [r_st[i]], [r_st[i]])
                yield
                fw.op("dve", lambda e: e.tensor_scalar(out=xs[s][:], in0=xt[s][:], scalar1=rstd[:, i:i + 1],
                                                       scalar2=None, op0=ALU.mult), [r_xt[s], r_st[i]], [r_xs[s]])
                yield
                pt, rp = psget()
                pb = pt[:].bitcast(BF16)
                for k in range(8):
                    fw.op("pe", lambda e: e.transpose(out=pb[:, k * 128:(k + 1) * 128],
                                                      in_=xs[s][:, k * 128:(k + 1) * 128], identity=ident_b),
                          [r_xs[s], r_cb], [rp])
                yield
                for k in range(8):
                    dst = hT[:, k, i * 128:(i + 1) * 128]
                    src = pb[:, k * 128:(k + 1) * 128]
                    if k % 2 == 0:
                        fw.op("act", lambda e: e.activation(out=dst, in_=src, func=AF.Copy,
                                                            scale=cfs("nw", k, k + 1)), [rp, r_cf], [r_hT[i]])
                    else:
                        fw.op("dve", lambda e: e.tensor_scalar(out=dst, in0=src, scalar1=cfs("nw", k, k + 1),
                                                               scalar2=None, op0=ALU.mult), [rp, r_cf], [r_hT[i]])
                    if k % 2 == 1:
                        yield

            run_streams(list(range(NT)), 6, p0_stream, stagger=2)
            fw.barrier_all()

        def hres(n):
            return r_hT[4 * n:4 * n + 4]

        if stage >= 1:
            phase_dn(nc, fw, ph_dn, sb, psget, mm, load_w, wres, cfs, cbs, hT, oT, r_hT, r_oT, r_cf, r_cb, hres,
                     wdn_d, pst, r_ps, wdn_pre, wp_pre)
            fw.barrier_all()
        ph_dn.close()
        oT.s_ = sb("oTs", [128, 4, T], BF16)

        if stage >= 2:
            with ExitStack() as ph:
                phase_swa(nc, fw, ph, sb, psget, mm, load_w, wres, cfs, cbs, hT, oT, r_hT, r_oT, r_cf, r_cb, hres,
                          wswa_d, rope_d, rot, pst, r_ps)
                fw.barrier_all()
        r_out = [Res(), Res()]
        if stage >= 3:
            with ExitStack() as ph:
                phase_out(nc, fw, ph, sb, psget, mm, load_w, wres, cfs, cbs, hT, oT, r_hT, r_oT, r_cf, r_cb, hres,
                          wg_d, wbd_d, wbs_d, wo_d, x_d, out_d, r_out)
        else:
            with ExitStack() as ph:
                zt = sb("zt", [128, D], F32, ph)
                fw.op("dve", lambda e: e.memset(zt[:], 0.0), [], [r_out[0]])
                for i in range(NT):
                    fw.dma(out_d[i * 128:(i + 1) * 128, :], zt[:], reads=[r_out[0]], semres=r_out[0])
                fw.final_wait(r_out)
        if debug:
            r_dbg = Res()
            allr = [r for hh in r_oT for r in hh]
            for hh in range(8):
                fw.dma(dbg_d[:, hh, :], (hT if stage == 0 else oT)[:, hh, :], reads=allr + r_hT, semres=r_dbg)
            fw.final_wait([r_dbg] + allr)
        fw.final_wait(r_out)
        fw.barrier_all()
    return nc


def phase_dn(nc, fw, ph, sb, psget, mm, load_w, wres, cfs, cbs, hT, oT, r_hT, r_oT, r_cf, r_cb, hres, wdn_d,
             pst, r_ps, wdn, wp):
    ident_f = cfs("ident")
    ident_b = cbs("ident")
    ones_b = cbs("ones")
    eps_c = cfs("eps")
    psp = PSP(pst, r_ps)

    def pget():
        n = 0
        while True:
            p = psp.try_get()
            if p is not None:
                return p
            n += 1
            if n > 10000:
                raise RuntimeError("PSUM starvation")
            yield


    ba = sb("ba", [128, NT, 8], F32, ph)
    beta = sb("beta", [128, NT, 4], F32, ph)
    nbeta = sb("nbeta", [128, NT, 4], F32, ph)
    g = sb("g", [128, NT, 4], F32, ph)
    t0 = sb("a0t0", [128, NT, 4], F32, ph)
    t1 = sb("a0t1", [128, NT, 4], F32, ph)
    t2 = sb("a0t2", [128, NT, 4], F32, ph)
    negA = sb("negA", [128, 4], F32, ph)
    gc = sb("gc", [128, NT, 4], F32, ph)
    egc = sb("egc", [128, NT, 4], F32, ph)
    sckbg = sb("sckbg", [128, NT, 4], F32, ph)
    sckd = sb("sckd", [128, NT, 4], F32, ph)
    egl = sb("egl", [128, 2, NT, 4], F32, ph)
    r_a0 = Res()
    pba, rpba = psp.get()
    for i in range(NT):
        for k in range(8):
            mm(pba[:, i * 8:(i + 1) * 8], hT[:, k, i * 128:(i + 1) * 128], wdn[:, k, 2048:2056], k == 0, k == 7,
               [r_hT[i]] + wres(wp, 2048, 2056), [rpba])
    bav = ba[:].rearrange("p t c -> p (t c)")
    fw.op("act", lambda e: e.activation(out=bav, in_=pba[:, 0:128], func=AF.Copy), [rpba], [r_a0])
    psp.rel(rpba)
    fw.op("act", lambda e: e.activation(out=beta[:], in_=ba[:, :, 0:4], func=AF.Sigmoid), [r_a0], [r_a0])
    fw.op("dve", lambda e: e.tensor_scalar(out=nbeta[:], in0=beta[:], scalar1=-1.0, scalar2=None, op0=ALU.mult),
          [r_a0], [r_a0])
    dtb_b = cfs("dtb").unsqueeze(1).to_broadcast([128, NT, 4])
    fw.op("dve", lambda e: e.tensor_tensor(out=t0[:], in0=ba[:, :, 4:8], in1=dtb_b, op=ALU.add), [r_a0, r_cf], [r_a0])
    fw.op("dve", lambda e: e.tensor_scalar(out=t2[:], in0=t0[:], scalar1=-1.0, scalar2=None, op0=ALU.mult),
          [r_a0], [r_a0])
    fw.op("dve", lambda e: e.tensor_tensor(out=t1[:], in0=t0[:], in1=t2[:], op=ALU.max), [r_a0], [r_a0])
    fw.op("act", lambda e: e.activation(out=t1[:], in_=t1[:], func=AF.Exp, scale=-1.0), [r_a0], [r_a0])
    fw.op("act", lambda e: e.activation(out=t1[:], in_=t1[:], func=AF.Ln, bias=1.0), [r_a0], [r_a0])
    fw.op("dve", lambda e: e.tensor_scalar(out=t2[:], in0=t0[:], scalar1=0.0, scalar2=None, op0=ALU.max),
          [r_a0], [r_a0])
    fw.op("dve", lambda e: e.tensor_tensor(out=t2[:], in0=t2[:], in1=t1[:], op=ALU.add), [r_a0], [r_a0])
    fw.op("act", lambda e: e.activation(out=negA[:], in_=cfs("alog"), func=AF.Exp), [r_cf], [r_a0])
    fw.op("dve", lambda e: e.tensor_scalar(out=negA[:], in0=negA[:], scalar1=-1.0, scalar2=None, op0=ALU.mult),
          [r_a0], [r_a0])
    negA_b = negA[:].unsqueeze(1).to_broadcast([128, NT, 4])
    fw.op("dve", lambda e: e.tensor_tensor(out=g[:], in0=t2[:], in1=negA_b, op=ALU.mult), [r_a0], [r_a0])
    gv = g[:].rearrange("p t c -> p (t c)")
    pg1, rpg1 = psp.get()
    mm(pg1[:, 0:64], cfs("tri"), gv, True, True, [r_a0, r_cf], [rpg1])
    mm(pg1[:, 64:128], cfs("tris"), gv, True, True, [r_a0, r_cf], [rpg1])
    mm(pg1[:, 128:192], cfs("ch0"), gv, True, True, [r_a0, r_cf], [rpg1])
    mm(pg1[:, 192:256], cfs("ch1"), gv, True, True, [r_a0, r_cf], [rpg1])
    fw.op("act", lambda e: e.activation(out=gc[:].rearrange("p t c -> p (t c)"), in_=pg1[:, 0:64], func=AF.Copy),
          [rpg1], [r_a0])
    fw.op("act", lambda e: e.activation(out=egc[:].rearrange("p t c -> p (t c)"), in_=pg1[:, 0:64], func=AF.Exp),
          [rpg1], [r_a0])
    fw.op("act", lambda e: e.activation(out=sckd[:].rearrange("p t c -> p (t c)"), in_=pg1[:, 64:128], func=AF.Exp),
          [rpg1], [r_a0])
    fw.op("act", lambda e: e.activation(out=egl[:].rearrange("p a t c -> p (a t c)"), in_=pg1[:, 128:256],
                                        func=AF.Exp), [rpg1], [r_a0])
    psp.rel(rpg1)
    fw.op("dve", lambda e: e.tensor_tensor(out=sckbg[:], in0=beta[:], in1=egc[:], op=ALU.mult), [r_a0], [r_a0])

    NS = 2
    halo = sb("halo", [128, 12, 3], F32, ph)
    r_halo = [Res() for _ in range(12)]
    raw = [sb("raw%d" % i, [128, 515], F32, ph) for i in range(NS)]
    cv = [sb("cv%d" % i, [128, 512], F32, ph) for i in range(NS)]
    cs = [sb("cs%d" % i, [128, 512], F32, ph) for i in range(NS)]
    sq = [sb("sq%d" % i, [128, 512], BF16, ph) for i in range(NS)]
    rn = [sb("rn%d" % i, [128, 512], F32, ph) for i in range(NS)]
    r_raw = [Res() for _ in range(NS)]
    r_cv = [Res() for _ in range(NS)]
    r_cs = [Res() for _ in range(NS)]
    r_sq = [Res() for _ in range(NS)]
    r_rn = [Res() for _ in range(NS)]
    qT = [sb("qT%d" % i, [128, 4, 512], BF16, ph) for i in range(2)]
    kT = [sb("kT%d" % i, [128, 4, 512], BF16, ph) for i in range(2)]
    kbg = [sb("kbg%d" % i, [128, 4, 4, 128], BF16, ph) for i in range(2)]
    kd = [sb("kd%d" % i, [128, 4, 4, 128], BF16, ph) for i in range(2)]
    vb = [sb("vb%d" % i, [128, 4, 4, 128], BF16, ph) for i in range(2)]
    r_qT = [[Res() for _ in range(4)] for _ in range(2)]
    r_kT = [[Res() for _ in range(4)] for _ in range(2)]
    r_kbg = [[Res() for _ in range(4)] for _ in range(2)]
    r_kd = [[Res() for _ in range(4)] for _ in range(2)]
    r_vb = [[Res() for _ in range(4)] for _ in range(2)]
    NP = 2
    arena = [sb("prearena%d" % i, [128, 2816], F32, ph) for i in range(NP)]
    e1m = [arena[i][:, 0:512] for i in range(NP)]
    tg = [arena[i][:, 512:1024] for i in range(NP)]
    abf = [arena[i][:, 1024:2816].bitcast(BF16) for i in range(NP)]
    Mb = [[abf[i][:, 0:512], abf[i][:, 512:1024]] for i in range(NP)]
    MTb = [[abf[i][:, 1024:1536], abf[i][:, 1536:2048]] for i in range(NP)]
    IMb = [abf[i][:, 2048:2560] for i in range(NP)]
    PTb = [[abf[i][:, 2560:3072], abf[i][:, 3072:3584]] for i in range(NP)]
    for i in range(NP):
        raw.append(arena[i][:, 0:515])
        cv.append(arena[i][:, 516:1028])
        cs.append(arena[i][:, 1028:1540])
        rn.append(arena[i][:, 1540:2052])
        sq.append(arena[i][:, 2052:2308].bitcast(BF16))
        for lst in (r_raw, r_cv, r_cs, r_sq, r_rn):
            lst.append(Res())
    HS = 4
    AT = [sb("AT%d" % i, [128, 512], BF16, ph) for i in range(HS)]
    u32 = [sb("u32%d" % i, [128, 512], F32, ph) for i in range(HS)]
    wT = [sb("wT%d" % i, [128, 512], BF16, ph) for i in range(HS)]
    r_e1m = [Res() for _ in range(NP)]
    r_tg = [Res() for _ in range(NP)]
    r_M = [[Res(), Res()] for _ in range(NP)]
    r_MT = [[Res(), Res()] for _ in range(NP)]
    r_IM = [Res() for _ in range(NP)]
    r_PT = [[Res(), Res()] for _ in range(NP)]
    r_AT = [Res() for _ in range(HS)]
    r_u32 = [Res() for _ in range(HS)]
    r_wT = [Res() for _ in range(HS)]
    vn = sb("vn", [128, 512], BF16, ph)
    tq = sb("tq", [128, 512], F32, ph)
    o32b = [sb("o32%d" % i, [128, 512], F32, ph) for i in range(2)]
    r_o32b = [Res(), Res()]
    osq = sb("osq", [128, 512], F32, ph)
    oss = sb("oss", [128, 4], F32, ph)
    on = sb("on", [128, 512], BF16, ph)
    zs = sb("zs", [128, 512], F32, ph)
    zsw = zs
    S32 = sb("S32", [128, 512], F32, ph)
    Sb = sb("Sb", [128, 512], BF16, ph)
    r_vn, r_tq, r_osq, r_oss, r_on, r_zs, r_zsw, r_S32, r_Sb = [Res() for _ in range(9)]
    fw.op("dve", lambda e: e.memset(S32[:], 0.0), [], [r_S32])
    fw.op("dve", lambda e: e.memset(Sb[:], 0.0), [], [r_Sb])

    def v4(ap):
        return ap.rearrange("p (h d) -> p h d", h=4)

    idf_b = ident_f.unsqueeze(1).to_broadcast([128, 4, 128])
    idb_b = ident_b.unsqueeze(1).to_broadcast([128, 4, 128])

    def sec_stream(arg, s):
        n, h, typ = arg
        par = n % 2
        tok = slice(n * 512, (n + 1) * 512)
        sec = typ * 4 + h
        col0 = typ * 512 + h * 128
        pr, rpr = yield from pget()
        for k in range(8):
            mm(pr[:, 0:512], wdn[:, k, col0:col0 + 128], hT[:, k, tok], k == 0, k == 7,
               hres(n) + wres(wp, col0, col0 + 128), [rpr])
        if n == 0:
            fw.op("pool", lambda e: e.memset(raw[s][:, 0:3], 0.0), [], [r_raw[s]])
        else:
            fw.op("pool", lambda e: e.tensor_copy(out=raw[s][:, 0:3], in_=halo[:, sec, :]),
                  [r_halo[sec]], [r_raw[s]])
        yield
        fw.op("act", lambda e: e.activation(out=raw[s][:, 3:515], in_=pr[:, 0:512], func=AF.Copy),
              [rpr], [r_raw[s]])
        psp.rel(rpr)
        yield
        fw.op("pool", lambda e: e.tensor_copy(out=halo[:, sec, :], in_=raw[s][:, 512:515]),
              [r_raw[s]], [r_halo[sec]])
        cwo = sec * 4
        fw.op("act", lambda e: e.activation(out=cv[s][:], in_=raw[s][:, 0:512], func=AF.Copy,
                                            scale=cfs("cw", cwo, cwo + 1)), [r_raw[s], r_cf], [r_cv[s]])
        yield
        for j in range(1, 4):
            fw.op("dve", lambda e: e.scalar_tensor_tensor(out=cv[s][:], in0=raw[s][:, j:j + 512],
                                                          scalar=cfs("cw", cwo + j, cwo + j + 1),
                                                          in1=cv[s][:], op0=ALU.mult, op1=ALU.add),
                  [r_raw[s], r_cf, r_cv[s]], [r_cv[s]])
            yield
        fw.op("act", lambda e: e.activation(out=cs[s][:], in_=cv[s][:], func=AF.Silu), [r_cv[s]], [r_cs[s]])
        yield
        if typ < 2:
            fw.op("act", lambda e: e.activation(out=sq[s][:], in_=cs[s][:], func=AF.Square), [r_cs[s]], [r_sq[s]])
            yield
            pss, rpss = yield from pget()
            mm(pss[:, 0:512], ones_b, sq[s][:], True, True, [r_sq[s], r_cb], [rpss])
            yield
            fw.op("act", lambda e: e.activation(out=rn[s][:], in_=pss[:, 0:512], func=AF.Ln, bias=eps_c,
                                                scale=1.0), [rpss, r_cf], [r_rn[s]])
            psp.rel(rpss)
            yield
            fw.op("act", lambda e: e.activation(out=rn[s][:], in_=rn[s][:], func=AF.Exp, scale=-0.5),
                  [r_rn[s]], [r_rn[s]])
            yield
        if typ == 0:
            fw.op("dve", lambda e: e.scalar_tensor_tensor(out=qT[par][:, h, :], in0=cs[s][:], scalar=QSCALE,
                                                          in1=rn[s][:], op0=ALU.mult, op1=ALU.mult),
                  [r_cs[s], r_rn[s]], [r_qT[par][h]])
            return
        if typ == 1:
            fw.op("dve", lambda e: e.tensor_tensor(out=cv[s][:], in0=cs[s][:], in1=rn[s][:], op=ALU.mult),
                  [r_cs[s], r_rn[s]], [r_cv[s]])
            yield
            fw.op("act", lambda e: e.activation(out=kT[par][:, h, :], in_=cv[s][:], func=AF.Copy),
                  [r_cv[s]], [r_kT[par][h]])
            src, rsrc = cv[s], r_cv[s]
        else:
            src, rsrc = cs[s], r_cs[s]
        ptr, rptr = yield from pget()
        for j in range(4):
            fw.op("pe", lambda e: e.transpose(out=ptr[:, j * 128:(j + 1) * 128],
                                              in_=src[:, j * 128:(j + 1) * 128], identity=ident_f),
                  [rsrc, r_cf], [rptr])
        yield
        pv = ptr[:, 0:512].rearrange("p (j d) -> p j d", j=4)
        if typ == 1:
            sc1 = sckbg[:, 4 * n:4 * n + 4, h:h + 1].to_broadcast([128, 4, 128])
            sc2 = sckd[:, 4 * n:4 * n + 4, h:h + 1].to_broadcast([128, 4, 128])
            fw.op("dve", lambda e: e.tensor_tensor(out=kbg[par][:, :, h, :], in0=pv, in1=sc1, op=ALU.mult),
                  [rptr, r_a0], [r_kbg[par][h]])
            yield
            fw.op("dve", lambda e: e.tensor_tensor(out=kd[par][:, :, h, :], in0=pv, in1=sc2, op=ALU.mult),
                  [rptr, r_a0], [r_kd[par][h]])
        else:
            sc3 = beta[:, 4 * n:4 * n + 4, h:h + 1].to_broadcast([128, 4, 128])
            fw.op("dve", lambda e: e.tensor_tensor(out=vb[par][:, :, h, :], in0=pv, in1=sc3, op=ALU.mult),
                  [rptr, r_a0], [r_vb[par][h]])
        psp.rel(rptr)

    def pre_stream(i, s):
        n, j = i // 4, i % 4
        par = n % 2
        hs_ = i % HS
        tk = slice(j * 128, (j + 1) * 128)
        pB, rpB = yield from pget()
        for h in range(4):
            gb = g[:, i, h:h + 1].to_broadcast([128, 128])
            mm(pB[:, h * 128:(h + 1) * 128], gb, cfs("tri"), True, False, [r_a0, r_cf], [rpB])
            mm(pB[:, h * 128:(h + 1) * 128], ident_b, cbs("x1"), False, True, [r_cb], [rpB])
        yield
        for h in range(4):
            fw.op("act", lambda e: e.activation(out=e1m[s][:, h * 128:(h + 1) * 128],
                                                in_=pB[:, h * 128:(h + 1) * 128], func=AF.Exp,
                                                bias=gc[:, i, h:h + 1], scale=-1.0), [rpB, r_a0], [r_e1m[s]])
        psp.rel(rpB)
        pG, rpG = yield from pget()
        for h in range(4):
            mm(pG[:, h * 128:(h + 1) * 128], kT[par][:, h, tk], kT[par][:, h, tk], True, True,
               [r_kT[par][h]], [rpG])
        yield
        nb_b = nbeta[:, i, :].unsqueeze(2).to_broadcast([128, 4, 128])
        fw.op("dve", lambda e: e.tensor_tensor(out=v4(tg[s][:]), in0=v4(pG[:, 0:512]), in1=nb_b, op=ALU.mult),
              [rpG, r_a0], [r_tg[s]])
        psp.rel(rpG)
        yield
        fw.op("dve", lambda e: e.tensor_tensor(out=Mb[s][0][:], in0=tg[s][:], in1=e1m[s][:], op=ALU.mult),
              [r_tg[s], r_e1m[s]], [r_M[s][0]])
        pE, rpE = yield from pget()
        for h in range(4):
            fw.op("pe", lambda e: e.transpose(out=pE[:, h * 128:(h + 1) * 128],
                                              in_=e1m[s][:, h * 128:(h + 1) * 128], identity=ident_f),
                  [r_e1m[s], r_cf], [rpE])
        yield
        pT, rpT = yield from pget()
        pTb = pT[:].bitcast(BF16)
        for h in range(4):
            fw.op("pe", lambda e: e.transpose(out=pTb[:, h * 128:(h + 1) * 128],
                                              in_=Mb[s][0][:, h * 128:(h + 1) * 128], identity=ident_b),
                  [r_M[s][0], r_cb], [rpT])
        fw.op("dve", lambda e: e.tensor_tensor(out=v4(tg[s][:]), in0=v4(pE[:, 0:512]), in1=idf_b, op=ALU.add),
              [rpE, r_cf, r_tg[s]], [r_tg[s]])
        psp.rel(rpE)
        yield
        fw.op("act", lambda e: e.activation(out=MTb[s][0][:], in_=pTb[:, 0:512], func=AF.Copy), [rpT], [r_MT[s][0]])
        yield
        fw.op("dve", lambda e: e.tensor_tensor(out=v4(PTb[s][0][:]), in0=v4(pTb[:, 0:512]), in1=idb_b, op=ALU.add),
              [rpT, r_cb], [r_PT[s][0]])
        psp.rel(rpT)
        pQ, rpQ = yield from pget()
        for h in range(4):
            mm(pQ[:, h * 128:(h + 1) * 128], kT[par][:, h, tk], qT[par][:, h, tk], True, True,
               [r_kT[par][h], r_qT[par][h]], [rpQ])
        yield
        fw.op("dve", lambda e: e.tensor_tensor(out=AT[hs_][:], in0=pQ[:, 0:512], in1=tg[s][:], op=ALU.mult),
              [rpQ, r_tg[s]], [r_AT[hs_]])
        psp.rel(rpQ)
        cur = 0
        for lev in range(1, 6):
            nxt = 1 - cur
            pM, rpM = yield from pget()
            for h in range(4):
                hs = slice(h * 128, (h + 1) * 128)
                mm(pM[:, hs], MTb[s][cur][:, hs], Mb[s][cur][:, hs], True, True, [r_MT[s][cur], r_M[s][cur]], [rpM])
            if lev < 5:
                pMT, rpMT = yield from pget()
                for h in range(4):
                    hs = slice(h * 128, (h + 1) * 128)
                    mm(pMT[:, hs], Mb[s][cur][:, hs], MTb[s][cur][:, hs], True, True,
                       [r_MT[s][cur], r_M[s][cur]], [rpMT])
            yield
            fw.op("dve", lambda e: e.tensor_tensor(out=v4(IMb[s][:]), in0=v4(pM[:, 0:512]), in1=idf_b, op=ALU.add),
                  [rpM, r_cf], [r_IM[s]])
            if lev < 5:
                fw.op("act", lambda e: e.activation(out=MTb[s][nxt][:], in_=pMT[:, 0:512], func=AF.Copy),
                      [rpMT], [r_MT[s][nxt]])
                psp.rel(rpMT)
            yield
            if lev < 5:
                fw.op("act", lambda e: e.activation(out=Mb[s][nxt][:], in_=pM[:, 0:512], func=AF.Copy),
                      [rpM], [r_M[s][nxt]])
            psp.rel(rpM)
            pP, rpP = yield from pget()
            for h in range(4):
                hs = slice(h * 128, (h + 1) * 128)
                mm(pP[:, hs], IMb[s][:, hs], PTb[s][cur][:, hs], True, True, [r_IM[s], r_PT[s][cur]], [rpP])
            yield
            fw.op("act" if lev % 2 else "dve",
                  (lambda e: e.activation(out=PTb[s][nxt][:], in_=pP[:, 0:512], func=AF.Copy)) if lev % 2 else
                  (lambda e: e.tensor_copy(out=PTb[s][nxt][:], in_=pP[:, 0:512])), [rpP], [r_PT[s][nxt]])
            psp.rel(rpP)
            cur = nxt
            yield
        PTf, r_PTf = PTb[s][cur], r_PT[s][cur]
        pU, rpU = yield from pget()
        for h in range(4):
            hs = slice(h * 128, (h + 1) * 128)
            mm(pU[:, hs], PTf[:, hs], vb[par][:, j, h, :], True, True, [r_PTf, r_vb[par][h]], [rpU])
        pW, rpW = yield from pget()
        for h in range(4):
            hs = slice(h * 128, (h + 1) * 128)
            mm(pW[:, hs], kbg[par][:, j, h, :], PTf[:, hs], True, True, [r_PTf, r_kbg[par][h]], [rpW])
        yield
        fw.op("act", lambda e: e.activation(out=u32[hs_][:], in_=pU[:, 0:512], func=AF.Copy), [rpU], [r_u32[hs_]])
        psp.rel(rpU)
        fw.op("dve", lambda e: e.tensor_copy(out=wT[hs_][:], in_=pW[:, 0:512]), [rpW], [r_wT[hs_]])
        psp.rel(rpW)

    def chain_stream(i):
        n, j = i // 4, i % 4
        par = n % 2
        s = i % HS
        o32 = o32b[i % 2]
        r_o32 = r_o32b[i % 2]
        tk = slice(j * 128, (j + 1) * 128)
        for c in range(2):
            R = slice(c * 64, (c + 1) * 64)
            pSw, rpSw = yield from pget()
            for h in range(4):
                hs = slice(h * 128, (h + 1) * 128)
                mm(pSw[:, hs], wT[s][:, hs], Sb[:, hs], True, True, [r_wT[s], r_Sb], [rpSw])
            pSq, rpSq = yield from pget()
            for h in range(4):
                hs = slice(h * 128, (h + 1) * 128)
                mm(pSq[:, hs], qT[par][:, h, tk], Sb[:, hs], True, True, [r_qT[par][h], r_Sb], [rpSq])
            yield
            egl_b = egl[:, c, i, :].unsqueeze(2).to_broadcast([128, 4, 128])
            fw.op("dve", lambda e: e.tensor_tensor(out=v4(S32[:]), in0=v4(S32[:]), in1=egl_b, op=ALU.mult),
                  [r_S32, r_a0], [r_S32])
            fw.op("dve", lambda e: e.tensor_tensor(out=vn[R, :], in0=u32[s][R, :], in1=pSw[R, 0:512],
                                                   op=ALU.subtract), [r_u32[s], rpSw], [r_vn])
            psp.rel(rpSw)
            yield
            pSs, rpSs = yield from pget()
            for h in range(4):
                hs = slice(h * 128, (h + 1) * 128)
                mm(pSs[:, hs], kd[par][R, j, h, :], vn[R, hs], True, True, [r_kd[par][h], r_vn], [rpSs])
            pSa, rpSa = yield from pget()
            for h in range(4):
                hs = slice(h * 128, (h + 1) * 128)
                mm(pSa[:, hs], AT[s][R, hs], vn[R, hs], True, True, [r_AT[s], r_vn], [rpSa])
            egc_b = egc[R, i, :].unsqueeze(2).to_broadcast([64, 4, 128])
            fw.op("dve", lambda e: e.tensor_tensor(out=v4(tq[R, :]), in0=v4(pSq[R, 0:512]), in1=egc_b,
                                                   op=ALU.mult), [rpSq, r_a0], [r_tq])
            psp.rel(rpSq)
            yield
            fw.op("dve", lambda e: e.tensor_tensor(out=S32[:], in0=S32[:], in1=pSs[:, 0:512], op=ALU.add),
                  [r_S32, rpSs], [r_S32])
            psp.rel(rpSs)
            yield
            fw.op("act", lambda e: e.activation(out=Sb[:], in_=S32[:], func=AF.Copy), [r_S32], [r_Sb])
            fw.op("dve", lambda e: e.tensor_tensor(out=o32[R, :], in0=tq[R, :], in1=pSa[R, 0:512], op=ALU.add),
                  [r_tq, rpSa], [r_o32])
            psp.rel(rpSa)
            yield

    def epi_stream(i):
        n, j = i // 4, i % 4
        o32 = o32b[i % 2]
        r_o32 = r_o32b[i % 2]
        pz, rpz = yield from pget()
        for k in range(8):
            mm(pz[:, 0:512], hT[:, k, i * 128:(i + 1) * 128], wdn[:, k, 1536:2048], k == 0, k == 7,
               [r_hT[i]] + wres(wp, 1536, 2048), [rpz])
        yield
        fw.op("act", lambda e: e.activation(out=zs[:], in_=pz[:, 0:512], func=AF.Silu), [rpz], [r_zs])
        psp.rel(rpz)
        fw.op("act", lambda e: e.activation(out=osq[:], in_=o32[:], func=AF.Square), [r_o32], [r_osq])
        yield
        fw.op("dve", lambda e: e.tensor_tensor(out=zsw[:], in0=zs[:], in1=cfs("dnw"), op=ALU.mult),
              [r_zs, r_cf], [r_zsw])
        fw.op("dve", lambda e: e.tensor_reduce(out=oss[:], in_=v4(osq[:]), axis=AX.X, op=ALU.add),
              [r_osq], [r_oss])
        yield
        fw.op("act", lambda e: e.activation(out=oss[:], in_=oss[:], func=AF.Ln, bias=eps_c, scale=1.0 / 128),
              [r_oss, r_cf], [r_oss])
        fw.op("act", lambda e: e.activation(out=oss[:], in_=oss[:], func=AF.Exp, scale=-0.5), [r_oss], [r_oss])
        yield
        oss_b = oss[:].unsqueeze(2).to_broadcast([128, 4, 128])
        fw.op("dve", lambda e: e.tensor_tensor(out=v4(osq[:]), in0=v4(o32[:]), in1=oss_b, op=ALU.mult),
              [r_o32, r_oss, r_osq], [r_osq])
        fw.op("dve", lambda e: e.tensor_tensor(out=on[:], in0=osq[:], in1=zsw[:], op=ALU.mult),
              [r_osq, r_zsw], [r_on])
        yield
        pO, rpO = yield from pget()
        pOb = pO[:].bitcast(BF16)
        for h in range(4):
            fw.op("pe", lambda e: e.transpose(out=pOb[:, h * 128:(h + 1) * 128],
                                              in_=on[:, h * 128:(h + 1) * 128], identity=ident_b),
                  [r_on, r_cb], [rpO])
        yield
        fw.op("act", lambda e: e.activation(out=oT[:, 0:4, i * 128:(i + 1) * 128], in_=v4(pOb[:, 0:512]),
                                            func=AF.Copy), [rpO], [r_oT[h][n] for h in range(4)])
        psp.rel(rpO)

    def sec_items(n):
        return [(n, h, typ) for h in range(4) for typ in range(2)] + [(n, h, 2) for h in range(4)]

    def chain(gens):
        for g_ in gens:
            yield from g_

    def tiles_gen():
        for i in range(NT):
            yield ("need_sections", i // 4)
            yield from pre_stream(i, i % NP)
            yield ("pre_done", i)

    import os
    nbanks = int(os.environ.get('DN_BANKS', '4'))
    ntl = 4 * nbanks
    sec_done = [False] * 5
    pre_done = [False] * (NT + 1)
    chain_done = [False] * (NT + 1)
    epi_done = [False] * (NT + 1)

    def sections_meta():
        for n in range(nbanks):
            while n >= 2 and not chain_done[4 * (n - 2) + 3]:
                yield
            if n == 0:
                yield from streams_gen(sec_items(n), NS + NP, sec_stream)
                fw.barrier_all()
            else:
                yield from streams_gen(sec_items(n), NS, sec_stream)
            sec_done[n] = True

    def pre_one(i, s):
        while not sec_done[i // 4]:
            yield
        while i >= HS and not chain_done[i - HS]:
            yield
        yield from pre_stream(i, s)
        pre_done[i] = True

    def pre_meta():
        yield from streams_gen(list(range(ntl)), NP, pre_one, stagger=8)

    def chain_meta():
        for i in range(ntl):
            while not pre_done[i]:
                yield
            while i >= 2 and not epi_done[i - 2]:
                yield
            yield from chain_stream(i)
            chain_done[i] = True

    def epi_meta():
        for i in range(ntl):
            while not chain_done[i]:
                yield
            yield from epi_stream(i)
            epi_done[i] = True

    metas = [sections_meta(), pre_meta(), chain_meta(), epi_meta()]
    stall = 0
    last = fw.ninst
    while metas:
        for m_ in list(metas):
            try:
                next(m_)
            except StopIteration:
                metas.remove(m_)
        if fw.ninst == last:
            stall += 1
            if stall > 100000:
                raise RuntimeError("stream scheduler livelock")
        else:
            stall = 0
            last = fw.ninst


def phase_swa(nc, fw, ph, sb, psget, mm, load_w, wres, cfs, cbs, hT, oT, r_hT, r_oT, r_cf, r_cb, hres,
              wswa_d, rope_d, rot, pst, r_ps):
    ident_b = cbs("ident")
    ones_b = cbs("ones")
    eps_c = cfs("eps")
    psp = PSP(pst, r_ps)

    def pget():
        n = 0
        while True:
            p = psp.try_get()
            if p is not None:
                return p
            n += 1
            if n > 100000:
                raise RuntimeError("PSUM starvation")
            yield

    rope = sb("rope", [32, 2, T], F32, ph)
    r_rope = Res()
    fw.dma(rope[:], rope_d[:, :, :], writes=[r_rope])
    wsw = sb("wsw", [128, 8, 1280], BF16, ph)
    wr = {"qk": Res(), "v": Res(), "z": Res()}
    wrange = {"qk": (0, 768), "v": (768, 1152), "z": (1152, 1280)}

    def load_head(h, part):
        c0, c1 = wrange[part]
        srcv = wswa_d[h].rearrange("(k p) c -> p k c", p=128)
        fw.dma(wsw[:, :, c0:c1], srcv[:, :, c0:c1], writes=[wr[part]], q="pool")

    qk = [[sb("qk%d%d" % (t_, g_), [128, T], BF16, ph) for g_ in range(3)] for t_ in range(2)]
    r_qk = [[[Res() for _ in range(4)] for _ in range(3)] for _ in range(2)]
    V = [sb("V%d" % g_, [128, 16, 128], BF16, ph) for g_ in range(3)]
    r_V = [[Res() for _ in range(16)] for _ in range(3)]
    NQ = 6
    sq = [sb("ssq%d" % i, [128, 512], BF16, ph) for i in range(NQ)]
    rn = [sb("srn%d" % i, [128, 512], F32, ph) for i in range(NQ)]
    qn = [sb("sqn%d" % i, [128, 512], F32, ph) for i in range(NQ)]
    qnb = [sb("sqnb%d" % i, [32, 512], BF16, ph) for i in range(NQ)]
    r_sq = [Res() for _ in range(NQ)]
    r_rn = [Res() for _ in range(NQ)]
    r_qn = [Res() for _ in range(NQ)]
    r_qnb = [Res() for _ in range(NQ)]
    NA = 3
    zsall = sb("zsall", [128, T], F32, ph)
    r_zsall = [Res() for _ in range(4)]
    ee = [[sb("see%d%d" % (s, i), [128, 512], BF16, ph) for i in range(2)] for s in range(NA)]
    PP = ee
    r_ee = [[Res(), Res()] for _ in range(NA)]
    r_PP = r_ee
    rden = [sb("srden%d" % s, [128, 512], F32, ph) for s in range(NA)]
    tn = rden
    r_rden = [Res() for _ in range(NA)]
    r_tn = r_rden

    for part in ("qk", "v", "z"):
        load_head(0, part)

    for h in range(4):
        def qk_stream(arg, s):
            g_, typ, n = arg
            c0 = typ * 384 + g_ * 128
            dst = qk[typ][g_]
            wname = "qw" if typ == 0 else "kw"
            tok = slice(n * 512, (n + 1) * 512)
            pr, rpr = yield from pget()
            for k in range(8):
                mm(pr[:, 0:512], wsw[:, k, c0:c0 + 128], hT[:, k, tok], k == 0, k == 7, hres(n) + [wr["qk"]], [rpr])
            yield
            fw.op("act", lambda e: e.activation(out=sq[s][:], in_=pr[:, 0:512], func=AF.Square), [rpr], [r_sq[s]])
            yield
            pss, rpss = yield from pget()
            mm(pss[:, 0:512], ones_b, sq[s][:], True, True, [r_sq[s], r_cb], [rpss])
            yield
            fw.op("act", lambda e: e.activation(out=rn[s][:], in_=pss[:, 0:512], func=AF.Ln, bias=eps_c,
                                                scale=1.0 / 128), [rpss, r_cf], [r_rn[s]])
            psp.rel(rpss)
            yield
            fw.op("act", lambda e: e.activation(out=rn[s][:], in_=rn[s][:], func=AF.Exp, scale=-0.5),
                  [r_rn[s]], [r_rn[s]])
            yield
            fw.op("dve", lambda e: e.scalar_tensor_tensor(out=qn[s][:], in0=pr[:, 0:512],
                                                          scalar=cfs(wname, g_, g_ + 1), in1=rn[s][:],
                                                          op0=ALU.mult, op1=ALU.mult),
                  [rpr, r_rn[s], r_cf], [r_qn[s]])
            psp.rel(rpr)
            yield
            fw.op("dve", lambda e: e.tensor_tensor(out=qnb[s][:], in0=qn[s][0:32, :], in1=rope[:, 1, tok],
                                                   op=ALU.mult), [r_qn[s], r_rope], [r_qnb[s]])
            fw.op("act", lambda e: e.activation(out=dst[:, tok], in_=qn[s][:], func=AF.Copy),
                  [r_qn[s]], [r_qk[typ][g_][n]])
            yield
            prt, rprt = yield from pget()
            mm(prt[0:32, 0:512], cbs("rm")[0:32, :], qnb[s][:], True, True, [r_qnb[s], r_cb], [rprt])
            fw.op("dve", lambda e: e.tensor_tensor(out=rn[s][0:32, :], in0=qn[s][0:32, :], in1=rope[:, 0, tok],
                                                   op=ALU.mult), [r_qn[s], r_rope, r_rn[s]], [r_rn[s]])
            yield
            fw.op("dve", lambda e: e.tensor_tensor(out=dst[0:32, tok], in0=rn[s][0:32, :], in1=prt[0:32, 0:512],
                                                   op=ALU.add), [r_rn[s], rprt], [r_qk[typ][g_][n]])
            psp.rel(rprt)

        def v_stream(arg, s):
            g_, b4 = arg
            c0 = 768 + g_ * 128
            pv, rpv = yield from pget()
            for bb in range(4):
                blk = b4 * 4 + bb
                if g_ == 0:
                    tsl = slice(blk * 128, (blk + 1) * 128)
                    hr = [r_hT[blk]]
                elif g_ == 1:
                    n_, r_ = blk // 4, blk % 4
                    tsl = slice(512 * n_ + r_, 512 * (n_ + 1), 4)
                    hr = hres(n_)
                else:
                    tsl = slice(blk, T, 16)
                    hr = r_hT
                for k in range(8):
                    mm(pv[:, bb * 128:(bb + 1) * 128], hT[:, k, tsl], wsw[:, k, c0:c0 + 128], k == 0, k == 7,
                       list(hr) + [wr["v"]], [rpv])
                yield
            vout = V[g_][:, b4 * 4:(b4 + 1) * 4, :]
            vin = pv[:, 0:512].rearrange("p (b d) -> p b d", b=4)
            vres = [r_V[g_][b4 * 4 + bb] for bb in range(4)]
            if (g_ * 4 + b4) % 2 == 0:
                fw.op("act", lambda e: e.activation(out=vout, in_=vin, func=AF.Copy), [rpv], vres)
            else:
                fw.op("dve", lambda e: e.tensor_copy(out=vout, in_=vin), [rpv], vres)
            psp.rel(rpv)

        def mixed_stream(arg, s):
            if arg[0] == "v":
                return v_stream(arg[1:], s)
            return qk_stream(arg[1:], s)

        qk_items = [("qk", g_, typ, n) for g_ in range(3) for typ in range(2) for n in range(4)]
        v_items = [("v", g_, b4) for g_ in range(3) for b4 in range(4)]
        items = []
        for ii in range(12):
            items += qk_items[2 * ii:2 * ii + 2] + [v_items[ii]]
        run_streams(items, NQ, mixed_stream, stagger=2)
        if h < 3:
            load_head(h + 1, "qk")
        if h < 3:
            load_head(h + 1, "v")

        def attn_stream(m, s):
            num, r_num = yield from pget()
            den, r_den = yield from pget()
            first = [True, True]

            def st(idx):
                f = first[idx]
                first[idx] = False
                return f

            def geom(g_):
                def cols(n_, r_):
                    if g_ == 0:
                        tt = 4 * n_ + r_
                        return slice(tt * 128, (tt + 1) * 128)
                    return slice(512 * n_ + r_, 512 * (n_ + 1), 4)

                def outsl(r_):
                    if g_ == 0:
                        return slice(r_ * 128, (r_ + 1) * 128)
                    return slice(r_, 512, 4)

                def prevblk(n_, r_):
                    if g_ == 0:
                        tt = 4 * n_ + r_ - 1
                        if tt < 0:
                            return None
                        return (tt // 4, tt % 4)
                    if n_ == 0:
                        return None
                    return (n_ - 1, r_)
                return cols, outsl, prevblk

            groups = []
            for g_ in range(2):
                groups.append((g_, "cur"))
                _, _, prevblk = geom(g_)
                if any(prevblk(m, r_) is not None for r_ in range(4)):
                    groups.append((g_, "prev"))
            groups.append((2, "cur"))
            pend = None
            gi = 0

            def emit_pv(p):
                g_, kind, buf, rbuf, lo = p
                if g_ == 2:
                    for r_ in range(16):
                        mm(num[:, slice(r_, 512, 16)], V[2][:, r_, :], buf[:, r_ * 32:(r_ + 1) * 32], st(0), False,
                           [r_V[2][r_], rbuf], [r_num])
                        mm(den[:, slice(r_, 512, 16)], ones_b, buf[:, r_ * 32:(r_ + 1) * 32], st(1), False,
                           [r_cb, rbuf], [r_den])
                    return
                cols, outsl, prevblk = geom(g_)
                for r_ in range(4):
                    if kind == "cur":
                        vb_ = 4 * m + r_
                    else:
                        pb = prevblk(m, r_)
                        if pb is None:
                            continue
                        vb_ = 4 * pb[0] + pb[1]
                    mm(num[:, outsl(r_)], V[g_][:, vb_, :], buf[:, r_ * 128:(r_ + 1) * 128], st(0), False,
                       [r_V[g_][vb_], rbuf], [r_num])
                    mm(den[:, outsl(r_)], ones_b, buf[:, r_ * 128:(r_ + 1) * 128], st(1), False,
                       [r_cb, rbuf], [r_den])

            def emit_scores(pc, rpc, g_, kind):
                lo = 0
                if g_ == 2:
                    qTt, kTt = qk[0][2], qk[1][2]
                    for r_ in range(16):
                        mm(pc[:, r_ * 32:(r_ + 1) * 32], kTt[:, slice(r_, T, 16)],
                           qTt[:, slice(512 * m + r_, 512 * (m + 1), 16)], True, True,
                           [r_qk[0][2][m]] + r_qk[1][2], [rpc])
                    mask = cbs("m2", m * 512, (m + 1) * 512)
                else:
                    qTt, kTt = qk[0][g_], qk[1][g_]
                    cols, outsl, prevblk = geom(g_)
                    if kind == "cur":
                        for r_ in range(4):
                            mm(pc[:, r_ * 128:(r_ + 1) * 128], kTt[:, cols(m, r_)], qTt[:, cols(m, r_)], True, True,
                               [r_qk[0][g_][m], r_qk[1][g_][m]], [rpc])
                        mask = cbs("masku")
                    else:
                        prs = [(r_, prevblk(m, r_)) for r_ in range(4) if prevblk(m, r_) is not None]
                        lo = prs[0][0] * 128
                        for r_, (pn, pr_) in prs:
                            mm(pc[:, r_ * 128:(r_ + 1) * 128], kTt[:, cols(pn, pr_)], qTt[:, cols(m, r_)], True, True,
                               [r_qk[0][g_][m], r_qk[1][g_][pn], r_qk[1][g_][m]], [rpc])
                        mask = cbs("maskl")
                return pc, rpc, lo, mask

            pc0, rpc0 = yield from pget()
            nxt = emit_scores(pc0, rpc0, *groups[0])
            yield
            for gi, (g_, kind) in enumerate(groups):
                b = gi % 2
                pc, rpc, lo, mask = nxt
                fw.op("act", lambda e: e.activation(out=ee[s][b][:, lo:512], in_=pc[:, lo:512], func=AF.Exp,
                                                    scale=QSCALE), [rpc], [r_ee[s][b]])
                psp.rel(rpc)
                yield
                meng = "dve"
                fw.op(meng, lambda e: e.tensor_tensor(out=PP[s][b][:, lo:512], in0=ee[s][b][:, lo:512],
                                                      in1=mask[:, lo:512], op=ALU.mult),
                      [r_ee[s][b], r_cb], [r_PP[s][b]])
                if gi + 1 < len(groups):
                    pcn, rpcn = yield from pget()
                    nxt = emit_scores(pcn, rpcn, *groups[gi + 1])
                yield
                emit_pv((g_, kind, PP[s][b], r_PP[s][b], lo))
                yield
            tok = slice(m * 512, (m + 1) * 512)
            fw.op("act", lambda e: e.activation(out=rden[s][:], in_=den[:, 0:512], func=AF.Ln), [r_den], [r_rden[s]])
            psp.rel(r_den)
            yield
            fw.op("act", lambda e: e.activation(out=rden[s][:], in_=rden[s][:], func=AF.Exp, scale=-1.0),
                  [r_rden[s]], [r_rden[s]])
            yield
            fw.op("dve", lambda e: e.tensor_tensor(out=tn[s][:], in0=num[:, 0:512], in1=rden[s][:], op=ALU.mult),
                  [r_num, r_rden[s]], [r_tn[s]])
            psp.rel(r_num)
            yield
            fw.op("dve", lambda e: e.tensor_tensor(out=oT[:, 4 + h, tok], in0=tn[s][:], in1=zsall[:, tok], op=ALU.mult),
                  [r_tn[s], r_zsall[m]], [r_oT[4 + h][m]])

        for m in range(4):
            tok = slice(m * 512, (m + 1) * 512)
            pz, rpz = psp.get()
            for k in range(8):
                mm(pz[:, 0:512], wsw[:, k, 1152:1280], hT[:, k, tok], k == 0, k == 7, hres(m) + [wr["z"]], [rpz])
            fw.op("act", lambda e: e.activation(out=zsall[:, tok], in_=pz[:, 0:512], func=AF.Silu), [rpz], [r_zsall[m]])
            psp.rel(rpz)
        run_streams(list(range(4)), NA, attn_stream, stagger=5)
        if h < 3:
            load_head(h + 1, "z")


def phase_out(nc, fw, ph, sb, psget, mm, load_w, wres, cfs, cbs, hT, oT, r_hT, r_oT, r_cf, r_cb, hres,
              wg_d, wbd_d, wbs_d, wo_d, x_d, out_d, r_out):
    wbd = sb("wbd", [128, 4, 1024], BF16, ph)
    wbs = sb("wbs", [128, 4, 1024], BF16, ph)
    wpbd = load_w(wbd, wbd_d, 4, 1024)
    wpbs = load_w(wbs, wbs_d, 4, 1024)
    wg = sb("wg", [128, 8, 2048], BF16, ph)
    wo = sb("wo", [128, 8, 1024], BF16, ph)
    wgv = wg_d.rearrange("(k p) c -> p k c", p=128)
    wpg = []
    for c in range(0, 8, 2):
        for base in (0, 1024):
            c0, c1 = base + c * 128, base + (c + 2) * 128
            r_ = Res()
            fw.dma(wg[:, :, c0:c1], wgv[:, :, c0:c1], writes=[r_], q="pool")
            wpg.append((c0, c1, r_))
    wpo = load_w(wo, wo_d, 8, 1024)
    mT = sb("mT", [128, 8, 512], BF16, ph)
    r_mT = [Res() for _ in range(8)]
    sg = [sb("sg%d" % i, [128, 512], F32, ph) for i in range(2)]
    r_sg = [Res(), Res()]
    m1 = [sb("m1%d" % i, [128, 512], F32, ph) for i in range(2)]
    r_m1 = [Res(), Res()]
    xt = [sb("oxt%d" % i, [128, D], F32, ph) for i in range(2)]
    r_xt = [Res(), Res()]
    ot = [sb("oot%d" % i, [128, D], F32, ph) for i in range(2)]
    for n in range(4):
        tok = slice(n * 512, (n + 1) * 512)
        for c in range(8):
            cs_ = slice(c * 128, (c + 1) * 128)
            pyd, rpyd = psget()
            for hh in range(4):
                mm(pyd[:, 0:512], wbd[:, hh, cs_], oT[:, hh, tok], hh == 0, hh == 3,
                   [r_oT[hh][n]] + wres(wpbd, c * 128, (c + 1) * 128), [rpyd])
            pys, rpys = psget()
            for hh in range(4):
                mm(pys[:, 0:512], wbs[:, hh, cs_], oT[:, 4 + hh, tok], hh == 0, hh == 3,
                   [r_oT[4 + hh][n]] + wres(wpbs, c * 128, (c + 1) * 128), [rpys])
            pgd, rpgd = psget()
            for k in range(8):
                mm(pgd[:, 0:512], wg[:, k, cs_], hT[:, k, tok], k == 0, k == 7,
                   hres(n) + wres(wpg, c * 128, (c + 1) * 128), [rpgd])
            pgs, rpgs = psget()
            for k in range(8):
                mm(pgs[:, 0:512], wg[:, k, 1024 + c * 128:1024 + (c + 1) * 128], hT[:, k, tok], k == 0, k == 7,
                   hres(n) + wres(wpg, 1024 + c * 128, 1024 + (c + 1) * 128), [rpgs])
            fw.op("act", lambda e: e.activation(out=sg[0][:], in_=pgd[:, 0:512], func=AF.Sigmoid), [rpgd], [r_sg[0]])
            fw.op("act", lambda e: e.activation(out=sg[1][:], in_=pgs[:, 0:512], func=AF.Sigmoid), [rpgs], [r_sg[1]])
            fw.op("dve", lambda e: e.tensor_tensor(out=m1[0][:], in0=pyd[:, 0:512], in1=sg[0][:], op=ALU.mult),
                  [rpyd, r_sg[0]], [r_m1[0]])
            fw.op("dve", lambda e: e.tensor_tensor(out=m1[1][:], in0=pys[:, 0:512], in1=sg[1][:], op=ALU.mult),
                  [rpys, r_sg[1]], [r_m1[1]])
            fw.op("pool", lambda e: e.tensor_tensor(out=mT[:, c, :], in0=m1[0][:], in1=m1[1][:], op=ALU.add),
                  [r_m1[0], r_m1[1]], [r_mT[c]])
        for j in range(4):
            i = 4 * n + j
            s = i % 2
            fw.dma(xt[s][:], x_d[i * 128:(i + 1) * 128, :], writes=[r_xt[s]])
            for half in range(2):
                hs = slice(half * 512, (half + 1) * 512)
                po, rpo = psget()
                for k in range(8):
                    mm(po[:, 0:512], mT[:, k, j * 128:(j + 1) * 128], wo[:, k, hs], k == 0, k == 7,
                       [r_mT[k]] + wres(wpo, half * 512, (half + 1) * 512), [rpo])
                fw.op("dve", lambda e: e.tensor_tensor(out=ot[s][:, hs], in0=po[:, 0:512], in1=xt[s][:, hs],
                                                       op=ALU.add), [rpo, r_xt[s]], [r_out[s]])
            fw.dma(out_d[i * 128:(i + 1) * 128, :], ot[s][:], reads=[r_out[s]], semres=r_out[s])
    fw.final_wait(r_out)


_CACHE = {}


def kernel(**inputs):
    inp = {k: np.asarray(v) for k, v in inputs.items()}
    shared = _prep_shared(inp)
    if "nc" not in _CACHE:
        _CACHE["nc"] = build()
    nc = _CACHE["nc"]
    x = np.ascontiguousarray(inp["x"], dtype=np.float32)
    in_maps = []
    for b in range(8):
        m = dict(shared)
        m["x"] = x[b]
        in_maps.append(m)
    res = run_bass_kernel_spmd(nc, in_maps, core_ids=list(range(8)))
    out = np.stack([np.asarray(res.results[b]["out"]) for b in range(8)], axis=0)
    return out.astype(np.float32)
```

```python
import numpy as np
from contextlib import ExitStack
import concourse.bass as bass
import concourse.mybir as mybir
from concourse.bass_utils import run_bass_kernel_spmd

F32 = mybir.dt.float32
BF16 = mybir.dt.bfloat16
AF = mybir.ActivationFunctionType
ALU = mybir.AluOpType
AX = mybir.AxisListType

T = 2048
D = 1024
NT = 16
EPS = 1e-6
BIG = 30000.0
QSCALE = 128.0 ** -0.5


class Res:
    __slots__ = ("key", "w", "r", "dsem", "dcnt", "excl")

    def __init__(self, excl=False):
        self.excl = excl
        self.key = None
        self.w = None
        self.r = {}
        self.dsem = None
        self.dcnt = 0


class FW:
    def __init__(self, nc, es):
        self.nc = nc
        self.es = es
        self.eng = {"pe": nc.tensor, "act": nc.scalar, "dve": nc.vector, "pool": nc.gpsimd, "sp": nc.sync}
        self.sem = {}
        self.cnt = {}
        for k in self.eng:
            self.sem[k] = es.enter_context(nc.semaphore("s_" + k))
            self.cnt[k] = 0
        self.waited = {k: {} for k in self.eng}
        self.semobj = dict(self.sem)
        self.ndsem = 0
        self.ninst = 0

    def _wait(self, e, deps):
        best = {}
        for (k, v) in deps:
            if k == e and e == "pe":
                continue
            if best.get(k, 0) < v:
                best[k] = v
        for k, v in best.items():
            if self.waited[e].get(k, 0) >= v:
                continue
            self.eng[e].wait_ge(self.semobj[k], v)
            self.waited[e][k] = v

    @staticmethod
    def _deps(reads, writes, e=None):
        deps = []
        for r in reads:
            if r.w is not None:
                deps.append(r.w)
            if r.excl:
                deps.extend((k, v) for k, v in r.r.items() if k != e)
        for w in writes:
            if w.w is not None:
                deps.append(w.w)
            deps.extend(w.r.items())
        return deps

    def op(self, e, fn, reads=(), writes=()):
        self._wait(e, self._deps(reads, writes, e))
        inst = fn(self.eng[e])
        self.cnt[e] += 1
        c = self.cnt[e]
        inst.then_inc(self.sem[e], 1)
        for r in reads:
            if r.r.get(e, 0) < c:
                r.r[e] = c
        for w in writes:
            w.w = (e, c)
            w.r = {}
        self.ninst += 1
        return inst

    def dma(self, out, in_, reads=(), writes=(), q="sp", semres=None):
        self._wait(q, self._deps(reads, writes, q))
        sr = semres or (writes[0] if writes else reads[0])
        if sr.dsem is None:
            sr.key = "d_%d" % self.ndsem
            sr.dsem = self.es.enter_context(self.nc.semaphore(sr.key))
            self.ndsem += 1
            self.semobj[sr.key] = sr.dsem
        key = sr.key
        inst = self.eng[q].dma_start(out=out, in_=in_)
        sr.dcnt += 16
        inst.then_inc(sr.dsem, 16)
        for r in reads:
            if r.r.get(key, 0) < sr.dcnt:
                r.r[key] = sr.dcnt
        for w in writes:
            w.w = (key, sr.dcnt)
            w.r = {}
        self.ninst += 1
        return inst

    def barrier_all(self):
        for e in self.eng:
            deps = [(k, self.cnt[k]) for k in self.eng if self.cnt[k] > 0]
            best = {}
            for k, v in deps:
                best[k] = v
            for k, v in best.items():
                if self.waited[e].get(k, 0) >= v:
                    continue
                self.eng[e].wait_ge(self.semobj[k], v)
                self.waited[e][k] = v

    def final_wait(self, reslist, e="sp"):
        deps = []
        for r in reslist:
            if r.w is not None:
                deps.append(r.w)
            deps.extend(r.r.items())
        self._wait(e, deps)


class PSP:
    def __init__(self, tiles, res):
        self.t, self.r = tiles, res
        self.live = [False] * len(tiles)
        self.i = 0

    def get(self):
        n = len(self.t)
        for _ in range(n):
            i = self.i % n
            self.i += 1
            if not self.live[i]:
                self.live[i] = True
                return self.t[i], self.r[i]
        raise RuntimeError("no free PSUM bank")

    def try_get(self):
        n = len(self.t)
        for _ in range(n):
            i = self.i % n
            self.i += 1
            if not self.live[i]:
                self.live[i] = True
                return self.t[i], self.r[i]
        return None

    def rel(self, res):
        self.live[self.r.index(res)] = False


def streams_gen(items, W, make, stagger=0):
    it = iter(items)
    active = []
    free = list(range(W))
    pending = True
    rnd = 0
    next_start = 0
    while True:
        while free and pending and rnd >= next_start:
            try:
                a = next(it)
            except StopIteration:
                pending = False
                break
            s = free.pop(0)
            active.append([make(a, s), s])
            if stagger:
                next_start = rnd + stagger
                break
        if not active and not pending:
            break
        for ent in list(active):
            try:
                next(ent[0])
            except StopIteration:
                active.remove(ent)
                free.append(ent[1])
        rnd += 1
        yield


def run_streams(items, W, make, stagger=0):
    for _ in streams_gen(items, W, make, stagger):
        pass


CF = {}
_o = 0
for _n, _w in [("ident", 128), ("tri", 128), ("tris", 128), ("ch0", 128), ("ch1", 128), ("nw", 8), ("cw", 48),
               ("dtb", 4), ("alog", 4), ("dnw", 512), ("qw", 3), ("kw", 3), ("eps", 1)]:
    CF[_n] = (_o, _w)
    _o += _w
NCF = _o
CB = {}
_o = 0
for _n, _w in [("ident", 128), ("ones", 128), ("x1", 128), ("masku", 512), ("maskl", 512), ("m2", 2048), ("rm", 32)]:
    CB[_n] = (_o, _w)
    _o += _w
NCB = _o


def _const_arrays():
    p = np.arange(128)
    same = (p[:, None] // 64) == (p[None, :] // 64)
    cf = np.zeros((128, NCF), np.float32)

    def put(name, arr):
        o, w = CF[name]
        cf[:, o:o + w] = arr
    put("ident", np.eye(128, dtype=np.float32))
    put("tri", (same & (p[:, None] <= p[None, :])).astype(np.float32))
    put("tris", (same & (p[:, None] > p[None, :])).astype(np.float32))
    put("ch0", np.broadcast_to((p[:, None] < 64), (128, 128)).astype(np.float32))
    put("ch1", np.broadcast_to((p[:, None] >= 64), (128, 128)).astype(np.float32))
    put("eps", np.full((128, 1), EPS, np.float32))
    cb = np.zeros((128, NCB), np.float32)

    def putb(name, arr):
        o, w = CB[name]
        cb[:, o:o + w] = arr
    putb("ident", np.eye(128, dtype=np.float32))
    putb("ones", np.ones((128, 128), np.float32))
    putb("x1", np.where(same & (p[:, None] > p[None, :]), 0.0, BIG).astype(np.float32))
    mu = (p[:, None] <= p[None, :]).astype(np.float32)
    ml = (p[:, None] >= p[None, :]).astype(np.float32)
    putb("masku", np.tile(mu, (1, 4)))
    putb("maskl", np.tile(ml, (1, 4)))
    m2 = np.zeros((128, 4, 16, 32), np.float32)
    for m in range(4):
        q = 32 * m + np.arange(32)
        m2[:, m, :, :] = (p[:, None] <= q[None, :]).astype(np.float32)[:, None, :]
    putb("m2", m2.reshape(128, 2048))
    rm = np.zeros((128, 32), np.float32)
    for m in range(16):
        rm[16 + m, m] = -1.0
    for m in range(16, 32):
        rm[m - 16, m] = 1.0
    putb("rm", rm)
    pos = np.arange(T, dtype=np.float32)
    inv_freq = (np.float32(500000.0) ** (-np.arange(0, 32, 2, dtype=np.float32) / np.float32(32))).astype(np.float32)
    ang = (pos[:, None] * inv_freq[None, :]).astype(np.float32)
    cos = np.cos(ang).astype(np.float32).T
    sin = np.sin(ang).astype(np.float32).T
    rope = np.zeros((32, 2, T), np.float32)
    rope[0:16, 0] = cos
    rope[16:32, 0] = cos
    rope[0:16, 1] = sin
    rope[16:32, 1] = sin
    return cf, cb, rope


def _prep_shared(inp):
    cf, cb, rope = _const_arrays()
    w_in = inp["w_in"][0]

    def put(name, arr):
        o, w = CF[name]
        cf[:, o:o + w] = arr
    put("nw", inp["norm_w"][0].reshape(8, 128).T)
    cw = inp["conv_w"][0, :, 0, :]
    put("cw", cw.T.reshape(12, 128, 4).transpose(1, 0, 2).reshape(128, 48))
    put("dtb", np.broadcast_to(inp["dn_dt_bias"][0][None, :], (128, 4)))
    put("alog", np.broadcast_to(inp["dn_a_log"][0][None, :], (128, 4)))
    put("dnw", np.broadcast_to(np.tile(inp["dn_norm_w"][0], 4)[None, :], (128, 512)))
    put("qw", inp["q_norm_w"][0].T)
    put("kw", inp["k_norm_w"][0].T)
    w_dn = np.ascontiguousarray(w_in[:, 0:2056])
    base = 2056
    w_swa = np.empty((4, 1024, 1280), np.float32)
    for h in range(4):
        cols = []
        for typ in range(3):
            for g in range(3):
                c0 = base + typ * 1536 + g * 512 + h * 128
                cols.append(w_in[:, c0:c0 + 128])
        c0 = base + 3 * 1536 + h * 128
        cols.append(w_in[:, c0:c0 + 128])
        w_swa[h] = np.concatenate(cols, axis=1)
    w_g = np.ascontiguousarray(w_in[:, base + 3 * 1536 + 512:])
    assert w_g.shape[1] == 2048
    return {"cf": cf, "cb": cb, "rope": rope, "w_dn": w_dn, "w_swa": w_swa, "w_g": w_g,
            "w_bd": np.ascontiguousarray(inp["w_branch_dn"][0]), "w_bs": np.ascontiguousarray(inp["w_branch_swa"][0]),
            "w_o": np.ascontiguousarray(inp["w_out"][0])}


def build(stage=99, debug=False):
    nc = bass.Bass("TRN2", target_bir_lowering=False)
    x_d = nc.dram_tensor("x", [T, D], F32, kind="ExternalInput").ap()
    cf_d = nc.dram_tensor("cf", [128, NCF], F32, kind="ExternalInput").ap()
    cb_d = nc.dram_tensor("cb", [128, NCB], F32, kind="ExternalInput").ap()
    rope_d = nc.dram_tensor("rope", [32, 2, T], F32, kind="ExternalInput").ap()
    wdn_d = nc.dram_tensor("w_dn", [D, 2056], F32, kind="ExternalInput").ap()
    wswa_d = nc.dram_tensor("w_swa", [4, D, 1280], F32, kind="ExternalInput").ap()
    wg_d = nc.dram_tensor("w_g", [D, 2048], F32, kind="ExternalInput").ap()
    wbd_d = nc.dram_tensor("w_bd", [512, D], F32, kind="ExternalInput").ap()
    wbs_d = nc.dram_tensor("w_bs", [512, D], F32, kind="ExternalInput").ap()
    wo_d = nc.dram_tensor("w_o", [D, D], F32, kind="ExternalInput").ap()
    out_d = nc.dram_tensor("out", [T, D], F32, kind="ExternalOutput").ap()
    dbg_d = None
    if debug:
        dbg_d = nc.dram_tensor("dbg", [128, 8, T], BF16, kind="ExternalOutput").ap()

    with ExitStack() as es:
        fw = FW(nc, es)

        def sb(name, shape, dt, ctx=es):
            return ctx.enter_context(nc.sbuf_tensor("sb_" + name, shape, dt))

        hT = sb("hT", [128, 8, T], BF16)
        class _OT:
            d = None
            s_ = None

            def __getitem__(self, key):
                p, hsel, cols = key
                if isinstance(hsel, int):
                    return self.d[p, hsel, cols] if hsel < 4 else self.s_[p, hsel - 4, cols]
                assert hsel == slice(0, 4)
                return self.d[p, 0:4, cols]
        oT = _OT()
        oT.d = sb("oTd", [128, 4, T], BF16)
        cf = sb("cf", [128, NCF], F32)
        cb = sb("cb", [128, NCB], BF16)
        r_hT = [Res() for _ in range(NT)]
        r_oT = [[Res() for _ in range(4)] for _ in range(8)]
        r_cf, r_cb = Res(), Res()
        pst = [es.enter_context(nc.psum_tensor("ps%d" % i, [128, 512], F32)) for i in range(8)]
        r_ps = [Res(excl=True) for _ in range(8)]
        rot = {"list": list(range(8)), "i": 0}

        def psget():
            lst = rot["list"]
            i = lst[rot["i"] % len(lst)]
            rot["i"] += 1
            return pst[i], r_ps[i]

        def cfs(name, a=0, b=None):
            o, w = CF[name]
            if b is None:
                b = w
            return cf[:, o + a:o + b]

        def cbs(name, a=0, b=None):
            o, w = CB[name]
            if b is None:
                b = w
            return cb[:, o + a:o + b]

        def mm(out, lhsT, rhs, start, stop, reads, writes):
            return fw.op("pe", lambda e: e.matmul(out, lhsT, rhs, start=start, stop=stop, skip_group_check=True),
                         reads, writes)

        def load_w(dst, src, K, C):
            srcv = src.rearrange("(k p) c -> p k c", p=128)
            pieces = []
            for c0 in range(0, C, 512):
                c1 = min(C, c0 + 512)
                r = Res()
                fw.dma(dst[:, 0:K, c0:c1], srcv[:, :, c0:c1], writes=[r], q="pool")
                pieces.append((c0, c1, r))
            return pieces

        def wres(pieces, a, b):
            return [r for (c0, c1, r) in pieces if c0 < b and c1 > a]

        fw.dma(cf[:], cf_d[:, :], writes=[r_cf])
        with ExitStack() as ph:
            cbst = sb("cbst", [128, NCB], F32, ph)
            r_cbst = Res()
            fw.dma(cbst[:], cb_d[:, :], writes=[r_cbst])
            fw.op("dve", lambda e: e.tensor_copy(out=cb[:], in_=cbst[:]), [r_cbst], [r_cb])
            fw.barrier_all()
        ident_f = cfs("ident")
        ident_b = cbs("ident")
        ones_b = cbs("ones")
        eps_c = cfs("eps")

        ph_dn = es.enter_context(ExitStack())
        wdn_pre = sb("wdn", [128, 8, 2056], BF16, ph_dn)
        wdv_ = wdn_d.rearrange("(k p) c -> p k c", p=128)
        wp_pre = []
        for c0 in range(0, 2056, 512):
            c1 = min(2056, c0 + 512)
            r_ = Res()
            fw.dma(wdn_pre[:, :, c0:c1], wdv_[:, :, c0:c1], writes=[r_], q="pool")
            wp_pre.append((c0, c1, r_))

        with ExitStack() as ph:
            xt = [sb("xt%d" % i, [128, D], F32, ph) for i in range(6)]
            xs = [sb("xs%d" % i, [128, D], BF16, ph) for i in range(6)]
            junk = sb("junk", [128, D], F32, ph)
            ss = sb("ss", [128, NT], F32, ph)
            rt = sb("rt", [128, NT], F32, ph)
            rstd = sb("rstd", [128, NT], F32, ph)
            r_xt = [Res() for _ in range(6)]
            r_xs = [Res() for _ in range(6)]
            r_junk = Res()
            r_st = [Res() for _ in range(NT)]
            def p0_stream(i, s):
                fw.dma(xt[s][:], x_d[i * 128:(i + 1) * 128, :], writes=[r_xt[s]])
                yield
                fw.op("act", lambda e: e.activation(out=junk[:], in_=xt[s][:], func=AF.Square,
                                                    accum_out=ss[:, i:i + 1]), [r_xt[s]], [r_junk, r_st[i]])
                yield
                fw.op("act", lambda e: e.activation(out=rt[:, i:i + 1], in_=ss[:, i:i + 1], func=AF.Sqrt,
                                                    bias=eps_c, scale=1.0 / D), [r_st[i], r_cf], [r_st[i]])
                yield
                fw.op("dve", lambda e: e.reciprocal(out=rstd[:, i:i + 1], in_=rt[:, i:i + 1]), [r_st[i]], [r_st[i]])
                yield
                fw.op("dve", lambda e: e.tensor_scalar(out=xs[s][:], in0=xt[s][:], scalar1=rstd[:, i:i + 1],
                                                       scalar2=None, op0=ALU.mult), [r_xt[s], r_st[i]], [r_xs[s]])
                yield
                pt, rp = psget()
                pb = pt[:].bitcast(BF16)
                for k in range(8):
                    fw.op("pe", lambda e: e.transpose(out=pb[:, k * 128:(k + 1) * 128],
                                                      in_=xs[s][:, k * 128:(k + 1) * 128], identity=ident_b),
                          [r_xs[s], r_cb], [rp])
                yield
                for k in range(8):
                    dst = hT[:, k, i * 128:(i + 1) * 128]
                    src = pb[:, k * 128:(k + 1) * 128]
                    if k % 2 == 0:
                        fw.op("act", lambda e: e.activation(out=dst, in_=src, func=AF.Copy,
                                                            scale=cfs("nw", k, k + 1)), [rp, r_cf], [r_hT[i]])
                    else:
                        fw.op("dve", lambda e: e.tensor_scalar(out=dst, in0=src, scalar1=cfs("nw", k, k + 1),
                                                               scalar2=None, op0=ALU.mult), [rp, r_cf], [r_hT[i]])
                    if k % 2 == 1:
                        yield

            run_streams(list(range(NT)), 6, p0_stream, stagger=2)
            fw.barrier_all()

        def hres(n):
            return r_hT[4 * n:4 * n + 4]

        if stage >= 1:
            phase_dn(nc, fw, ph_dn, sb, psget, mm, load_w, wres, cfs, cbs, hT, oT, r_hT, r_oT, r_cf, r_cb, hres,
                     wdn_d, pst, r_ps, wdn_pre, wp_pre)
            fw.barrier_all()
        ph_dn.close()
        oT.s_ = sb("oTs", [128, 4, T], BF16)

        if stage >= 2:
            with ExitStack() as ph:
                phase_swa(nc, fw, ph, sb, psget, mm, load_w, wres, cfs, cbs, hT, oT, r_hT, r_oT, r_cf, r_cb, hres,
                          wswa_d, rope_d, rot, pst, r_ps)
                fw.barrier_all()
        r_out = [Res(), Res()]
        if stage >= 3:
            with ExitStack() as ph:
                phase_out(nc, fw, ph, sb, psget, mm, load_w, wres, cfs, cbs, hT, oT, r_hT, r_oT, r_cf, r_cb, hres,
                          wg_d, wbd_d, wbs_d, wo_d, x_d, out_d, r_out)
        else:
            with ExitStack() as ph:
                zt = sb("zt", [128, D], F32, ph)
                fw.op("dve", lambda e: e.memset(zt[:], 0.0), [], [r_out[0]])
                for i in range(NT):
                    fw.dma(out_d[i * 128:(i + 1) * 128, :], zt[:], reads=[r_out[0]], semres=r_out[0])
                fw.final_wait(r_out)
        if debug:
            r_dbg = Res()
            allr = [r for hh in r_oT for r in hh]
            for hh in range(8):
                fw.dma(dbg_d[:, hh, :], (hT if stage == 0 else oT)[:, hh, :], reads=allr + r_hT, semres=r_dbg)
            fw.final_wait([r_dbg] + allr)
        fw.final_wait(r_out)
        fw.barrier_all()
    return nc


def phase_dn(nc, fw, ph, sb, psget, mm, load_w, wres, cfs, cbs, hT, oT, r_hT, r_oT, r_cf, r_cb, hres, wdn_d,
             pst, r_ps, wdn, wp):
    ident_f = cfs("ident")
    ident_b = cbs("ident")
    ones_b = cbs("ones")
    eps_c = cfs("eps")
    psp = PSP(pst, r_ps)

    def pget():
        n = 0
        while True:
            p = psp.try_get()
            if p is not None:
                return p
            n += 1
            if n > 10000:
                raise RuntimeError("PSUM starvation")
            yield


    ba = sb("ba", [128, NT, 8], F32, ph)
    beta = sb("beta", [128, NT, 4], F32, ph)
    nbeta = sb("nbeta", [128, NT, 4], F32, ph)
    g = sb("g", [128, NT, 4], F32, ph)
    t0 = sb("a0t0", [128, NT, 4], F32, ph)
    t1 = sb("a0t1", [128, NT, 4], F32, ph)
    t2 = sb("a0t2", [128, NT, 4], F32, ph)
    negA = sb("negA", [128, 4], F32, ph)
    gc = sb("gc", [128, NT, 4], F32, ph)
    egc = sb("egc", [128, NT, 4], F32, ph)
    sckbg = sb("sckbg", [128, NT, 4], F32, ph)
    sckd = sb("sckd", [128, NT, 4], F32, ph)
    egl = sb("egl", [128, 2, NT, 4], F32, ph)
    r_a0 = Res()
    pba, rpba = psp.get()
    for i in range(NT):
        for k in range(8):
            mm(pba[:, i * 8:(i + 1) * 8], hT[:, k, i * 128:(i + 1) * 128], wdn[:, k, 2048:2056], k == 0, k == 7,
               [r_hT[i]] + wres(wp, 2048, 2056), [rpba])
    bav = ba[:].rearrange("p t c -> p (t c)")
    fw.op("act", lambda e: e.activation(out=bav, in_=pba[:, 0:128], func=AF.Copy), [rpba], [r_a0])
    psp.rel(rpba)
    fw.op("act", lambda e: e.activation(out=beta[:], in_=ba[:, :, 0:4], func=AF.Sigmoid), [r_a0], [r_a0])
    fw.op("dve", lambda e: e.tensor_scalar(out=nbeta[:], in0=beta[:], scalar1=-1.0, scalar2=None, op0=ALU.mult),
          [r_a0], [r_a0])
    dtb_b = cfs("dtb").unsqueeze(1).to_broadcast([128, NT, 4])
    fw.op("dve", lambda e: e.tensor_tensor(out=t0[:], in0=ba[:, :, 4:8], in1=dtb_b, op=ALU.add), [r_a0, r_cf], [r_a0])
    fw.op("dve", lambda e: e.tensor_scalar(out=t2[:], in0=t0[:], scalar1=-1.0, scalar2=None, op0=ALU.mult),
          [r_a0], [r_a0])
    fw.op("dve", lambda e: e.tensor_tensor(out=t1[:], in0=t0[:], in1=t2[:], op=ALU.max), [r_a0], [r_a0])
    fw.op("act", lambda e: e.activation(out=t1[:], in_=t1[:], func=AF.Exp, scale=-1.0), [r_a0], [r_a0])
    fw.op("act", lambda e: e.activation(out=t1[:], in_=t1[:], func=AF.Ln, bias=1.0), [r_a0], [r_a0])
    fw.op("dve", lambda e: e.tensor_scalar(out=t2[:], in0=t0[:], scalar1=0.0, scalar2=None, op0=ALU.max),
          [r_a0], [r_a0])
    fw.op("dve", lambda e: e.tensor_tensor(out=t2[:], in0=t2[:], in1=t1[:], op=ALU.add), [r_a0], [r_a0])
    fw.op("act", lambda e: e.activation(out=negA[:], in_=cfs("alog"), func=AF.Exp), [r_cf], [r_a0])
    fw.op("dve", lambda e: e.tensor_scalar(out=negA[:], in0=negA[:], scalar1=-1.0, scalar2=None, op0=ALU.mult),
          [r_a0], [r_a0])
    negA_b = negA[:].unsqueeze(1).to_broadcast([128, NT, 4])
    fw.op("dve", lambda e: e.tensor_tensor(out=g[:], in0=t2[:], in1=negA_b, op=ALU.mult), [r_a0], [r_a0])
    gv = g[:].rearrange("p t c -> p (t c)")
    pg1, rpg1 = psp.get()
    mm(pg1[:, 0:64], cfs("tri"), gv, True, True, [r_a0, r_cf], [rpg1])
    mm(pg1[:, 64:128], cfs("tris"), gv, True, True, [r_a0, r_cf], [rpg1])
    mm(pg1[:, 128:192], cfs("ch0"), gv, True, True, [r_a0, r_cf], [rpg1])
    mm(pg1[:, 192:256], cfs("ch1"), gv, True, True, [r_a0, r_cf], [rpg1])
    fw.op("act", lambda e: e.activation(out=gc[:].rearrange("p t c -> p (t c)"), in_=pg1[:, 0:64], func=AF.Copy),
          [rpg1], [r_a0])
    fw.op("act", lambda e: e.activation(out=egc[:].rearrange("p t c -> p (t c)"), in_=pg1[:, 0:64], func=AF.Exp),
          [rpg1], [r_a0])
    fw.op("act", lambda e: e.activation(out=sckd[:].rearrange("p t c -> p (t c)"), in_=pg1[:, 64:128], func=AF.Exp),
          [rpg1], [r_a0])
    fw.op("act", lambda e: e.activation(out=egl[:].rearrange("p a t c -> p (a t c)"), in_=pg1[:, 128:256],
                                        func=AF.Exp), [rpg1], [r_a0])
    psp.rel(rpg1)
    fw.op("dve", lambda e: e.tensor_tensor(out=sckbg[:], in0=beta[:], in1=egc[:], op=ALU.mult), [r_a0], [r_a0])

    NS = 2
    halo = sb("halo", [128, 12, 3], F32, ph)
    r_halo = [Res() for _ in range(12)]
    raw = [sb("raw%d" % i, [128, 515], F32, ph) for i in range(NS)]
    cv = [sb("cv%d" % i, [128, 512], F32, ph) for i in range(NS)]
    cs = [sb("cs%d" % i, [128, 512], F32, ph) for i in range(NS)]
    sq = [sb("sq%d" % i, [128, 512], BF16, ph) for i in range(NS)]
    rn = [sb("rn%d" % i, [128, 512], F32, ph) for i in range(NS)]
    r_raw = [Res() for _ in range(NS)]
    r_cv = [Res() for _ in range(NS)]
    r_cs = [Res() for _ in range(NS)]
    r_sq = [Res() for _ in range(NS)]
    r_rn = [Res() for _ in range(NS)]
    qT = [sb("qT%d" % i, [128, 4, 512], BF16, ph) for i in range(2)]
    kT = [sb("kT%d" % i, [128, 4, 512], BF16, ph) for i in range(2)]
    kbg = [sb("kbg%d" % i, [128, 4, 4, 128], BF16, ph) for i in range(2)]
    kd = [sb("kd%d" % i, [128, 4, 4, 128], BF16, ph) for i in range(2)]
    vb = [sb("vb%d" % i, [128, 4, 4, 128], BF16, ph) for i in range(2)]
    r_qT = [[Res() for _ in range(4)] for _ in range(2)]
    r_kT = [[Res() for _ in range(4)] for _ in range(2)]
    r_kbg = [[Res() for _ in range(4)] for _ in range(2)]
    r_kd = [[Res() for _ in range(4)] for _ in range(2)]
    r_vb = [[Res() for _ in range(4)] for _ in range(2)]
    NP = 2
    arena = [sb("prearena%d" % i, [128, 2816], F32, ph) for i in range(NP)]
    e1m = [arena[i][:, 0:512] for i in range(NP)]
    tg = [arena[i][:, 512:1024] for i in range(NP)]
    abf = [arena[i][:, 1024:2816].bitcast(BF16) for i in range(NP)]
    Mb = [[abf[i][:, 0:512], abf[i][:, 512:1024]] for i in range(NP)]
    MTb = [[abf[i][:, 1024:1536], abf[i][:, 1536:2048]] for i in range(NP)]
    IMb = [abf[i][:, 2048:2560] for i in range(NP)]
    PTb = [[abf[i][:, 2560:3072], abf[i][:, 3072:3584]] for i in range(NP)]
    for i in range(NP):
        raw.append(arena[i][:, 0:515])
        cv.append(arena[i][:, 516:1028])
        cs.append(arena[i][:, 1028:1540])
        rn.append(arena[i][:, 1540:2052])
        sq.append(arena[i][:, 2052:2308].bitcast(BF16))
        for lst in (r_raw, r_cv, r_cs, r_sq, r_rn):
            lst.append(Res())
    HS = 4
    AT = [sb("AT%d" % i, [128, 512], BF16, ph) for i in range(HS)]
    u32 = [sb("u32%d" % i, [128, 512], F32, ph) for i in range(HS)]
    wT = [sb("wT%d" % i, [128, 512], BF16, ph) for i in range(HS)]
    r_e1m = [Res() for _ in range(NP)]
    r_tg = [Res() for _ in range(NP)]
    r_M = [[Res(), Res()] for _ in range(NP)]
    r_MT = [[Res(), Res()] for _ in range(NP)]
    r_IM = [Res() for _ in range(NP)]
    r_PT = [[Res(), Res()] for _ in range(NP)]
    r_AT = [Res() for _ in range(HS)]
    r_u32 = [Res() for _ in range(HS)]
    r_wT = [Res() for _ in range(HS)]
    vn = sb("vn", [128, 512], BF16, ph)
    tq = sb("tq", [128, 512], F32, ph)
    o32b = [sb("o32%d" % i, [128, 512], F32, ph) for i in range(2)]
    r_o32b = [Res(), Res()]
    osq = sb("osq", [128, 512], F32, ph)
    oss = sb("oss", [128, 4], F32, ph)
    on = sb("on", [128, 512], BF16, ph)
    zs = sb("zs", [128, 512], F32, ph)
    zsw = zs
    S32 = sb("S32", [128, 512], F32, ph)
    Sb = sb("Sb", [128, 512], BF16, ph)
    r_vn, r_tq, r_osq, r_oss, r_on, r_zs, r_zsw, r_S32, r_Sb = [Res() for _ in range(9)]
    fw.op("dve", lambda e: e.memset(S32[:], 0.0), [], [r_S32])
    fw.op("dve", lambda e: e.memset(Sb[:], 0.0), [], [r_Sb])

    def v4(ap):
        return ap.rearrange("p (h d) -> p h d", h=4)

    idf_b = ident_f.unsqueeze(1).to_broadcast([128, 4, 128])
    idb_b = ident_b.unsqueeze(1).to_broadcast([128, 4, 128])

    def sec_stream(arg, s):
        n, h, typ = arg
        par = n % 2
        tok = slice(n * 512, (n + 1) * 512)
        sec = typ * 4 + h
        col0 = typ * 512 + h * 128
        pr, rpr = yield from pget()
        for k in range(8):
            mm(pr[:, 0:512], wdn[:, k, col0:col0 + 128], hT[:, k, tok], k == 0, k == 7,
               hres(n) + wres(wp, col0, col0 + 128), [rpr])
        if n == 0:
            fw.op("pool", lambda e: e.memset(raw[s][:, 0:3], 0.0), [], [r_raw[s]])
        else:
            fw.op("pool", lambda e: e.tensor_copy(out=raw[s][:, 0:3], in_=halo[:, sec, :]),
                  [r_halo[sec]], [r_raw[s]])
        yield
        fw.op("act", lambda e: e.activation(out=raw[s][:, 3:515], in_=pr[:, 0:512], func=AF.Copy),
              [rpr], [r_raw[s]])
        psp.rel(rpr)
        yield
        fw.op("pool", lambda e: e.tensor_copy(out=halo[:, sec, :], in_=raw[s][:, 512:515]),
              [r_raw[s]], [r_halo[sec]])
        cwo = sec * 4
        fw.op("dve", lambda e: e.tensor_scalar(out=cv[s][:], in0=raw[s][:, 0:512], scalar1=cfs("cw", cwo, cwo + 1),
                                               scalar2=None, op0=ALU.mult), [r_raw[s], r_cf], [r_cv[s]])
        yield
        for j in range(1, 4):
            fw.op("dve", lambda e: e.scalar_tensor_tensor(out=cv[s][:], in0=raw[s][:, j:j + 512],
                                                          scalar=cfs("cw", cwo + j, cwo + j + 1),
                                                          in1=cv[s][:], op0=ALU.mult, op1=ALU.add),
                  [r_raw[s], r_cf, r_cv[s]], [r_cv[s]])
            yield
        fw.op("act", lambda e: e.activation(out=cs[s][:], in_=cv[s][:], func=AF.Silu), [r_cv[s]], [r_cs[s]])
        yield
        if typ < 2:
            fw.op("act", lambda e: e.activation(out=sq[s][:], in_=cs[s][:], func=AF.Square), [r_cs[s]], [r_sq[s]])
            yield
            pss, rpss = yield from pget()
            mm(pss[:, 0:512], ones_b, sq[s][:], True, True, [r_sq[s], r_cb], [rpss])
            yield
            fw.op("act", lambda e: e.activation(out=rn[s][:], in_=pss[:, 0:512], func=AF.Ln, bias=eps_c,
                                                scale=1.0), [rpss, r_cf], [r_rn[s]])
            psp.rel(rpss)
            yield
            fw.op("act", lambda e: e.activation(out=rn[s][:], in_=rn[s][:], func=AF.Exp, scale=-0.5),
                  [r_rn[s]], [r_rn[s]])
            yield
        if typ == 0:
            fw.op("dve", lambda e: e.scalar_tensor_tensor(out=qT[par][:, h, :], in0=cs[s][:], scalar=QSCALE,
                                                          in1=rn[s][:], op0=ALU.mult, op1=ALU.mult),
                  [r_cs[s], r_rn[s]], [r_qT[par][h]])
            return
        if typ == 1:
            fw.op("dve", lambda e: e.tensor_tensor(out=cv[s][:], in0=cs[s][:], in1=rn[s][:], op=ALU.mult),
                  [r_cs[s], r_rn[s]], [r_cv[s]])
            yield
            fw.op("act", lambda e: e.activation(out=kT[par][:, h, :], in_=cv[s][:], func=AF.Copy),
                  [r_cv[s]], [r_kT[par][h]])
            src, rsrc = cv[s], r_cv[s]
        else:
            src, rsrc = cs[s], r_cs[s]
        ptr, rptr = yield from pget()
        for j in range(4):
            fw.op("pe", lambda e: e.transpose(out=ptr[:, j * 128:(j + 1) * 128],
                                              in_=src[:, j * 128:(j + 1) * 128], identity=ident_f),
                  [rsrc, r_cf], [rptr])
        yield
        pv = ptr[:, 0:512].rearrange("p (j d) -> p j d", j=4)
        if typ == 1:
            sc1 = sckbg[:, 4 * n:4 * n + 4, h:h + 1].to_broadcast([128, 4, 128])
            sc2 = sckd[:, 4 * n:4 * n + 4, h:h + 1].to_broadcast([128, 4, 128])
            fw.op("dve", lambda e: e.tensor_tensor(out=kbg[par][:, :, h, :], in0=pv, in1=sc1, op=ALU.mult),
                  [rptr, r_a0], [r_kbg[par][h]])
            yield
            fw.op("dve", lambda e: e.tensor_tensor(out=kd[par][:, :, h, :], in0=pv, in1=sc2, op=ALU.mult),
                  [rptr, r_a0], [r_kd[par][h]])
        else:
            sc3 = beta[:, 4 * n:4 * n + 4, h:h + 1].to_broadcast([128, 4, 128])
            fw.op("dve", lambda e: e.tensor_tensor(out=vb[par][:, :, h, :], in0=pv, in1=sc3, op=ALU.mult),
                  [rptr, r_a0], [r_vb[par][h]])
        psp.rel(rptr)

    def pre_stream(i, s):
        n, j = i // 4, i % 4
        par = n % 2
        hs_ = i % HS
        tk = slice(j * 128, (j + 1) * 128)
        pB, rpB = yield from pget()
        for h in range(4):
            gb = g[:, i, h:h + 1].to_broadcast([128, 128])
            mm(pB[:, h * 128:(h + 1) * 128], gb, cfs("tri"), True, False, [r_a0, r_cf], [rpB])
            mm(pB[:, h * 128:(h + 1) * 128], ident_b, cbs("x1"), False, True, [r_cb], [rpB])
        yield
        for h in range(4):
            fw.op("act", lambda e: e.activation(out=e1m[s][:, h * 128:(h + 1) * 128],
                                                in_=pB[:, h * 128:(h + 1) * 128], func=AF.Exp,
                                                bias=gc[:, i, h:h + 1], scale=-1.0), [rpB, r_a0], [r_e1m[s]])
        psp.rel(rpB)
        pG, rpG = yield from pget()
        for h in range(4):
            mm(pG[:, h * 128:(h + 1) * 128], kT[par][:, h, tk], kT[par][:, h, tk], True, True,
               [r_kT[par][h]], [rpG])
        yield
        nb_b = nbeta[:, i, :].unsqueeze(2).to_broadcast([128, 4, 128])
        fw.op("dve", lambda e: e.tensor_tensor(out=v4(tg[s][:]), in0=v4(pG[:, 0:512]), in1=nb_b, op=ALU.mult),
              [rpG, r_a0], [r_tg[s]])
        psp.rel(rpG)
        yield
        fw.op("dve", lambda e: e.tensor_tensor(out=Mb[s][0][:], in0=tg[s][:], in1=e1m[s][:], op=ALU.mult),
              [r_tg[s], r_e1m[s]], [r_M[s][0]])
        pE, rpE = yield from pget()
        for h in range(4):
            fw.op("pe", lambda e: e.transpose(out=pE[:, h * 128:(h + 1) * 128],
                                              in_=e1m[s][:, h * 128:(h + 1) * 128], identity=ident_f),
                  [r_e1m[s], r_cf], [rpE])
        yield
        pT, rpT = yield from pget()
        pTb = pT[:].bitcast(BF16)
        for h in range(4):
            fw.op("pe", lambda e: e.transpose(out=pTb[:, h * 128:(h + 1) * 128],
                                              in_=Mb[s][0][:, h * 128:(h + 1) * 128], identity=ident_b),
                  [r_M[s][0], r_cb], [rpT])
        fw.op("dve", lambda e: e.tensor_tensor(out=v4(tg[s][:]), in0=v4(pE[:, 0:512]), in1=idf_b, op=ALU.add),
              [rpE, r_cf, r_tg[s]], [r_tg[s]])
        psp.rel(rpE)
        yield
        fw.op("act", lambda e: e.activation(out=MTb[s][0][:], in_=pTb[:, 0:512], func=AF.Copy), [rpT], [r_MT[s][0]])
        yield
        fw.op("dve", lambda e: e.tensor_tensor(out=v4(PTb[s][0][:]), in0=v4(pTb[:, 0:512]), in1=idb_b, op=ALU.add),
              [rpT, r_cb], [r_PT[s][0]])
        psp.rel(rpT)
        pQ, rpQ = yield from pget()
        for h in range(4):
            mm(pQ[:, h * 128:(h + 1) * 128], kT[par][:, h, tk], qT[par][:, h, tk], True, True,
               [r_kT[par][h], r_qT[par][h]], [rpQ])
        yield
        fw.op("dve", lambda e: e.tensor_tensor(out=AT[hs_][:], in0=pQ[:, 0:512], in1=tg[s][:], op=ALU.mult),
              [rpQ, r_tg[s]], [r_AT[hs_]])
        psp.rel(rpQ)
        cur = 0
        for lev in range(1, 6):
            nxt = 1 - cur
            pM, rpM = yield from pget()
            for h in range(4):
                hs = slice(h * 128, (h + 1) * 128)
                mm(pM[:, hs], MTb[s][cur][:, hs], Mb[s][cur][:, hs], True, True, [r_MT[s][cur], r_M[s][cur]], [rpM])
            if lev < 5:
                pMT, rpMT = yield from pget()
                for h in range(4):
                    hs = slice(h * 128, (h + 1) * 128)
                    mm(pMT[:, hs], Mb[s][cur][:, hs], MTb[s][cur][:, hs], True, True,
                       [r_MT[s][cur], r_M[s][cur]], [rpMT])
            yield
            fw.op("dve", lambda e: e.tensor_tensor(out=v4(IMb[s][:]), in0=v4(pM[:, 0:512]), in1=idf_b, op=ALU.add),
                  [rpM, r_cf], [r_IM[s]])
            if lev < 5:
                fw.op("act", lambda e: e.activation(out=MTb[s][nxt][:], in_=pMT[:, 0:512], func=AF.Copy),
                      [rpMT], [r_MT[s][nxt]])
                psp.rel(rpMT)
            yield
            if lev < 5:
                fw.op("act", lambda e: e.activation(out=Mb[s][nxt][:], in_=pM[:, 0:512], func=AF.Copy),
                      [rpM], [r_M[s][nxt]])
            psp.rel(rpM)
            pP, rpP = yield from pget()
            for h in range(4):
                hs = slice(h * 128, (h + 1) * 128)
                mm(pP[:, hs], IMb[s][:, hs], PTb[s][cur][:, hs], True, True, [r_IM[s], r_PT[s][cur]], [rpP])
            yield
            fw.op("act" if lev % 2 else "dve",
                  (lambda e: e.activation(out=PTb[s][nxt][:], in_=pP[:, 0:512], func=AF.Copy)) if lev % 2 else
                  (lambda e: e.tensor_copy(out=PTb[s][nxt][:], in_=pP[:, 0:512])), [rpP], [r_PT[s][nxt]])
            psp.rel(rpP)
            cur = nxt
            yield
        PTf, r_PTf = PTb[s][cur], r_PT[s][cur]
        pU, rpU = yield from pget()
        for h in range(4):
            hs = slice(h * 128, (h + 1) * 128)
            mm(pU[:, hs], PTf[:, hs], vb[par][:, j, h, :], True, True, [r_PTf, r_vb[par][h]], [rpU])
        pW, rpW = yield from pget()
        for h in range(4):
            hs = slice(h * 128, (h + 1) * 128)
            mm(pW[:, hs], kbg[par][:, j, h, :], PTf[:, hs], True, True, [r_PTf, r_kbg[par][h]], [rpW])
        yield
        fw.op("act", lambda e: e.activation(out=u32[hs_][:], in_=pU[:, 0:512], func=AF.Copy), [rpU], [r_u32[hs_]])
        psp.rel(rpU)
        fw.op("dve", lambda e: e.tensor_copy(out=wT[hs_][:], in_=pW[:, 0:512]), [rpW], [r_wT[hs_]])
        psp.rel(rpW)

    def chain_stream(i):
        n, j = i // 4, i % 4
        par = n % 2
        s = i % HS
        o32 = o32b[i % 2]
        r_o32 = r_o32b[i % 2]
        tk = slice(j * 128, (j + 1) * 128)
        for c in range(2):
            R = slice(c * 64, (c + 1) * 64)
            pSw, rpSw = yield from pget()
            for h in range(4):
                hs = slice(h * 128, (h + 1) * 128)
                mm(pSw[:, hs], wT[s][:, hs], Sb[:, hs], True, True, [r_wT[s], r_Sb], [rpSw])
            pSq, rpSq = yield from pget()
            for h in range(4):
                hs = slice(h * 128, (h + 1) * 128)
                mm(pSq[:, hs], qT[par][:, h, tk], Sb[:, hs], True, True, [r_qT[par][h], r_Sb], [rpSq])
            yield
            egl_b = egl[:, c, i, :].unsqueeze(2).to_broadcast([128, 4, 128])
            fw.op("dve", lambda e: e.tensor_tensor(out=v4(S32[:]), in0=v4(S32[:]), in1=egl_b, op=ALU.mult),
                  [r_S32, r_a0], [r_S32])
            fw.op("dve", lambda e: e.tensor_tensor(out=vn[R, :], in0=u32[s][R, :], in1=pSw[R, 0:512],
                                                   op=ALU.subtract), [r_u32[s], rpSw], [r_vn])
            psp.rel(rpSw)
            yield
            pSs, rpSs = yield from pget()
            for h in range(4):
                hs = slice(h * 128, (h + 1) * 128)
                mm(pSs[:, hs], kd[par][R, j, h, :], vn[R, hs], True, True, [r_kd[par][h], r_vn], [rpSs])
            pSa, rpSa = yield from pget()
            for h in range(4):
                hs = slice(h * 128, (h + 1) * 128)
                mm(pSa[:, hs], AT[s][R, hs], vn[R, hs], True, True, [r_AT[s], r_vn], [rpSa])
            egc_b = egc[R, i, :].unsqueeze(2).to_broadcast([64, 4, 128])
            fw.op("dve", lambda e: e.tensor_tensor(out=v4(tq[R, :]), in0=v4(pSq[R, 0:512]), in1=egc_b,
                                                   op=ALU.mult), [rpSq, r_a0], [r_tq])
            psp.rel(rpSq)
            yield
            fw.op("dve", lambda e: e.tensor_tensor(out=S32[:], in0=S32[:], in1=pSs[:, 0:512], op=ALU.add),
                  [r_S32, rpSs], [r_S32])
            psp.rel(rpSs)
            yield
            fw.op("act", lambda e: e.activation(out=Sb[:], in_=S32[:], func=AF.Copy), [r_S32], [r_Sb])
            fw.op("dve", lambda e: e.tensor_tensor(out=o32[R, :], in0=tq[R, :], in1=pSa[R, 0:512], op=ALU.add),
                  [r_tq, rpSa], [r_o32])
            psp.rel(rpSa)
            yield

    def epi_stream(i):
        n, j = i // 4, i % 4
        o32 = o32b[i % 2]
        r_o32 = r_o32b[i % 2]
        pz, rpz = yield from pget()
        for k in range(8):
            mm(pz[:, 0:512], hT[:, k, i * 128:(i + 1) * 128], wdn[:, k, 1536:2048], k == 0, k == 7,
               [r_hT[i]] + wres(wp, 1536, 2048), [rpz])
        yield
        fw.op("act", lambda e: e.activation(out=zs[:], in_=pz[:, 0:512], func=AF.Silu), [rpz], [r_zs])
        psp.rel(rpz)
        fw.op("act", lambda e: e.activation(out=osq[:], in_=o32[:], func=AF.Square), [r_o32], [r_osq])
        yield
        fw.op("dve", lambda e: e.tensor_tensor(out=zsw[:], in0=zs[:], in1=cfs("dnw"), op=ALU.mult),
              [r_zs, r_cf], [r_zsw])
        fw.op("dve", lambda e: e.tensor_reduce(out=oss[:], in_=v4(osq[:]), axis=AX.X, op=ALU.add),
              [r_osq], [r_oss])
        yield
        fw.op("act", lambda e: e.activation(out=oss[:], in_=oss[:], func=AF.Ln, bias=eps_c, scale=1.0 / 128),
              [r_oss, r_cf], [r_oss])
        fw.op("act", lambda e: e.activation(out=oss[:], in_=oss[:], func=AF.Exp, scale=-0.5), [r_oss], [r_oss])
        yield
        oss_b = oss[:].unsqueeze(2).to_broadcast([128, 4, 128])
        fw.op("dve", lambda e: e.tensor_tensor(out=v4(osq[:]), in0=v4(o32[:]), in1=oss_b, op=ALU.mult),
              [r_o32, r_oss, r_osq], [r_osq])
        fw.op("dve", lambda e: e.tensor_tensor(out=on[:], in0=osq[:], in1=zsw[:], op=ALU.mult),
              [r_osq, r_zsw], [r_on])
        yield
        pO, rpO = yield from pget()
        pOb = pO[:].bitcast(BF16)
        for h in range(4):
            fw.op("pe", lambda e: e.transpose(out=pOb[:, h * 128:(h + 1) * 128],
                                              in_=on[:, h * 128:(h + 1) * 128], identity=ident_b),
                  [r_on, r_cb], [rpO])
        yield
        fw.op("act", lambda e: e.activation(out=oT[:, 0:4, i * 128:(i + 1) * 128], in_=v4(pOb[:, 0:512]),
                                            func=AF.Copy), [rpO], [r_oT[h][n] for h in range(4)])
        psp.rel(rpO)

    def sec_items(n):
        return [(n, h, typ) for h in range(4) for typ in range(2)] + [(n, h, 2) for h in range(4)]

    def chain(gens):
        for g_ in gens:
            yield from g_

    def tiles_gen():
        for i in range(NT):
            yield ("need_sections", i // 4)
            yield from pre_stream(i, i % NP)
            yield ("pre_done", i)

    import os
    nbanks = int(os.environ.get('DN_BANKS', '4'))
    ntl = 4 * nbanks
    sec_done = [False] * 5
    pre_done = [False] * (NT + 1)
    chain_done = [False] * (NT + 1)
    epi_done = [False] * (NT + 1)

    def sections_meta():
        for n in range(nbanks):
            while n >= 2 and not chain_done[4 * (n - 2) + 3]:
                yield
            if n == 0:
                yield from streams_gen(sec_items(n), NS + NP, sec_stream)
                fw.barrier_all()
            else:
                yield from streams_gen(sec_items(n), NS, sec_stream)
            sec_done[n] = True

    def pre_one(i, s):
        while not sec_done[i // 4]:
            yield
        while i >= HS and not chain_done[i - HS]:
            yield
        yield from pre_stream(i, s)
        pre_done[i] = True

    def pre_meta():
        yield from streams_gen(list(range(ntl)), NP, pre_one, stagger=12)

    def chain_meta():
        for i in range(ntl):
            while not pre_done[i]:
                yield
            while i >= 2 and not epi_done[i - 2]:
                yield
            yield from chain_stream(i)
            chain_done[i] = True

    def epi_meta():
        for i in range(ntl):
            while not chain_done[i]:
                yield
            yield from epi_stream(i)
            epi_done[i] = True

    metas = [sections_meta(), pre_meta(), chain_meta(), epi_meta()]
    stall = 0
    last = fw.ninst
    while metas:
        for m_ in list(metas):
            try:
                next(m_)
            except StopIteration:
                metas.remove(m_)
        if fw.ninst == last:
            stall += 1
            if stall > 100000:
                raise RuntimeError("stream scheduler livelock")
        else:
            stall = 0
            last = fw.ninst


def phase_swa(nc, fw, ph, sb, psget, mm, load_w, wres, cfs, cbs, hT, oT, r_hT, r_oT, r_cf, r_cb, hres,
              wswa_d, rope_d, rot, pst, r_ps):
    ident_b = cbs("ident")
    ones_b = cbs("ones")
    eps_c = cfs("eps")
    psp = PSP(pst, r_ps)

    def pget():
        n = 0
        while True:
            p = psp.try_get()
            if p is not None:
                return p
            n += 1
            if n > 100000:
                raise RuntimeError("PSUM starvation")
            yield

    rope = sb("rope", [32, 2, T], F32, ph)
    r_rope = Res()
    fw.dma(rope[:], rope_d[:, :, :], writes=[r_rope])
    wsw = sb("wsw", [128, 8, 1280], BF16, ph)
    wr = {"qk": Res(), "v": Res(), "z": Res()}
    wrange = {"qk": (0, 768), "v": (768, 1152), "z": (1152, 1280)}

    def load_head(h, part):
        c0, c1 = wrange[part]
        srcv = wswa_d[h].rearrange("(k p) c -> p k c", p=128)
        fw.dma(wsw[:, :, c0:c1], srcv[:, :, c0:c1], writes=[wr[part]], q="pool")

    qk = [[sb("qk%d%d" % (t_, g_), [128, T], BF16, ph) for g_ in range(3)] for t_ in range(2)]
    r_qk = [[[Res() for _ in range(4)] for _ in range(3)] for _ in range(2)]
    V = [sb("V%d" % g_, [128, 16, 128], BF16, ph) for g_ in range(3)]
    r_V = [[Res() for _ in range(16)] for _ in range(3)]
    NQ = 6
    sq = [sb("ssq%d" % i, [128, 512], BF16, ph) for i in range(NQ)]
    rn = [sb("srn%d" % i, [128, 512], F32, ph) for i in range(NQ)]
    qn = [sb("sqn%d" % i, [128, 512], F32, ph) for i in range(NQ)]
    qnb = [sb("sqnb%d" % i, [32, 512], BF16, ph) for i in range(NQ)]
    r_sq = [Res() for _ in range(NQ)]
    r_rn = [Res() for _ in range(NQ)]
    r_qn = [Res() for _ in range(NQ)]
    r_qnb = [Res() for _ in range(NQ)]
    NA = 3
    zsall = sb("zsall", [128, T], F32, ph)
    r_zsall = [Res() for _ in range(4)]
    ee = [[sb("see%d%d" % (s, i), [128, 512], BF16, ph) for i in range(2)] for s in range(NA)]
    PP = ee
    r_ee = [[Res(), Res()] for _ in range(NA)]
    r_PP = r_ee
    rden = [sb("srden%d" % s, [128, 512], F32, ph) for s in range(NA)]
    tn = rden
    r_rden = [Res() for _ in range(NA)]
    r_tn = r_rden

    for part in ("qk", "v", "z"):
        load_head(0, part)

    for h in range(4):
        def qk_stream(arg, s):
            g_, typ, n = arg
            c0 = typ * 384 + g_ * 128
            dst = qk[typ][g_]
            wname = "qw" if typ == 0 else "kw"
            tok = slice(n * 512, (n + 1) * 512)
            pr, rpr = yield from pget()
            for k in range(8):
                mm(pr[:, 0:512], wsw[:, k, c0:c0 + 128], hT[:, k, tok], k == 0, k == 7, hres(n) + [wr["qk"]], [rpr])
            yield
            fw.op("act", lambda e: e.activation(out=sq[s][:], in_=pr[:, 0:512], func=AF.Square), [rpr], [r_sq[s]])
            yield
            pss, rpss = yield from pget()
            mm(pss[:, 0:512], ones_b, sq[s][:], True, True, [r_sq[s], r_cb], [rpss])
            yield
            fw.op("act", lambda e: e.activation(out=rn[s][:], in_=pss[:, 0:512], func=AF.Ln, bias=eps_c,
                                                scale=1.0 / 128), [rpss, r_cf], [r_rn[s]])
            psp.rel(rpss)
            yield
            fw.op("act", lambda e: e.activation(out=rn[s][:], in_=rn[s][:], func=AF.Exp, scale=-0.5),
                  [r_rn[s]], [r_rn[s]])
            yield
            fw.op("dve", lambda e: e.scalar_tensor_tensor(out=qn[s][:], in0=pr[:, 0:512],
                                                          scalar=cfs(wname, g_, g_ + 1), in1=rn[s][:],
                                                          op0=ALU.mult, op1=ALU.mult),
                  [rpr, r_rn[s], r_cf], [r_qn[s]])
            psp.rel(rpr)
            yield
            fw.op("dve", lambda e: e.tensor_tensor(out=qnb[s][:], in0=qn[s][0:32, :], in1=rope[:, 1, tok],
                                                   op=ALU.mult), [r_qn[s], r_rope], [r_qnb[s]])
            fw.op("act", lambda e: e.activation(out=dst[:, tok], in_=qn[s][:], func=AF.Copy),
                  [r_qn[s]], [r_qk[typ][g_][n]])
            yield
            prt, rprt = yield from pget()
            mm(prt[0:32, 0:512], cbs("rm")[0:32, :], qnb[s][:], True, True, [r_qnb[s], r_cb], [rprt])
            fw.op("dve", lambda e: e.tensor_tensor(out=rn[s][0:32, :], in0=qn[s][0:32, :], in1=rope[:, 0, tok],
                                                   op=ALU.mult), [r_qn[s], r_rope, r_rn[s]], [r_rn[s]])
            yield
            fw.op("dve", lambda e: e.tensor_tensor(out=dst[0:32, tok], in0=rn[s][0:32, :], in1=prt[0:32, 0:512],
                                                   op=ALU.add), [r_rn[s], rprt], [r_qk[typ][g_][n]])
            psp.rel(rprt)

        def v_stream(arg, s):
            g_, b4 = arg
            c0 = 768 + g_ * 128
            pv, rpv = yield from pget()
            for bb in range(4):
                blk = b4 * 4 + bb
                if g_ == 0:
                    tsl = slice(blk * 128, (blk + 1) * 128)
                    hr = [r_hT[blk]]
                elif g_ == 1:
                    n_, r_ = blk // 4, blk % 4
                    tsl = slice(512 * n_ + r_, 512 * (n_ + 1), 4)
                    hr = hres(n_)
                else:
                    tsl = slice(blk, T, 16)
                    hr = r_hT
                for k in range(8):
                    mm(pv[:, bb * 128:(bb + 1) * 128], hT[:, k, tsl], wsw[:, k, c0:c0 + 128], k == 0, k == 7,
                       list(hr) + [wr["v"]], [rpv])
                yield
            vout = V[g_][:, b4 * 4:(b4 + 1) * 4, :]
            vin = pv[:, 0:512].rearrange("p (b d) -> p b d", b=4)
            vres = [r_V[g_][b4 * 4 + bb] for bb in range(4)]
            if (g_ * 4 + b4) % 2 == 0:
                fw.op("act", lambda e: e.activation(out=vout, in_=vin, func=AF.Copy), [rpv], vres)
            else:
                fw.op("dve", lambda e: e.tensor_copy(out=vout, in_=vin), [rpv], vres)
            psp.rel(rpv)

        def mixed_stream(arg, s):
            if arg[0] == "v":
                return v_stream(arg[1:], s)
            return qk_stream(arg[1:], s)

        qk_items = [("qk", g_, typ, n) for g_ in range(3) for typ in range(2) for n in range(4)]
        v_items = [("v", g_, b4) for g_ in range(3) for b4 in range(4)]
        items = []
        for ii in range(12):
            items += qk_items[2 * ii:2 * ii + 2] + [v_items[ii]]
        run_streams(items, NQ, mixed_stream, stagger=2)
        if h < 3:
            load_head(h + 1, "qk")
        if h < 3:
            load_head(h + 1, "v")

        def attn_stream(m, s):
            num, r_num = yield from pget()
            den, r_den = yield from pget()
            first = [True, True]

            def st(idx):
                f = first[idx]
                first[idx] = False
                return f

            def geom(g_):
                def cols(n_, r_):
                    if g_ == 0:
                        tt = 4 * n_ + r_
                        return slice(tt * 128, (tt + 1) * 128)
                    return slice(512 * n_ + r_, 512 * (n_ + 1), 4)

                def outsl(r_):
                    if g_ == 0:
                        return slice(r_ * 128, (r_ + 1) * 128)
                    return slice(r_, 512, 4)

                def prevblk(n_, r_):
                    if g_ == 0:
                        tt = 4 * n_ + r_ - 1
                        if tt < 0:
                            return None
                        return (tt // 4, tt % 4)
                    if n_ == 0:
                        return None
                    return (n_ - 1, r_)
                return cols, outsl, prevblk

            groups = []
            for g_ in range(2):
                groups.append((g_, "cur"))
                _, _, prevblk = geom(g_)
                if any(prevblk(m, r_) is not None for r_ in range(4)):
                    groups.append((g_, "prev"))
            groups.append((2, "cur"))
            pend = None
            gi = 0

            def emit_pv(p):
                g_, kind, buf, rbuf, lo = p
                if g_ == 2:
                    for r_ in range(16):
                        mm(num[:, slice(r_, 512, 16)], V[2][:, r_, :], buf[:, r_ * 32:(r_ + 1) * 32], st(0), False,
                           [r_V[2][r_], rbuf], [r_num])
                        mm(den[:, slice(r_, 512, 16)], ones_b, buf[:, r_ * 32:(r_ + 1) * 32], st(1), False,
                           [r_cb, rbuf], [r_den])
                    return
                cols, outsl, prevblk = geom(g_)
                for r_ in range(4):
                    if kind == "cur":
                        vb_ = 4 * m + r_
                    else:
                        pb = prevblk(m, r_)
                        if pb is None:
                            continue
                        vb_ = 4 * pb[0] + pb[1]
                    mm(num[:, outsl(r_)], V[g_][:, vb_, :], buf[:, r_ * 128:(r_ + 1) * 128], st(0), False,
                       [r_V[g_][vb_], rbuf], [r_num])
                    mm(den[:, outsl(r_)], ones_b, buf[:, r_ * 128:(r_ + 1) * 128], st(1), False,
                       [r_cb, rbuf], [r_den])

            def emit_scores(pc, rpc, g_, kind):
                lo = 0
                if g_ == 2:
                    qTt, kTt = qk[0][2], qk[1][2]
                    for r_ in range(16):
                        mm(pc[:, r_ * 32:(r_ + 1) * 32], kTt[:, slice(r_, T, 16)],
                           qTt[:, slice(512 * m + r_, 512 * (m + 1), 16)], True, True,
                           [r_qk[0][2][m]] + r_qk[1][2], [rpc])
                    mask = cbs("m2", m * 512, (m + 1) * 512)
                else:
                    qTt, kTt = qk[0][g_], qk[1][g_]
                    cols, outsl, prevblk = geom(g_)
                    if kind == "cur":
                        for r_ in range(4):
                            mm(pc[:, r_ * 128:(r_ + 1) * 128], kTt[:, cols(m, r_)], qTt[:, cols(m, r_)], True, True,
                               [r_qk[0][g_][m], r_qk[1][g_][m]], [rpc])
                        mask = cbs("masku")
                    else:
                        prs = [(r_, prevblk(m, r_)) for r_ in range(4) if prevblk(m, r_) is not None]
                        lo = prs[0][0] * 128
                        for r_, (pn, pr_) in prs:
                            mm(pc[:, r_ * 128:(r_ + 1) * 128], kTt[:, cols(pn, pr_)], qTt[:, cols(m, r_)], True, True,
                               [r_qk[0][g_][m], r_qk[1][g_][pn], r_qk[1][g_][m]], [rpc])
                        mask = cbs("maskl")
                return pc, rpc, lo, mask

            pc0, rpc0 = yield from pget()
            nxt = emit_scores(pc0, rpc0, *groups[0])
            yield
            for gi, (g_, kind) in enumerate(groups):
                b = gi % 2
                pc, rpc, lo, mask = nxt
                fw.op("act", lambda e: e.activation(out=ee[s][b][:, lo:512], in_=pc[:, lo:512], func=AF.Exp,
                                                    scale=QSCALE), [rpc], [r_ee[s][b]])
                psp.rel(rpc)
                yield
                meng = "dve"
                fw.op(meng, lambda e: e.tensor_tensor(out=PP[s][b][:, lo:512], in0=ee[s][b][:, lo:512],
                                                      in1=mask[:, lo:512], op=ALU.mult),
                      [r_ee[s][b], r_cb], [r_PP[s][b]])
                if gi + 1 < len(groups):
                    pcn, rpcn = yield from pget()
                    nxt = emit_scores(pcn, rpcn, *groups[gi + 1])
                yield
                emit_pv((g_, kind, PP[s][b], r_PP[s][b], lo))
                yield
            tok = slice(m * 512, (m + 1) * 512)
            fw.op("act", lambda e: e.activation(out=rden[s][:], in_=den[:, 0:512], func=AF.Ln), [r_den], [r_rden[s]])
            psp.rel(r_den)
            yield
            fw.op("act", lambda e: e.activation(out=rden[s][:], in_=rden[s][:], func=AF.Exp, scale=-1.0),
                  [r_rden[s]], [r_rden[s]])
            yield
            fw.op("dve", lambda e: e.tensor_tensor(out=tn[s][:], in0=num[:, 0:512], in1=rden[s][:], op=ALU.mult),
                  [r_num, r_rden[s]], [r_tn[s]])
            psp.rel(r_num)
            yield
            fw.op("dve", lambda e: e.tensor_tensor(out=oT[:, 4 + h, tok], in0=tn[s][:], in1=zsall[:, tok], op=ALU.mult),
                  [r_tn[s], r_zsall[m]], [r_oT[4 + h][m]])

        for m in range(4):
            tok = slice(m * 512, (m + 1) * 512)
            pz, rpz = psp.get()
            for k in range(8):
                mm(pz[:, 0:512], wsw[:, k, 1152:1280], hT[:, k, tok], k == 0, k == 7, hres(m) + [wr["z"]], [rpz])
            fw.op("act", lambda e: e.activation(out=zsall[:, tok], in_=pz[:, 0:512], func=AF.Silu), [rpz], [r_zsall[m]])
            psp.rel(rpz)
        run_streams(list(range(4)), NA, attn_stream, stagger=5)
        if h < 3:
            load_head(h + 1, "z")


def phase_out(nc, fw, ph, sb, psget, mm, load_w, wres, cfs, cbs, hT, oT, r_hT, r_oT, r_cf, r_cb, hres,
              wg_d, wbd_d, wbs_d, wo_d, x_d, out_d, r_out):
    wbd = sb("wbd", [128, 4, 1024], BF16, ph)
    wbs = sb("wbs", [128, 4, 1024], BF16, ph)
    wpbd = load_w(wbd, wbd_d, 4, 1024)
    wpbs = load_w(wbs, wbs_d, 4, 1024)
    wg = sb("wg", [128, 8, 2048], BF16, ph)
    wo = sb("wo", [128, 8, 1024], BF16, ph)
    wgv = wg_d.rearrange("(k p) c -> p k c", p=128)
    wpg = []
    for c in range(0, 8, 2):
        for base in (0, 1024):
            c0, c1 = base + c * 128, base + (c + 2) * 128
            r_ = Res()
            fw.dma(wg[:, :, c0:c1], wgv[:, :, c0:c1], writes=[r_], q="pool")
            wpg.append((c0, c1, r_))
    wpo = load_w(wo, wo_d, 8, 1024)
    mT = sb("mT", [128, 8, 512], BF16, ph)
    r_mT = [Res() for _ in range(8)]
    sg = [sb("sg%d" % i, [128, 512], F32, ph) for i in range(2)]
    r_sg = [Res(), Res()]
    m1 = [sb("m1%d" % i, [128, 512], F32, ph) for i in range(2)]
    r_m1 = [Res(), Res()]
    xt = [sb("oxt%d" % i, [128, D], F32, ph) for i in range(2)]
    r_xt = [Res(), Res()]
    ot = [sb("oot%d" % i, [128, D], F32, ph) for i in range(2)]
    for n in range(4):
        tok = slice(n * 512, (n + 1) * 512)
        for c in range(8):
            cs_ = slice(c * 128, (c + 1) * 128)
            pyd, rpyd = psget()
            for hh in range(4):
                mm(pyd[:, 0:512], wbd[:, hh, cs_], oT[:, hh, tok], hh == 0, hh == 3,
                   [r_oT[hh][n]] + wres(wpbd, c * 128, (c + 1) * 128), [rpyd])
            pys, rpys = psget()
            for hh in range(4):
                mm(pys[:, 0:512], wbs[:, hh, cs_], oT[:, 4 + hh, tok], hh == 0, hh == 3,
                   [r_oT[4 + hh][n]] + wres(wpbs, c * 128, (c + 1) * 128), [rpys])
            pgd, rpgd = psget()
            for k in range(8):
                mm(pgd[:, 0:512], wg[:, k, cs_], hT[:, k, tok], k == 0, k == 7,
                   hres(n) + wres(wpg, c * 128, (c + 1) * 128), [rpgd])
            pgs, rpgs = psget()
            for k in range(8):
                mm(pgs[:, 0:512], wg[:, k, 1024 + c * 128:1024 + (c + 1) * 128], hT[:, k, tok], k == 0, k == 7,
                   hres(n) + wres(wpg, 1024 + c * 128, 1024 + (c + 1) * 128), [rpgs])
            fw.op("act", lambda e: e.activation(out=sg[0][:], in_=pgd[:, 0:512], func=AF.Sigmoid), [rpgd], [r_sg[0]])
            fw.op("act", lambda e: e.activation(out=sg[1][:], in_=pgs[:, 0:512], func=AF.Sigmoid), [rpgs], [r_sg[1]])
            fw.op("dve", lambda e: e.tensor_tensor(out=m1[0][:], in0=pyd[:, 0:512], in1=sg[0][:], op=ALU.mult),
                  [rpyd, r_sg[0]], [r_m1[0]])
            fw.op("dve", lambda e: e.tensor_tensor(out=m1[1][:], in0=pys[:, 0:512], in1=sg[1][:], op=ALU.mult),
                  [rpys, r_sg[1]], [r_m1[1]])
            fw.op("pool", lambda e: e.tensor_tensor(out=mT[:, c, :], in0=m1[0][:], in1=m1[1][:], op=ALU.add),
                  [r_m1[0], r_m1[1]], [r_mT[c]])
        for j in range(4):
            i = 4 * n + j
            s = i % 2
            fw.dma(xt[s][:], x_d[i * 128:(i + 1) * 128, :], writes=[r_xt[s]])
            for half in range(2):
                hs = slice(half * 512, (half + 1) * 512)
                po, rpo = psget()
                for k in range(8):
                    mm(po[:, 0:512], mT[:, k, j * 128:(j + 1) * 128], wo[:, k, hs], k == 0, k == 7,
                       [r_mT[k]] + wres(wpo, half * 512, (half + 1) * 512), [rpo])
                fw.op("dve", lambda e: e.tensor_tensor(out=ot[s][:, hs], in0=po[:, 0:512], in1=xt[s][:, hs],
                                                       op=ALU.add), [rpo, r_xt[s]], [r_out[s]])
            fw.dma(out_d[i * 128:(i + 1) * 128, :], ot[s][:], reads=[r_out[s]], semres=r_out[s])
    fw.final_wait(r_out)


_CACHE = {}


def kernel(**inputs):
    inp = {k: np.asarray(v) for k, v in inputs.items()}
    shared = _prep_shared(inp)
    if "nc" not in _CACHE:
        _CACHE["nc"] = build()
    nc = _CACHE["nc"]
    x = np.ascontiguousarray(inp["x"], dtype=np.float32)
    in_maps = []
    for b in range(8):
        m = dict(shared)
        m["x"] = x[b]
        in_maps.append(m)
    res = run_bass_kernel_spmd(nc, in_maps, core_ids=list(range(8)))
    out = np.stack([np.asarray(res.results[b]["out"]) for b in range(8)], axis=0)
    return out.astype(np.float32)
```

```python
import numpy as np
from contextlib import ExitStack
import concourse.bass as bass
import concourse.mybir as mybir
from concourse.bass_utils import run_bass_kernel_spmd

F32 = mybir.dt.float32
BF16 = mybir.dt.bfloat16
AF = mybir.ActivationFunctionType
ALU = mybir.AluOpType
AX = mybir.AxisListType

T = 2048
D = 1024
NT = 16
EPS = 1e-6
BIG = 30000.0
QSCALE = 128.0 ** -0.5


class Res:
    __slots__ = ("key", "w", "r", "dsem", "dcnt", "excl")

    def __init__(self, excl=False):
        self.excl = excl
        self.key = None
        self.w = None
        self.r = {}
        self.dsem = None
        self.dcnt = 0


class FW:
    def __init__(self, nc, es):
        self.nc = nc
        self.es = es
        self.eng = {"pe": nc.tensor, "act": nc.scalar, "dve": nc.vector, "pool": nc.gpsimd, "sp": nc.sync}
        self.sem = {}
        self.cnt = {}
        for k in self.eng:
            self.sem[k] = es.enter_context(nc.semaphore("s_" + k))
            self.cnt[k] = 0
        self.waited = {k: {} for k in self.eng}
        self.semobj = dict(self.sem)
        self.ndsem = 0
        self.ninst = 0

    def _wait(self, e, deps):
        best = {}
        for (k, v) in deps:
            if k == e and e == "pe":
                continue
            if best.get(k, 0) < v:
                best[k] = v
        for k, v in best.items():
            if self.waited[e].get(k, 0) >= v:
                continue
            self.eng[e].wait_ge(self.semobj[k], v)
            self.waited[e][k] = v

    @staticmethod
    def _deps(reads, writes, e=None):
        deps = []
        for r in reads:
            if r.w is not None:
                deps.append(r.w)
            if r.excl:
                deps.extend((k, v) for k, v in r.r.items() if k != e)
        for w in writes:
            if w.w is not None:
                deps.append(w.w)
            deps.extend(w.r.items())
        return deps

    def op(self, e, fn, reads=(), writes=()):
        self._wait(e, self._deps(reads, writes, e))
        inst = fn(self.eng[e])
        self.cnt[e] += 1
        c = self.cnt[e]
        inst.then_inc(self.sem[e], 1)
        for r in reads:
            if r.r.get(e, 0) < c:
                r.r[e] = c
        for w in writes:
            w.w = (e, c)
            w.r = {}
        self.ninst += 1
        return inst

    def dma(self, out, in_, reads=(), writes=(), q="sp", semres=None):
        self._wait(q, self._deps(reads, writes, q))
        sr = semres or (writes[0] if writes else reads[0])
        if sr.dsem is None:
            sr.key = "d_%d" % self.ndsem
            sr.dsem = self.es.enter_context(self.nc.semaphore(sr.key))
            self.ndsem += 1
            self.semobj[sr.key] = sr.dsem
        key = sr.key
        inst = self.eng[q].dma_start(out=out, in_=in_)
        sr.dcnt += 16
        inst.then_inc(sr.dsem, 16)
        for r in reads:
            if r.r.get(key, 0) < sr.dcnt:
                r.r[key] = sr.dcnt
        for w in writes:
            w.w = (key, sr.dcnt)
            w.r = {}
        self.ninst += 1
        return inst

    def barrier_all(self):
        for e in self.eng:
            deps = [(k, self.cnt[k]) for k in self.eng if self.cnt[k] > 0]
            best = {}
            for k, v in deps:
                best[k] = v
            for k, v in best.items():
                if self.waited[e].get(k, 0) >= v:
                    continue
                self.eng[e].wait_ge(self.semobj[k], v)
                self.waited[e][k] = v

    def final_wait(self, reslist, e="sp"):
        deps = []
        for r in reslist:
            if r.w is not None:
                deps.append(r.w)
            deps.extend(r.r.items())
        self._wait(e, deps)


class PSP:
    def __init__(self, tiles, res):
        self.t, self.r = tiles, res
        self.live = [False] * len(tiles)
        self.i = 0

    def get(self):
        n = len(self.t)
        for _ in range(n):
            i = self.i % n
            self.i += 1
            if not self.live[i]:
                self.live[i] = True
                return self.t[i], self.r[i]
        raise RuntimeError("no free PSUM bank")

    def try_get(self):
        n = len(self.t)
        for _ in range(n):
            i = self.i % n
            self.i += 1
            if not self.live[i]:
                self.live[i] = True
                return self.t[i], self.r[i]
        return None

    def rel(self, res):
        self.live[self.r.index(res)] = False


def streams_gen(items, W, make, stagger=0):
    it = iter(items)
    active = []
    free = list(range(W))
    pending = True
    rnd = 0
    next_start = 0
    while True:
        while free and pending and rnd >= next_start:
            try:
                a = next(it)
            except StopIteration:
                pending = False
                break
            s = free.pop(0)
            active.append([make(a, s), s])
            if stagger:
                next_start = rnd + stagger
                break
        if not active and not pending:
            break
        for ent in list(active):
            try:
                next(ent[0])
            except StopIteration:
                active.remove(ent)
                free.append(ent[1])
        rnd += 1
        yield


def run_streams(items, W, make, stagger=0):
    for _ in streams_gen(items, W, make, stagger):
        pass


CF = {}
_o = 0
for _n, _w in [("ident", 128), ("tri", 128), ("tris", 128), ("ch0", 128), ("ch1", 128), ("nw", 8), ("cw", 48),
               ("dtb", 4), ("alog", 4), ("dnw", 512), ("qw", 3), ("kw", 3), ("eps", 1)]:
    CF[_n] = (_o, _w)
    _o += _w
NCF = _o
CB = {}
_o = 0
for _n, _w in [("ident", 128), ("ones", 128), ("x1", 128), ("masku", 512), ("maskl", 512), ("m2", 2048), ("rm", 32)]:
    CB[_n] = (_o, _w)
    _o += _w
NCB = _o


def _const_arrays():
    p = np.arange(128)
    same = (p[:, None] // 64) == (p[None, :] // 64)
    cf = np.zeros((128, NCF), np.float32)

    def put(name, arr):
        o, w = CF[name]
        cf[:, o:o + w] = arr
    put("ident", np.eye(128, dtype=np.float32))
    put("tri", (same & (p[:, None] <= p[None, :])).astype(np.float32))
    put("tris", (same & (p[:, None] > p[None, :])).astype(np.float32))
    put("ch0", np.broadcast_to((p[:, None] < 64), (128, 128)).astype(np.float32))
    put("ch1", np.broadcast_to((p[:, None] >= 64), (128, 128)).astype(np.float32))
    put("eps", np.full((128, 1), EPS, np.float32))
    cb = np.zeros((128, NCB), np.float32)

    def putb(name, arr):
        o, w = CB[name]
        cb[:, o:o + w] = arr
    putb("ident", np.eye(128, dtype=np.float32))
    putb("ones", np.ones((128, 128), np.float32))
    putb("x1", np.where(same & (p[:, None] > p[None, :]), 0.0, BIG).astype(np.float32))
    mu = (p[:, None] <= p[None, :]).astype(np.float32)
    ml = (p[:, None] >= p[None, :]).astype(np.float32)
    putb("masku", np.tile(mu, (1, 4)))
    putb("maskl", np.tile(ml, (1, 4)))
    m2 = np.zeros((128, 4, 16, 32), np.float32)
    for m in range(4):
        q = 32 * m + np.arange(32)
        m2[:, m, :, :] = (p[:, None] <= q[None, :]).astype(np.float32)[:, None, :]
    putb("m2", m2.reshape(128, 2048))
    rm = np.zeros((128, 32), np.float32)
    for m in range(16):
        rm[16 + m, m] = -1.0
    for m in range(16, 32):
        rm[m - 16, m] = 1.0
    putb("rm", rm)
    pos = np.arange(T, dtype=np.float32)
    inv_freq = (np.float32(500000.0) ** (-np.arange(0, 32, 2, dtype=np.float32) / np.float32(32))).astype(np.float32)
    ang = (pos[:, None] * inv_freq[None, :]).astype(np.float32)
    cos = np.cos(ang).astype(np.float32).T
    sin = np.sin(ang).astype(np.float32).T
    rope = np.zeros((32, 2, T), np.float32)
    rope[0:16, 0] = cos
    rope[16:32, 0] = cos
    rope[0:16, 1] = sin
    rope[16:32, 1] = sin
    return cf, cb, rope


def _prep_shared(inp):
    cf, cb, rope = _const_arrays()
    w_in = inp["w_in"][0]

    def put(name, arr):
        o, w = CF[name]
        cf[:, o:o + w] = arr
    put("nw", inp["norm_w"][0].reshape(8, 128).T)
    cw = inp["conv_w"][0, :, 0, :]
    put("cw", cw.T.reshape(12, 128, 4).transpose(1, 0, 2).reshape(128, 48))
    put("dtb", np.broadcast_to(inp["dn_dt_bias"][0][None, :], (128, 4)))
    put("alog", np.broadcast_to(inp["dn_a_log"][0][None, :], (128, 4)))
    put("dnw", np.broadcast_to(np.tile(inp["dn_norm_w"][0], 4)[None, :], (128, 512)))
    put("qw", inp["q_norm_w"][0].T)
    put("kw", inp["k_norm_w"][0].T)
    w_dn = np.ascontiguousarray(w_in[:, 0:2056])
    base = 2056
    w_swa = np.empty((4, 1024, 1280), np.float32)
    for h in range(4):
        cols = []
        for typ in range(3):
            for g in range(3):
                c0 = base + typ * 1536 + g * 512 + h * 128
                cols.append(w_in[:, c0:c0 + 128])
        c0 = base + 3 * 1536 + h * 128
        cols.append(w_in[:, c0:c0 + 128])
        w_swa[h] = np.concatenate(cols, axis=1)
    w_g = np.ascontiguousarray(w_in[:, base + 3 * 1536 + 512:])
    assert w_g.shape[1] == 2048
    return {"cf": cf, "cb": cb, "rope": rope, "w_dn": w_dn, "w_swa": w_swa, "w_g": w_g,
            "w_bd": np.ascontiguousarray(inp["w_branch_dn"][0]), "w_bs": np.ascontiguousarray(inp["w_branch_swa"][0]),
            "w_o": np.ascontiguousarray(inp["w_out"][0])}


def build(stage=99, debug=False):
    nc = bass.Bass("TRN2", target_bir_lowering=False)
    x_d = nc.dram_tensor("x", [T, D], F32, kind="ExternalInput").ap()
    cf_d = nc.dram_tensor("cf", [128, NCF], F32, kind="ExternalInput").ap()
    cb_d = nc.dram_tensor("cb", [128, NCB], F32, kind="ExternalInput").ap()
    rope_d = nc.dram_tensor("rope", [32, 2, T], F32, kind="ExternalInput").ap()
    wdn_d = nc.dram_tensor("w_dn", [D, 2056], F32, kind="ExternalInput").ap()
    wswa_d = nc.dram_tensor("w_swa", [4, D, 1280], F32, kind="ExternalInput").ap()
    wg_d = nc.dram_tensor("w_g", [D, 2048], F32, kind="ExternalInput").ap()
    wbd_d = nc.dram_tensor("w_bd", [512, D], F32, kind="ExternalInput").ap()
    wbs_d = nc.dram_tensor("w_bs", [512, D], F32, kind="ExternalInput").ap()
    wo_d = nc.dram_tensor("w_o", [D, D], F32, kind="ExternalInput").ap()
    out_d = nc.dram_tensor("out", [T, D], F32, kind="ExternalOutput").ap()
    dbg_d = None
    if debug:
        dbg_d = nc.dram_tensor("dbg", [128, 8, T], BF16, kind="ExternalOutput").ap()

    with ExitStack() as es:
        fw = FW(nc, es)

        def sb(name, shape, dt, ctx=es):
            return ctx.enter_context(nc.sbuf_tensor("sb_" + name, shape, dt))

        hT = sb("hT", [128, 8, T], BF16)
        class _OT:
            d = None
            s_ = None

            def __getitem__(self, key):
                p, hsel, cols = key
                if isinstance(hsel, int):
                    return self.d[p, hsel, cols] if hsel < 4 else self.s_[p, hsel - 4, cols]
                assert hsel == slice(0, 4)
                return self.d[p, 0:4, cols]
        oT = _OT()
        oT.d = sb("oTd", [128, 4, T], BF16)
        cf = sb("cf", [128, NCF], F32)
        cb = sb("cb", [128, NCB], BF16)
        r_hT = [Res() for _ in range(NT)]
        r_oT = [[Res() for _ in range(4)] for _ in range(8)]
        r_cf, r_cb = Res(), Res()
        pst = [es.enter_context(nc.psum_tensor("ps%d" % i, [128, 512], F32)) for i in range(8)]
        r_ps = [Res(excl=True) for _ in range(8)]
        rot = {"list": list(range(8)), "i": 0}

        def psget():
            lst = rot["list"]
            i = lst[rot["i"] % len(lst)]
            rot["i"] += 1
            return pst[i], r_ps[i]

        def cfs(name, a=0, b=None):
            o, w = CF[name]
            if b is None:
                b = w
            return cf[:, o + a:o + b]

        def cbs(name, a=0, b=None):
            o, w = CB[name]
            if b is None:
                b = w
            return cb[:, o + a:o + b]

        def mm(out, lhsT, rhs, start, stop, reads, writes):
            return fw.op("pe", lambda e: e.matmul(out, lhsT, rhs, start=start, stop=stop, skip_group_check=True),
                         reads, writes)

        def load_w(dst, src, K, C):
            srcv = src.rearrange("(k p) c -> p k c", p=128)
            pieces = []
            for c0 in range(0, C, 512):
                c1 = min(C, c0 + 512)
                r = Res()
                fw.dma(dst[:, 0:K, c0:c1], srcv[:, :, c0:c1], writes=[r], q="pool")
                pieces.append((c0, c1, r))
            return pieces

        def wres(pieces, a, b):
            return [r for (c0, c1, r) in pieces if c0 < b and c1 > a]

        fw.dma(cf[:], cf_d[:, :], writes=[r_cf])
        with ExitStack() as ph:
            cbst = sb("cbst", [128, NCB], F32, ph)
            r_cbst = Res()
            fw.dma(cbst[:], cb_d[:, :], writes=[r_cbst])
            fw.op("dve", lambda e: e.tensor_copy(out=cb[:], in_=cbst[:]), [r_cbst], [r_cb])
            fw.barrier_all()
        ident_f = cfs("ident")
        ident_b = cbs("ident")
        ones_b = cbs("ones")
        eps_c = cfs("eps")

        ph_dn = es.enter_context(ExitStack())
        wdn_pre = sb("wdn", [128, 8, 2056], BF16, ph_dn)
        wdv_ = wdn_d.rearrange("(k p) c -> p k c", p=128)
        wp_pre = []
        for c0 in range(0, 2056, 512):
            c1 = min(2056, c0 + 512)
            r_ = Res()
            fw.dma(wdn_pre[:, :, c0:c1], wdv_[:, :, c0:c1], writes=[r_], q="pool")
            wp_pre.append((c0, c1, r_))

        with ExitStack() as ph:
            xt = [sb("xt%d" % i, [128, D], F32, ph) for i in range(6)]
            xs = [sb("xs%d" % i, [128, D], BF16, ph) for i in range(6)]
            junk = sb("junk", [128, D], F32, ph)
            ss = sb("ss", [128, NT], F32, ph)
            rt = sb("rt", [128, NT], F32, ph)
            rstd = sb("rstd", [128, NT], F32, ph)
            r_xt = [Res() for _ in range(6)]
            r_xs = [Res() for _ in range(6)]
            r_junk = Res()
            r_st = [Res() for _ in range(NT)]
            def p0_stream(i, s):
                fw.dma(xt[s][:], x_d[i * 128:(i + 1) * 128, :], writes=[r_xt[s]])
                yield
                fw.op("act", lambda e: e.activation(out=junk[:], in_=xt[s][:], func=AF.Square,
                                                    accum_out=ss[:, i:i + 1]), [r_xt[s]], [r_junk, r_st[i]])
                yield
                fw.op("act", lambda e: e.activation(out=rt[:, i:i + 1], in_=ss[:, i:i + 1], func=AF.Sqrt,
                                                    bias=eps_c, scale=1.0 / D), [r_st[i], r_cf], [r_st[i]])
                yield
                fw.op("dve", lambda e: e.reciprocal(out=rstd[:, i:i + 1], in_=rt[:, i:i + 1]), [r_st[i]], [r_st[i]])
                yield
                fw.op("dve", lambda e: e.tensor_scalar(out=xs[s][:], in0=xt[s][:], scalar1=rstd[:, i:i + 1],
                                                       scalar2=None, op0=ALU.mult), [r_xt[s], r_st[i]], [r_xs[s]])
                yield
                pt, rp = psget()
                pb = pt[:].bitcast(BF16)
                for k in range(8):
                    fw.op("pe", lambda e: e.transpose(out=pb[:, k * 128:(k + 1) * 128],
                                                      in_=xs[s][:, k * 128:(k + 1) * 128], identity=ident_b),
                          [r_xs[s], r_cb], [rp])
                yield
                for k in range(8):
                    dst = hT[:, k, i * 128:(i + 1) * 128]
                    src = pb[:, k * 128:(k + 1) * 128]
                    if k % 2 == 0:
                        fw.op("act", lambda e: e.activation(out=dst, in_=src, func=AF.Copy,
                                                            scale=cfs("nw", k, k + 1)), [rp, r_cf], [r_hT[i]])
                    else:
                        fw.op("dve", lambda e: e.tensor_scalar(out=dst, in0=src, scalar1=cfs("nw", k, k + 1),
                                                               scalar2=None, op0=ALU.mult), [rp, r_cf], [r_hT[i]])
                    if k % 2 == 1:
                        yield

            run_streams(list(range(NT)), 6, p0_stream, stagger=2)
            fw.barrier_all()

        def hres(n):
            return r_hT[4 * n:4 * n + 4]

        if stage >= 1:
            phase_dn(nc, fw, ph_dn, sb, psget, mm, load_w, wres, cfs, cbs, hT, oT, r_hT, r_oT, r_cf, r_cb, hres,
                     wdn_d, pst, r_ps, wdn_pre, wp_pre)
            fw.barrier_all()
        ph_dn.close()
        oT.s_ = sb("oTs", [128, 4, T], BF16)

        if stage >= 2:
            with ExitStack() as ph:
                phase_swa(nc, fw, ph, sb, psget, mm, load_w, wres, cfs, cbs, hT, oT, r_hT, r_oT, r_cf, r_cb, hres,
                          wswa_d, rope_d, rot, pst, r_ps)
                fw.barrier_all()
        r_out = [Res(), Res()]
        if stage >= 3:
            with ExitStack() as ph:
                phase_out(nc, fw, ph, sb, psget, mm, load_w, wres, cfs, cbs, hT, oT, r_hT, r_oT, r_cf, r_cb, hres,
                          wg_d, wbd_d, wbs_d, wo_d, x_d, out_d, r_out)
        else:
            with ExitStack() as ph:
                zt = sb("zt", [128, D], F32, ph)
                fw.op("dve", lambda e: e.memset(zt[:], 0.0), [], [r_out[0]])
                for i in range(NT):
                    fw.dma(out_d[i * 128:(i + 1) * 128, :], zt[:], reads=[r_out[0]], semres=r_out[0])
                fw.final_wait(r_out)
        if debug:
            r_dbg = Res()
            allr = [r for hh in r_oT for r in hh]
            for hh in range(8):
                fw.dma(dbg_d[:, hh, :], (hT if stage == 0 else oT)[:, hh, :], reads=allr + r_hT, semres=r_dbg)
            fw.final_wait([r_dbg] + allr)
        fw.final_wait(r_out)
        fw.barrier_all()
    return nc


def phase_dn(nc, fw, ph, sb, psget, mm, load_w, wres, cfs, cbs, hT, oT, r_hT, r_oT, r_cf, r_cb, hres, wdn_d,
             pst, r_ps, wdn, wp):
    ident_f = cfs("ident")
    ident_b = cbs("ident")
    ones_b = cbs("ones")
    eps_c = cfs("eps")
    psp = PSP(pst, r_ps)

    def pget():
        n = 0
        while True:
            p = psp.try_get()
            if p is not None:
                return p
            n += 1
            if n > 10000:
                raise RuntimeError("PSUM starvation")
            yield


    ba = sb("ba", [128, NT, 8], F32, ph)
    beta = sb("beta", [128, NT, 4], F32, ph)
    nbeta = sb("nbeta", [128, NT, 4], F32, ph)
    g = sb("g", [128, NT, 4], F32, ph)
    t0 = sb("a0t0", [128, NT, 4], F32, ph)
    t1 = sb("a0t1", [128, NT, 4], F32, ph)
    t2 = sb("a0t2", [128, NT, 4], F32, ph)
    negA = sb("negA", [128, 4], F32, ph)
    gc = sb("gc", [128, NT, 4], F32, ph)
    egc = sb("egc", [128, NT, 4], F32, ph)
    sckbg = sb("sckbg", [128, NT, 4], F32, ph)
    sckd = sb("sckd", [128, NT, 4], F32, ph)
    egl = sb("egl", [128, 2, NT, 4], F32, ph)
    r_a0 = Res()
    pba, rpba = psp.get()
    for i in range(NT):
        for k in range(8):
            mm(pba[:, i * 8:(i + 1) * 8], hT[:, k, i * 128:(i + 1) * 128], wdn[:, k, 2048:2056], k == 0, k == 7,
               [r_hT[i]] + wres(wp, 2048, 2056), [rpba])
    bav = ba[:].rearrange("p t c -> p (t c)")
    fw.op("act", lambda e: e.activation(out=bav, in_=pba[:, 0:128], func=AF.Copy), [rpba], [r_a0])
    psp.rel(rpba)
    fw.op("act", lambda e: e.activation(out=beta[:], in_=ba[:, :, 0:4], func=AF.Sigmoid), [r_a0], [r_a0])
    fw.op("dve", lambda e: e.tensor_scalar(out=nbeta[:], in0=beta[:], scalar1=-1.0, scalar2=None, op0=ALU.mult),
          [r_a0], [r_a0])
    dtb_b = cfs("dtb").unsqueeze(1).to_broadcast([128, NT, 4])
    fw.op("dve", lambda e: e.tensor_tensor(out=t0[:], in0=ba[:, :, 4:8], in1=dtb_b, op=ALU.add), [r_a0, r_cf], [r_a0])
    fw.op("dve", lambda e: e.tensor_scalar(out=t2[:], in0=t0[:], scalar1=-1.0, scalar2=None, op0=ALU.mult),
          [r_a0], [r_a0])
    fw.op("dve", lambda e: e.tensor_tensor(out=t1[:], in0=t0[:], in1=t2[:], op=ALU.max), [r_a0], [r_a0])
    fw.op("act", lambda e: e.activation(out=t1[:], in_=t1[:], func=AF.Exp, scale=-1.0), [r_a0], [r_a0])
    fw.op("act", lambda e: e.activation(out=t1[:], in_=t1[:], func=AF.Ln, bias=1.0), [r_a0], [r_a0])
    fw.op("dve", lambda e: e.tensor_scalar(out=t2[:], in0=t0[:], scalar1=0.0, scalar2=None, op0=ALU.max),
          [r_a0], [r_a0])
    fw.op("dve", lambda e: e.tensor_tensor(out=t2[:], in0=t2[:], in1=t1[:], op=ALU.add), [r_a0], [r_a0])
    fw.op("act", lambda e: e.activation(out=negA[:], in_=cfs("alog"), func=AF.Exp), [r_cf], [r_a0])
    fw.op("dve", lambda e: e.tensor_scalar(out=negA[:], in0=negA[:], scalar1=-1.0, scalar2=None, op0=ALU.mult),
          [r_a0], [r_a0])
    negA_b = negA[:].unsqueeze(1).to_broadcast([128, NT, 4])
    fw.op("dve", lambda e: e.tensor_tensor(out=g[:], in0=t2[:], in1=negA_b, op=ALU.mult), [r_a0], [r_a0])
    gv = g[:].rearrange("p t c -> p (t c)")
    pg1, rpg1 = psp.get()
    mm(pg1[:, 0:64], cfs("tri"), gv, True, True, [r_a0, r_cf], [rpg1])
    mm(pg1[:, 64:128], cfs("tris"), gv, True, True, [r_a0, r_cf], [rpg1])
    mm(pg1[:, 128:192], cfs("ch0"), gv, True, True, [r_a0, r_cf], [rpg1])
    mm(pg1[:, 192:256], cfs("ch1"), gv, True, True, [r_a0, r_cf], [rpg1])
    fw.op("act", lambda e: e.activation(out=gc[:].rearrange("p t c -> p (t c)"), in_=pg1[:, 0:64], func=AF.Copy),
          [rpg1], [r_a0])
    fw.op("act", lambda e: e.activation(out=egc[:].rearrange("p t c -> p (t c)"), in_=pg1[:, 0:64], func=AF.Exp),
          [rpg1], [r_a0])
    fw.op("act", lambda e: e.activation(out=sckd[:].rearrange("p t c -> p (t c)"), in_=pg1[:, 64:128], func=AF.Exp),
          [rpg1], [r_a0])
    fw.op("act", lambda e: e.activation(out=egl[:].rearrange("p a t c -> p (a t c)"), in_=pg1[:, 128:256],
                                        func=AF.Exp), [rpg1], [r_a0])
    psp.rel(rpg1)
    fw.op("dve", lambda e: e.tensor_tensor(out=sckbg[:], in0=beta[:], in1=egc[:], op=ALU.mult), [r_a0], [r_a0])

    NS = 2
    halo = sb("halo", [128, 12, 3], F32, ph)
    r_halo = [Res() for _ in range(12)]
    raw = [sb("raw%d" % i, [128, 515], F32, ph) for i in range(NS)]
    cv = [sb("cv%d" % i, [128, 512], F32, ph) for i in range(NS)]
    cs = [sb("cs%d" % i, [128, 512], F32, ph) for i in range(NS)]
    sq = [sb("sq%d" % i, [128, 512], BF16, ph) for i in range(NS)]
    rn = [sb("rn%d" % i, [128, 512], F32, ph) for i in range(NS)]
    r_raw = [Res() for _ in range(NS)]
    r_cv = [Res() for _ in range(NS)]
    r_cs = [Res() for _ in range(NS)]
    r_sq = [Res() for _ in range(NS)]
    r_rn = [Res() for _ in range(NS)]
    qT = [sb("qT%d" % i, [128, 4, 512], BF16, ph) for i in range(2)]
    kT = [sb("kT%d" % i, [128, 4, 512], BF16, ph) for i in range(2)]
    kbg = [sb("kbg%d" % i, [128, 4, 4, 128], BF16, ph) for i in range(2)]
    kd = [sb("kd%d" % i, [128, 4, 4, 128], BF16, ph) for i in range(2)]
    vb = [sb("vb%d" % i, [128, 4, 4, 128], BF16, ph) for i in range(2)]
    r_qT = [[Res() for _ in range(4)] for _ in range(2)]
    r_kT = [[Res() for _ in range(4)] for _ in range(2)]
    r_kbg = [[Res() for _ in range(4)] for _ in range(2)]
    r_kd = [[Res() for _ in range(4)] for _ in range(2)]
    r_vb = [[Res() for _ in range(4)] for _ in range(2)]
    NP = 2
    arena = [sb("prearena%d" % i, [128, 2816], F32, ph) for i in range(NP)]
    e1m = [arena[i][:, 0:512] for i in range(NP)]
    tg = [arena[i][:, 512:1024] for i in range(NP)]
    abf = [arena[i][:, 1024:2816].bitcast(BF16) for i in range(NP)]
    Mb = [[abf[i][:, 0:512], abf[i][:, 512:1024]] for i in range(NP)]
    MTb = [[abf[i][:, 1024:1536], abf[i][:, 1536:2048]] for i in range(NP)]
    IMb = [abf[i][:, 2048:2560] for i in range(NP)]
    PTb = [[abf[i][:, 2560:3072], abf[i][:, 3072:3584]] for i in range(NP)]
    for i in range(NP):
        raw.append(arena[i][:, 0:515])
        cv.append(arena[i][:, 516:1028])
        cs.append(arena[i][:, 1028:1540])
        rn.append(arena[i][:, 1540:2052])
        sq.append(arena[i][:, 2052:2308].bitcast(BF16))
        for lst in (r_raw, r_cv, r_cs, r_sq, r_rn):
            lst.append(Res())
    HS = 4
    AT = [sb("AT%d" % i, [128, 512], BF16, ph) for i in range(HS)]
    u32 = [sb("u32%d" % i, [128, 512], F32, ph) for i in range(HS)]
    wT = [sb("wT%d" % i, [128, 512], BF16, ph) for i in range(HS)]
    r_e1m = [Res() for _ in range(NP)]
    r_tg = [Res() for _ in range(NP)]
    r_M = [[Res(), Res()] for _ in range(NP)]
    r_MT = [[Res(), Res()] for _ in range(NP)]
    r_IM = [Res() for _ in range(NP)]
    r_PT = [[Res(), Res()] for _ in range(NP)]
    r_AT = [Res() for _ in range(HS)]
    r_u32 = [Res() for _ in range(HS)]
    r_wT = [Res() for _ in range(HS)]
    vn = sb("vn", [128, 512], BF16, ph)
    tq = sb("tq", [128, 512], F32, ph)
    o32b = [sb("o32%d" % i, [128, 512], F32, ph) for i in range(2)]
    r_o32b = [Res(), Res()]
    osq = sb("osq", [128, 512], F32, ph)
    oss = sb("oss", [128, 4], F32, ph)
    on = sb("on", [128, 512], BF16, ph)
    zs = sb("zs", [128, 512], F32, ph)
    zsw = zs
    S32 = sb("S32", [128, 512], F32, ph)
    Sb = sb("Sb", [128, 512], BF16, ph)
    r_vn, r_tq, r_osq, r_oss, r_on, r_zs, r_zsw, r_S32, r_Sb = [Res() for _ in range(9)]
    fw.op("dve", lambda e: e.memset(S32[:], 0.0), [], [r_S32])
    fw.op("dve", lambda e: e.memset(Sb[:], 0.0), [], [r_Sb])

    def v4(ap):
        return ap.rearrange("p (h d) -> p h d", h=4)

    idf_b = ident_f.unsqueeze(1).to_broadcast([128, 4, 128])
    idb_b = ident_b.unsqueeze(1).to_broadcast([128, 4, 128])

    def sec_stream(arg, s):
        n, h, typ = arg
        par = n % 2
        tok = slice(n * 512, (n + 1) * 512)
        sec = typ * 4 + h
        col0 = typ * 512 + h * 128
        pr, rpr = yield from pget()
        for k in range(8):
            mm(pr[:, 0:512], wdn[:, k, col0:col0 + 128], hT[:, k, tok], k == 0, k == 7,
               hres(n) + wres(wp, col0, col0 + 128), [rpr])
        if n == 0:
            fw.op("pool", lambda e: e.memset(raw[s][:, 0:3], 0.0), [], [r_raw[s]])
        else:
            fw.op("pool", lambda e: e.tensor_copy(out=raw[s][:, 0:3], in_=halo[:, sec, :]),
                  [r_halo[sec]], [r_raw[s]])
        yield
        fw.op("act", lambda e: e.activation(out=raw[s][:, 3:515], in_=pr[:, 0:512], func=AF.Copy),
              [rpr], [r_raw[s]])
        psp.rel(rpr)
        yield
        fw.op("pool", lambda e: e.tensor_copy(out=halo[:, sec, :], in_=raw[s][:, 512:515]),
              [r_raw[s]], [r_halo[sec]])
        cwo = sec * 4
        fw.op("dve", lambda e: e.tensor_scalar(out=cv[s][:], in0=raw[s][:, 0:512], scalar1=cfs("cw", cwo, cwo + 1),
                                               scalar2=None, op0=ALU.mult), [r_raw[s], r_cf], [r_cv[s]])
        yield
        for j in range(1, 4):
            fw.op("dve", lambda e: e.scalar_tensor_tensor(out=cv[s][:], in0=raw[s][:, j:j + 512],
                                                          scalar=cfs("cw", cwo + j, cwo + j + 1),
                                                          in1=cv[s][:], op0=ALU.mult, op1=ALU.add),
                  [r_raw[s], r_cf, r_cv[s]], [r_cv[s]])
            yield
        fw.op("act", lambda e: e.activation(out=cs[s][:], in_=cv[s][:], func=AF.Silu), [r_cv[s]], [r_cs[s]])
        yield
        if typ < 2:
            fw.op("act", lambda e: e.activation(out=sq[s][:], in_=cs[s][:], func=AF.Square), [r_cs[s]], [r_sq[s]])
            yield
            pss, rpss = yield from pget()
            mm(pss[:, 0:512], ones_b, sq[s][:], True, True, [r_sq[s], r_cb], [rpss])
            yield
            fw.op("act", lambda e: e.activation(out=rn[s][:], in_=pss[:, 0:512], func=AF.Ln, bias=eps_c,
                                                scale=1.0), [rpss, r_cf], [r_rn[s]])
            psp.rel(rpss)
            yield
            fw.op("act", lambda e: e.activation(out=rn[s][:], in_=rn[s][:], func=AF.Exp, scale=-0.5),
                  [r_rn[s]], [r_rn[s]])
            yield
        if typ == 0:
            fw.op("dve", lambda e: e.scalar_tensor_tensor(out=qT[par][:, h, :], in0=cs[s][:], scalar=QSCALE,
                                                          in1=rn[s][:], op0=ALU.mult, op1=ALU.mult),
                  [r_cs[s], r_rn[s]], [r_qT[par][h]])
            return
        if typ == 1:
            fw.op("dve", lambda e: e.tensor_tensor(out=cv[s][:], in0=cs[s][:], in1=rn[s][:], op=ALU.mult),
                  [r_cs[s], r_rn[s]], [r_cv[s]])
            yield
            fw.op("dve", lambda e: e.tensor_copy(out=kT[par][:, h, :], in_=cv[s][:]),
                  [r_cv[s]], [r_kT[par][h]])
            src, rsrc = cv[s], r_cv[s]
        else:
            src, rsrc = cs[s], r_cs[s]
        ptr, rptr = yield from pget()
        for j in range(4):
            fw.op("pe", lambda e: e.transpose(out=ptr[:, j * 128:(j + 1) * 128],
                                              in_=src[:, j * 128:(j + 1) * 128], identity=ident_f),
                  [rsrc, r_cf], [rptr])
        yield
        pv = ptr[:, 0:512].rearrange("p (j d) -> p j d", j=4)
        if typ == 1:
            sc1 = sckbg[:, 4 * n:4 * n + 4, h:h + 1].to_broadcast([128, 4, 128])
            sc2 = sckd[:, 4 * n:4 * n + 4, h:h + 1].to_broadcast([128, 4, 128])
            fw.op("dve", lambda e: e.tensor_tensor(out=kbg[par][:, :, h, :], in0=pv, in1=sc1, op=ALU.mult),
                  [rptr, r_a0], [r_kbg[par][h]])
            yield
            fw.op("dve", lambda e: e.tensor_tensor(out=kd[par][:, :, h, :], in0=pv, in1=sc2, op=ALU.mult),
                  [rptr, r_a0], [r_kd[par][h]])
        else:
            sc3 = beta[:, 4 * n:4 * n + 4, h:h + 1].to_broadcast([128, 4, 128])
            fw.op("dve", lambda e: e.tensor_tensor(out=vb[par][:, :, h, :], in0=pv, in1=sc3, op=ALU.mult),
                  [rptr, r_a0], [r_vb[par][h]])
        psp.rel(rptr)

    def pre_stream(i, s):
        n, j = i // 4, i % 4
        par = n % 2
        hs_ = i % HS
        tk = slice(j * 128, (j + 1) * 128)
        pB, rpB = yield from pget()
        for h in range(4):
            gb = g[:, i, h:h + 1].to_broadcast([128, 128])
            mm(pB[:, h * 128:(h + 1) * 128], gb, cfs("tri"), True, False, [r_a0, r_cf], [rpB])
            mm(pB[:, h * 128:(h + 1) * 128], ident_b, cbs("x1"), False, True, [r_cb], [rpB])
        yield
        for h in range(4):
            fw.op("act", lambda e: e.activation(out=e1m[s][:, h * 128:(h + 1) * 128],
                                                in_=pB[:, h * 128:(h + 1) * 128], func=AF.Exp,
                                                bias=gc[:, i, h:h + 1], scale=-1.0), [rpB, r_a0], [r_e1m[s]])
        psp.rel(rpB)
        pG, rpG = yield from pget()
        for h in range(4):
            mm(pG[:, h * 128:(h + 1) * 128], kT[par][:, h, tk], kT[par][:, h, tk], True, True,
               [r_kT[par][h]], [rpG])
        yield
        nb_b = nbeta[:, i, :].unsqueeze(2).to_broadcast([128, 4, 128])
        fw.op("dve", lambda e: e.tensor_tensor(out=v4(tg[s][:]), in0=v4(pG[:, 0:512]), in1=nb_b, op=ALU.mult),
              [rpG, r_a0], [r_tg[s]])
        psp.rel(rpG)
        yield
        fw.op("dve", lambda e: e.tensor_tensor(out=Mb[s][0][:], in0=tg[s][:], in1=e1m[s][:], op=ALU.mult),
              [r_tg[s], r_e1m[s]], [r_M[s][0]])
        pE, rpE = yield from pget()
        for h in range(4):
            fw.op("pe", lambda e: e.transpose(out=pE[:, h * 128:(h + 1) * 128],
                                              in_=e1m[s][:, h * 128:(h + 1) * 128], identity=ident_f),
                  [r_e1m[s], r_cf], [rpE])
        yield
        pT, rpT = yield from pget()
        pTb = pT[:].bitcast(BF16)
        for h in range(4):
            fw.op("pe", lambda e: e.transpose(out=pTb[:, h * 128:(h + 1) * 128],
                                              in_=Mb[s][0][:, h * 128:(h + 1) * 128], identity=ident_b),
                  [r_M[s][0], r_cb], [rpT])
        fw.op("dve", lambda e: e.tensor_tensor(out=v4(tg[s][:]), in0=v4(pE[:, 0:512]), in1=idf_b, op=ALU.add),
              [rpE, r_cf, r_tg[s]], [r_tg[s]])
        psp.rel(rpE)
        yield
        fw.op("act", lambda e: e.activation(out=MTb[s][0][:], in_=pTb[:, 0:512], func=AF.Copy), [rpT], [r_MT[s][0]])
        yield
        fw.op("dve", lambda e: e.tensor_tensor(out=v4(PTb[s][0][:]), in0=v4(pTb[:, 0:512]), in1=idb_b, op=ALU.add),
              [rpT, r_cb], [r_PT[s][0]])
        psp.rel(rpT)
        pQ, rpQ = yield from pget()
        for h in range(4):
            mm(pQ[:, h * 128:(h + 1) * 128], kT[par][:, h, tk], qT[par][:, h, tk], True, True,
               [r_kT[par][h], r_qT[par][h]], [rpQ])
        yield
        fw.op("dve", lambda e: e.tensor_tensor(out=AT[hs_][:], in0=pQ[:, 0:512], in1=tg[s][:], op=ALU.mult),
              [rpQ, r_tg[s]], [r_AT[hs_]])
        psp.rel(rpQ)
        cur = 0
        for lev in range(1, 6):
            nxt = 1 - cur
            pM, rpM = yield from pget()
            for h in range(4):
                hs = slice(h * 128, (h + 1) * 128)
                mm(pM[:, hs], MTb[s][cur][:, hs], Mb[s][cur][:, hs], True, True, [r_MT[s][cur], r_M[s][cur]], [rpM])
            if lev < 5:
                pMT, rpMT = yield from pget()
                for h in range(4):
                    hs = slice(h * 128, (h + 1) * 128)
                    mm(pMT[:, hs], Mb[s][cur][:, hs], MTb[s][cur][:, hs], True, True,
                       [r_MT[s][cur], r_M[s][cur]], [rpMT])
            yield
            fw.op("dve", lambda e: e.tensor_tensor(out=v4(IMb[s][:]), in0=v4(pM[:, 0:512]), in1=idf_b, op=ALU.add),
                  [rpM, r_cf], [r_IM[s]])
            if lev < 5:
                fw.op("act", lambda e: e.activation(out=MTb[s][nxt][:], in_=pMT[:, 0:512], func=AF.Copy),
                      [rpMT], [r_MT[s][nxt]])
                psp.rel(rpMT)
            yield
            if lev < 5:
                fw.op("act", lambda e: e.activation(out=Mb[s][nxt][:], in_=pM[:, 0:512], func=AF.Copy),
                      [rpM], [r_M[s][nxt]])
            psp.rel(rpM)
            pP, rpP = yield from pget()
            for h in range(4):
                hs = slice(h * 128, (h + 1) * 128)
                mm(pP[:, hs], IMb[s][:, hs], PTb[s][cur][:, hs], True, True, [r_IM[s], r_PT[s][cur]], [rpP])
            yield
            fw.op("act" if lev % 2 else "dve",
                  (lambda e: e.activation(out=PTb[s][nxt][:], in_=pP[:, 0:512], func=AF.Copy)) if lev % 2 else
                  (lambda e: e.tensor_copy(out=PTb[s][nxt][:], in_=pP[:, 0:512])), [rpP], [r_PT[s][nxt]])
            psp.rel(rpP)
            cur = nxt
            yield
        PTf, r_PTf = PTb[s][cur], r_PT[s][cur]
        pU, rpU = yield from pget()
        for h in range(4):
            hs = slice(h * 128, (h + 1) * 128)
            mm(pU[:, hs], PTf[:, hs], vb[par][:, j, h, :], True, True, [r_PTf, r_vb[par][h]], [rpU])
        pW, rpW = yield from pget()
        for h in range(4):
            hs = slice(h * 128, (h + 1) * 128)
            mm(pW[:, hs], kbg[par][:, j, h, :], PTf[:, hs], True, True, [r_PTf, r_kbg[par][h]], [rpW])
        yield
        fw.op("act", lambda e: e.activation(out=u32[hs_][:], in_=pU[:, 0:512], func=AF.Copy), [rpU], [r_u32[hs_]])
        psp.rel(rpU)
        fw.op("dve", lambda e: e.tensor_copy(out=wT[hs_][:], in_=pW[:, 0:512]), [rpW], [r_wT[hs_]])
        psp.rel(rpW)

    def chain_stream(i):
        n, j = i // 4, i % 4
        par = n % 2
        s = i % HS
        o32 = o32b[i % 2]
        r_o32 = r_o32b[i % 2]
        tk = slice(j * 128, (j + 1) * 128)
        for c in range(2):
            R = slice(c * 64, (c + 1) * 64)
            pSw, rpSw = yield from pget()
            for h in range(4):
                hs = slice(h * 128, (h + 1) * 128)
                mm(pSw[:, hs], wT[s][:, hs], Sb[:, hs], True, True, [r_wT[s], r_Sb], [rpSw])
            pSq, rpSq = yield from pget()
            for h in range(4):
                hs = slice(h * 128, (h + 1) * 128)
                mm(pSq[:, hs], qT[par][:, h, tk], Sb[:, hs], True, True, [r_qT[par][h], r_Sb], [rpSq])
            yield
            egl_b = egl[:, c, i, :].unsqueeze(2).to_broadcast([128, 4, 128])
            fw.op("dve", lambda e: e.tensor_tensor(out=v4(S32[:]), in0=v4(S32[:]), in1=egl_b, op=ALU.mult),
                  [r_S32, r_a0], [r_S32])
            fw.op("dve", lambda e: e.tensor_tensor(out=vn[R, :], in0=u32[s][R, :], in1=pSw[R, 0:512],
                                                   op=ALU.subtract), [r_u32[s], rpSw], [r_vn])
            psp.rel(rpSw)
            yield
            pSs, rpSs = yield from pget()
            for h in range(4):
                hs = slice(h * 128, (h + 1) * 128)
                mm(pSs[:, hs], kd[par][R, j, h, :], vn[R, hs], True, True, [r_kd[par][h], r_vn], [rpSs])
            pSa, rpSa = yield from pget()
            for h in range(4):
                hs = slice(h * 128, (h + 1) * 128)
                mm(pSa[:, hs], AT[s][R, hs], vn[R, hs], True, True, [r_AT[s], r_vn], [rpSa])
            egc_b = egc[R, i, :].unsqueeze(2).to_broadcast([64, 4, 128])
            fw.op("dve", lambda e: e.tensor_tensor(out=v4(tq[R, :]), in0=v4(pSq[R, 0:512]), in1=egc_b,
                                                   op=ALU.mult), [rpSq, r_a0], [r_tq])
            psp.rel(rpSq)
            yield
            fw.op("dve", lambda e: e.tensor_tensor(out=S32[:], in0=S32[:], in1=pSs[:, 0:512], op=ALU.add),
                  [r_S32, rpSs], [r_S32])
            psp.rel(rpSs)
            yield
            fw.op("act", lambda e: e.activation(out=Sb[:], in_=S32[:], func=AF.Copy), [r_S32], [r_Sb])
            fw.op("dve", lambda e: e.tensor_tensor(out=o32[R, :], in0=tq[R, :], in1=pSa[R, 0:512], op=ALU.add),
                  [r_tq, rpSa], [r_o32])
            psp.rel(rpSa)
            yield

    def epi_stream(i):
        n, j = i // 4, i % 4
        o32 = o32b[i % 2]
        r_o32 = r_o32b[i % 2]
        pz, rpz = yield from pget()
        for k in range(8):
            mm(pz[:, 0:512], hT[:, k, i * 128:(i + 1) * 128], wdn[:, k, 1536:2048], k == 0, k == 7,
               [r_hT[i]] + wres(wp, 1536, 2048), [rpz])
        yield
        fw.op("act", lambda e: e.activation(out=zs[:], in_=pz[:, 0:512], func=AF.Silu), [rpz], [r_zs])
        psp.rel(rpz)
        fw.op("act", lambda e: e.activation(out=osq[:], in_=o32[:], func=AF.Square), [r_o32], [r_osq])
        yield
        fw.op("dve", lambda e: e.tensor_tensor(out=zsw[:], in0=zs[:], in1=cfs("dnw"), op=ALU.mult),
              [r_zs, r_cf], [r_zsw])
        fw.op("dve", lambda e: e.tensor_reduce(out=oss[:], in_=v4(osq[:]), axis=AX.X, op=ALU.add),
              [r_osq], [r_oss])
        yield
        fw.op("act", lambda e: e.activation(out=oss[:], in_=oss[:], func=AF.Ln, bias=eps_c, scale=1.0 / 128),
              [r_oss, r_cf], [r_oss])
        fw.op("act", lambda e: e.activation(out=oss[:], in_=oss[:], func=AF.Exp, scale=-0.5), [r_oss], [r_oss])
        yield
        oss_b = oss[:].unsqueeze(2).to_broadcast([128, 4, 128])
        fw.op("dve", lambda e: e.tensor_tensor(out=v4(osq[:]), in0=v4(o32[:]), in1=oss_b, op=ALU.mult),
              [r_o32, r_oss, r_osq], [r_osq])
        fw.op("dve", lambda e: e.tensor_tensor(out=on[:], in0=osq[:], in1=zsw[:], op=ALU.mult),
              [r_osq, r_zsw], [r_on])
        yield
        pO, rpO = yield from pget()
        pOb = pO[:].bitcast(BF16)
        for h in range(4):
            fw.op("pe", lambda e: e.transpose(out=pOb[:, h * 128:(h + 1) * 128],
                                              in_=on[:, h * 128:(h + 1) * 128], identity=ident_b),
                  [r_on, r_cb], [rpO])
        yield
        fw.op("act", lambda e: e.activation(out=oT[:, 0:4, i * 128:(i + 1) * 128], in_=v4(pOb[:, 0:512]),
                                            func=AF.Copy), [rpO], [r_oT[h][n] for h in range(4)])
        psp.rel(rpO)

    def sec_items(n):
        return [(n, h, typ) for h in range(4) for typ in range(2)] + [(n, h, 2) for h in range(4)]

    def chain(gens):
        for g_ in gens:
            yield from g_

    def tiles_gen():
        for i in range(NT):
            yield ("need_sections", i // 4)
            yield from pre_stream(i, i % NP)
            yield ("pre_done", i)

    import os
    nbanks = int(os.environ.get('DN_BANKS', '4'))
    ntl = 4 * nbanks
    sec_done = [False] * 5
    pre_done = [False] * (NT + 1)
    chain_done = [False] * (NT + 1)
    epi_done = [False] * (NT + 1)

    def sections_meta():
        for n in range(nbanks):
            while n >= 2 and not chain_done[4 * (n - 2) + 3]:
                yield
            if n == 0:
                yield from streams_gen(sec_items(n), NS + NP, sec_stream)
                fw.barrier_all()
            else:
                yield from streams_gen(sec_items(n), NS, sec_stream)
            sec_done[n] = True

    def pre_one(i, s):
        while not sec_done[i // 4]:
            yield
        while i >= HS and not chain_done[i - HS]:
            yield
        yield from pre_stream(i, s)
        pre_done[i] = True

    def pre_meta():
        yield from streams_gen(list(range(ntl)), NP, pre_one, stagger=12)

    def chain_meta():
        for i in range(ntl):
            while not pre_done[i]:
                yield
            while i >= 2 and not epi_done[i - 2]:
                yield
            yield from chain_stream(i)
            chain_done[i] = True

    def epi_meta():
        for i in range(ntl):
            while not chain_done[i]:
                yield
            yield from epi_stream(i)
            epi_done[i] = True

    metas = [sections_meta(), pre_meta(), chain_meta(), epi_meta()]
    stall = 0
    last = fw.ninst
    while metas:
        for m_ in list(metas):
            try:
                next(m_)
            except StopIteration:
                metas.remove(m_)
        if fw.ninst == last:
            stall += 1
            if stall > 100000:
                raise RuntimeError("stream scheduler livelock")
        else:
            stall = 0
            last = fw.ninst


def phase_swa(nc, fw, ph, sb, psget, mm, load_w, wres, cfs, cbs, hT, oT, r_hT, r_oT, r_cf, r_cb, hres,
              wswa_d, rope_d, rot, pst, r_ps):
    ident_b = cbs("ident")
    ones_b = cbs("ones")
    eps_c = cfs("eps")
    psp = PSP(pst, r_ps)

    def pget():
        n = 0
        while True:
            p = psp.try_get()
            if p is not None:
                return p
            n += 1
            if n > 100000:
                raise RuntimeError("PSUM starvation")
            yield

    rope = sb("rope", [32, 2, T], F32, ph)
    r_rope = Res()
    fw.dma(rope[:], rope_d[:, :, :], writes=[r_rope])
    wsw = sb("wsw", [128, 8, 1280], BF16, ph)
    wr = {"qk": Res(), "v": Res(), "z": Res()}
    wrange = {"qk": (0, 768), "v": (768, 1152), "z": (1152, 1280)}

    def load_head(h, part):
        c0, c1 = wrange[part]
        srcv = wswa_d[h].rearrange("(k p) c -> p k c", p=128)
        fw.dma(wsw[:, :, c0:c1], srcv[:, :, c0:c1], writes=[wr[part]], q="pool")

    qk = [[sb("qk%d%d" % (t_, g_), [128, T], BF16, ph) for g_ in range(3)] for t_ in range(2)]
    r_qk = [[[Res() for _ in range(4)] for _ in range(3)] for _ in range(2)]
    V = [sb("V%d" % g_, [128, 16, 128], BF16, ph) for g_ in range(3)]
    r_V = [[Res() for _ in range(16)] for _ in range(3)]
    NQ = 6
    sq = [sb("ssq%d" % i, [128, 512], BF16, ph) for i in range(NQ)]
    rn = [sb("srn%d" % i, [128, 512], F32, ph) for i in range(NQ)]
    qn = [sb("sqn%d" % i, [128, 512], F32, ph) for i in range(NQ)]
    qnb = [sb("sqnb%d" % i, [32, 512], BF16, ph) for i in range(NQ)]
    r_sq = [Res() for _ in range(NQ)]
    r_rn = [Res() for _ in range(NQ)]
    r_qn = [Res() for _ in range(NQ)]
    r_qnb = [Res() for _ in range(NQ)]
    NA = 3
    zsall = sb("zsall", [128, T], F32, ph)
    r_zsall = [Res() for _ in range(4)]
    ee = [[sb("see%d%d" % (s, i), [128, 512], BF16, ph) for i in range(2)] for s in range(NA)]
    PP = ee
    r_ee = [[Res(), Res()] for _ in range(NA)]
    r_PP = r_ee
    rden = [sb("srden%d" % s, [128, 512], F32, ph) for s in range(NA)]
    tn = rden
    r_rden = [Res() for _ in range(NA)]
    r_tn = r_rden

    for part in ("qk", "v", "z"):
        load_head(0, part)

    for h in range(4):
        def qk_stream(arg, s):
            g_, typ, n = arg
            c0 = typ * 384 + g_ * 128
            dst = qk[typ][g_]
            wname = "qw" if typ == 0 else "kw"
            tok = slice(n * 512, (n + 1) * 512)
            pr, rpr = yield from pget()
            for k in range(8):
                mm(pr[:, 0:512], wsw[:, k, c0:c0 + 128], hT[:, k, tok], k == 0, k == 7, hres(n) + [wr["qk"]], [rpr])
            yield
            fw.op("act", lambda e: e.activation(out=sq[s][:], in_=pr[:, 0:512], func=AF.Square), [rpr], [r_sq[s]])
            yield
            pss, rpss = yield from pget()
            mm(pss[:, 0:512], ones_b, sq[s][:], True, True, [r_sq[s], r_cb], [rpss])
            yield
            fw.op("act", lambda e: e.activation(out=rn[s][:], in_=pss[:, 0:512], func=AF.Ln, bias=eps_c,
                                                scale=1.0 / 128), [rpss, r_cf], [r_rn[s]])
            psp.rel(rpss)
            yield
            fw.op("act", lambda e: e.activation(out=rn[s][:], in_=rn[s][:], func=AF.Exp, scale=-0.5),
                  [r_rn[s]], [r_rn[s]])
            yield
            fw.op("dve", lambda e: e.scalar_tensor_tensor(out=qn[s][:], in0=pr[:, 0:512],
                                                          scalar=cfs(wname, g_, g_ + 1), in1=rn[s][:],
                                                          op0=ALU.mult, op1=ALU.mult),
                  [rpr, r_rn[s], r_cf], [r_qn[s]])
            psp.rel(rpr)
            yield
            fw.op("dve", lambda e: e.tensor_tensor(out=qnb[s][:], in0=qn[s][0:32, :], in1=rope[:, 1, tok],
                                                   op=ALU.mult), [r_qn[s], r_rope], [r_qnb[s]])
            fw.op("act", lambda e: e.activation(out=dst[:, tok], in_=qn[s][:], func=AF.Copy),
                  [r_qn[s]], [r_qk[typ][g_][n]])
            yield
            prt, rprt = yield from pget()
            mm(prt[0:32, 0:512], cbs("rm")[0:32, :], qnb[s][:], True, True, [r_qnb[s], r_cb], [rprt])
            fw.op("dve", lambda e: e.tensor_tensor(out=rn[s][0:32, :], in0=qn[s][0:32, :], in1=rope[:, 0, tok],
                                                   op=ALU.mult), [r_qn[s], r_rope, r_rn[s]], [r_rn[s]])
            yield
            fw.op("dve", lambda e: e.tensor_tensor(out=dst[0:32, tok], in0=rn[s][0:32, :], in1=prt[0:32, 0:512],
                                                   op=ALU.add), [r_rn[s], rprt], [r_qk[typ][g_][n]])
            psp.rel(rprt)

        def v_stream(arg, s):
            g_, b4 = arg
            c0 = 768 + g_ * 128
            pv, rpv = yield from pget()
            for bb in range(4):
                blk = b4 * 4 + bb
                if g_ == 0:
                    tsl = slice(blk * 128, (blk + 1) * 128)
                    hr = [r_hT[blk]]
                elif g_ == 1:
                    n_, r_ = blk // 4, blk % 4
                    tsl = slice(512 * n_ + r_, 512 * (n_ + 1), 4)
                    hr = hres(n_)
                else:
                    tsl = slice(blk, T, 16)
                    hr = r_hT
                for k in range(8):
                    mm(pv[:, bb * 128:(bb + 1) * 128], hT[:, k, tsl], wsw[:, k, c0:c0 + 128], k == 0, k == 7,
                       list(hr) + [wr["v"]], [rpv])
                yield
            vout = V[g_][:, b4 * 4:(b4 + 1) * 4, :]
            vin = pv[:, 0:512].rearrange("p (b d) -> p b d", b=4)
            vres = [r_V[g_][b4 * 4 + bb] for bb in range(4)]
            if (g_ * 4 + b4) % 2 == 0:
                fw.op("act", lambda e: e.activation(out=vout, in_=vin, func=AF.Copy), [rpv], vres)
            else:
                fw.op("dve", lambda e: e.tensor_copy(out=vout, in_=vin), [rpv], vres)
            psp.rel(rpv)

        def mixed_stream(arg, s):
            if arg[0] == "v":
                return v_stream(arg[1:], s)
            return qk_stream(arg[1:], s)

        qk_items = [("qk", g_, typ, n) for g_ in range(3) for typ in range(2) for n in range(4)]
        v_items = [("v", g_, b4) for g_ in range(3) for b4 in range(4)]
        items = []
        for ii in range(12):
            items += qk_items[2 * ii:2 * ii + 2] + [v_items[ii]]
        run_streams(items, NQ, mixed_stream, stagger=2)
        if h < 3:
            load_head(h + 1, "qk")
        if h < 3:
            load_head(h + 1, "v")

        def attn_stream(m, s):
            num, r_num = yield from pget()
            den, r_den = yield from pget()
            first = [True, True]

            def st(idx):
                f = first[idx]
                first[idx] = False
                return f

            def geom(g_):
                def cols(n_, r_):
                    if g_ == 0:
                        tt = 4 * n_ + r_
                        return slice(tt * 128, (tt + 1) * 128)
                    return slice(512 * n_ + r_, 512 * (n_ + 1), 4)

                def outsl(r_):
                    if g_ == 0:
                        return slice(r_ * 128, (r_ + 1) * 128)
                    return slice(r_, 512, 4)

                def prevblk(n_, r_):
                    if g_ == 0:
                        tt = 4 * n_ + r_ - 1
                        if tt < 0:
                            return None
                        return (tt // 4, tt % 4)
                    if n_ == 0:
                        return None
                    return (n_ - 1, r_)
                return cols, outsl, prevblk

            groups = []
            for g_ in range(2):
                groups.append((g_, "cur"))
                _, _, prevblk = geom(g_)
                if any(prevblk(m, r_) is not None for r_ in range(4)):
                    groups.append((g_, "prev"))
            groups.append((2, "cur"))
            pend = None
            gi = 0

            def emit_pv(p):
                g_, kind, buf, rbuf, lo = p
                if g_ == 2:
                    for r_ in range(16):
                        mm(num[:, slice(r_, 512, 16)], V[2][:, r_, :], buf[:, r_ * 32:(r_ + 1) * 32], st(0), False,
                           [r_V[2][r_], rbuf], [r_num])
                        mm(den[:, slice(r_, 512, 16)], ones_b, buf[:, r_ * 32:(r_ + 1) * 32], st(1), False,
                           [r_cb, rbuf], [r_den])
                    return
                cols, outsl, prevblk = geom(g_)
                for r_ in range(4):
                    if kind == "cur":
                        vb_ = 4 * m + r_
                    else:
                        pb = prevblk(m, r_)
                        if pb is None:
                            continue
                        vb_ = 4 * pb[0] + pb[1]
                    mm(num[:, outsl(r_)], V[g_][:, vb_, :], buf[:, r_ * 128:(r_ + 1) * 128], st(0), False,
                       [r_V[g_][vb_], rbuf], [r_num])
                    mm(den[:, outsl(r_)], ones_b, buf[:, r_ * 128:(r_ + 1) * 128], st(1), False,
                       [r_cb, rbuf], [r_den])

            def emit_scores(pc, rpc, g_, kind):
                lo = 0
                if g_ == 2:
                    qTt, kTt = qk[0][2], qk[1][2]
                    for r_ in range(16):
                        mm(pc[:, r_ * 32:(r_ + 1) * 32], kTt[:, slice(r_, T, 16)],
                           qTt[:, slice(512 * m + r_, 512 * (m + 1), 16)], True, True,
                           [r_qk[0][2][m]] + r_qk[1][2], [rpc])
                    mask = cbs("m2", m * 512, (m + 1) * 512)
                else:
                    qTt, kTt = qk[0][g_], qk[1][g_]
                    cols, outsl, prevblk = geom(g_)
                    if kind == "cur":
                        for r_ in range(4):
                            mm(pc[:, r_ * 128:(r_ + 1) * 128], kTt[:, cols(m, r_)], qTt[:, cols(m, r_)], True, True,
                               [r_qk[0][g_][m], r_qk[1][g_][m]], [rpc])
                        mask = cbs("masku")
                    else:
                        prs = [(r_, prevblk(m, r_)) for r_ in range(4) if prevblk(m, r_) is not None]
                        lo = prs[0][0] * 128
                        for r_, (pn, pr_) in prs:
                            mm(pc[:, r_ * 128:(r_ + 1) * 128], kTt[:, cols(pn, pr_)], qTt[:, cols(m, r_)], True, True,
                               [r_qk[0][g_][m], r_qk[1][g_][pn], r_qk[1][g_][m]], [rpc])
                        mask = cbs("maskl")
                return pc, rpc, lo, mask

            pc0, rpc0 = yield from pget()
            nxt = emit_scores(pc0, rpc0, *groups[0])
            yield
            for gi, (g_, kind) in enumerate(groups):
                b = gi % 2
                pc, rpc, lo, mask = nxt
                fw.op("act", lambda e: e.activation(out=ee[s][b][:, lo:512], in_=pc[:, lo:512], func=AF.Exp,
                                                    scale=QSCALE), [rpc], [r_ee[s][b]])
                psp.rel(rpc)
                yield
                meng = "dve"
                fw.op(meng, lambda e: e.tensor_tensor(out=PP[s][b][:, lo:512], in0=ee[s][b][:, lo:512],
                                                      in1=mask[:, lo:512], op=ALU.mult),
                      [r_ee[s][b], r_cb], [r_PP[s][b]])
                if gi + 1 < len(groups):
                    pcn, rpcn = yield from pget()
                    nxt = emit_scores(pcn, rpcn, *groups[gi + 1])
                yield
                emit_pv((g_, kind, PP[s][b], r_PP[s][b], lo))
                yield
            tok = slice(m * 512, (m + 1) * 512)
            fw.op("act", lambda e: e.activation(out=rden[s][:], in_=den[:, 0:512], func=AF.Ln), [r_den], [r_rden[s]])
            psp.rel(r_den)
            yield
            fw.op("act", lambda e: e.activation(out=rden[s][:], in_=rden[s][:], func=AF.Exp, scale=-1.0),
                  [r_rden[s]], [r_rden[s]])
            yield
            fw.op("dve", lambda e: e.tensor_tensor(out=tn[s][:], in0=num[:, 0:512], in1=rden[s][:], op=ALU.mult),
                  [r_num, r_rden[s]], [r_tn[s]])
            psp.rel(r_num)
            yield
            fw.op("dve", lambda e: e.tensor_tensor(out=oT[:, 4 + h, tok], in0=tn[s][:], in1=zsall[:, tok], op=ALU.mult),
                  [r_tn[s], r_zsall[m]], [r_oT[4 + h][m]])

        for m in range(4):
            tok = slice(m * 512, (m + 1) * 512)
            pz, rpz = psp.get()
            for k in range(8):
                mm(pz[:, 0:512], wsw[:, k, 1152:1280], hT[:, k, tok], k == 0, k == 7, hres(m) + [wr["z"]], [rpz])
            fw.op("act", lambda e: e.activation(out=zsall[:, tok], in_=pz[:, 0:512], func=AF.Silu), [rpz], [r_zsall[m]])
            psp.rel(rpz)
        run_streams(list(range(4)), NA, attn_stream, stagger=5)
        if h < 3:
            load_head(h + 1, "z")


def phase_out(nc, fw, ph, sb, psget, mm, load_w, wres, cfs, cbs, hT, oT, r_hT, r_oT, r_cf, r_cb, hres,
              wg_d, wbd_d, wbs_d, wo_d, x_d, out_d, r_out):
    wbd = sb("wbd", [128, 4, 1024], BF16, ph)
    wbs = sb("wbs", [128, 4, 1024], BF16, ph)
    wpbd = load_w(wbd, wbd_d, 4, 1024)
    wpbs = load_w(wbs, wbs_d, 4, 1024)
    wg = sb("wg", [128, 8, 2048], BF16, ph)
    wo = sb("wo", [128, 8, 1024], BF16, ph)
    wgv = wg_d.rearrange("(k p) c -> p k c", p=128)
    wpg = []
    for c in range(0, 8, 2):
        for base in (0, 1024):
            c0, c1 = base + c * 128, base + (c + 2) * 128
            r_ = Res()
            fw.dma(wg[:, :, c0:c1], wgv[:, :, c0:c1], writes=[r_], q="pool")
            wpg.append((c0, c1, r_))
    wpo = load_w(wo, wo_d, 8, 1024)
    mT = sb("mT", [128, 8, 512], BF16, ph)
    r_mT = [Res() for _ in range(8)]
    sg = [sb("sg%d" % i, [128, 512], F32, ph) for i in range(2)]
    r_sg = [Res(), Res()]
    m1 = [sb("m1%d" % i, [128, 512], F32, ph) for i in range(2)]
    r_m1 = [Res(), Res()]
    xt = [sb("oxt%d" % i, [128, D], F32, ph) for i in range(2)]
    r_xt = [Res(), Res()]
    ot = [sb("oot%d" % i, [128, D], F32, ph) for i in range(2)]
    for n in range(4):
        tok = slice(n * 512, (n + 1) * 512)
        for c in range(8):
            cs_ = slice(c * 128, (c + 1) * 128)
            pyd, rpyd = psget()
            for hh in range(4):
                mm(pyd[:, 0:512], wbd[:, hh, cs_], oT[:, hh, tok], hh == 0, hh == 3,
                   [r_oT[hh][n]] + wres(wpbd, c * 128, (c + 1) * 128), [rpyd])
            pys, rpys = psget()
            for hh in range(4):
                mm(pys[:, 0:512], wbs[:, hh, cs_], oT[:, 4 + hh, tok], hh == 0, hh == 3,
                   [r_oT[4 + hh][n]] + wres(wpbs, c * 128, (c + 1) * 128), [rpys])
            pgd, rpgd = psget()
            for k in range(8):
                mm(pgd[:, 0:512], wg[:, k, cs_], hT[:, k, tok], k == 0, k == 7,
                   hres(n) + wres(wpg, c * 128, (c + 1) * 128), [rpgd])
            pgs, rpgs = psget()
            for k in range(8):
                mm(pgs[:, 0:512], wg[:, k, 1024 + c * 128:1024 + (c + 1) * 128], hT[:, k, tok], k == 0, k == 7,
                   hres(n) + wres(wpg, 1024 + c * 128, 1024 + (c + 1) * 128), [rpgs])
            fw.op("act", lambda e: e.activation(out=sg[0][:], in_=pgd[:, 0:512], func=AF.Sigmoid), [rpgd], [r_sg[0]])
            fw.op("act", lambda e: e.activation(out=sg[1][:], in_=pgs[:, 0:512], func=AF.Sigmoid), [rpgs], [r_sg[1]])
            fw.op("dve", lambda e: e.tensor_tensor(out=m1[0][:], in0=pyd[:, 0:512], in1=sg[0][:], op=ALU.mult),
                  [rpyd, r_sg[0]], [r_m1[0]])
            fw.op("dve", lambda e: e.tensor_tensor(out=m1[1][:], in0=pys[:, 0:512], in1=sg[1][:], op=ALU.mult),
                  [rpys, r_sg[1]], [r_m1[1]])
            fw.op("pool", lambda e: e.tensor_tensor(out=mT[:, c, :], in0=m1[0][:], in1=m1[1][:], op=ALU.add),
                  [r_m1[0], r_m1[1]], [r_mT[c]])
        for j in range(4):
            i = 4 * n + j
            s = i % 2
            fw.dma(xt[s][:], x_d[i * 128:(i + 1) * 128, :], writes=[r_xt[s]])
            for half in range(2):
                hs = slice(half * 512, (half + 1) * 512)
                po, rpo = psget()
                for k in range(8):
                    mm(po[:, 0:512], mT[:, k, j * 128:(j + 1) * 128], wo[:, k, hs], k == 0, k == 7,
                       [r_mT[k]] + wres(wpo, half * 512, (half + 1) * 512), [rpo])
                fw.op("dve", lambda e: e.tensor_tensor(out=ot[s][:, hs], in0=po[:, 0:512], in1=xt[s][:, hs],
                                                       op=ALU.add), [rpo, r_xt[s]], [r_out[s]])
            fw.dma(out_d[i * 128:(i + 1) * 128, :], ot[s][:], reads=[r_out[s]], semres=r_out[s])
    fw.final_wait(r_out)


_CACHE = {}


def kernel(**inputs):
    inp = {k: np.asarray(v) for k, v in inputs.items()}
    shared = _prep_shared(inp)
    if "nc" not in _CACHE:
        _CACHE["nc"] = build()
    nc = _CACHE["nc"]
    x = np.ascontiguousarray(inp["x"], dtype=np.float32)
    in_maps = []
    for b in range(8):
        m = dict(shared)
        m["x"] = x[b]
        in_maps.append(m)
    res = run_bass_kernel_spmd(nc, in_maps, core_ids=list(range(8)))
    out = np.stack([np.asarray(res.results[b]["out"]) for b in range(8)], axis=0)
    return out.astype(np.float32)
```
